# Optimizing a Trainium2 kernel written in Bass

```python
import jax, jax.numpy as jnp
from jax import lax
import numpy as np

D_MODEL = 1024
BATCH = 8
SEQ = 4096
DEPTH = 4

HEAD_DIM = 64
HEADS_PER_GROUP = 8
DILATION_GROUPS = ((128, 1), (512, 4), (2048, 16))
N_GROUPS = len(DILATION_GROUPS)
N_ATTN_HEADS = N_GROUPS * HEADS_PER_GROUP
QKV_WIDTH = N_ATTN_HEADS * HEAD_DIM
ATTN_OUT = HEADS_PER_GROUP * HEAD_DIM
CONV_WIDTH = D_MODEL
CONV_K = 3
D_FF = ((8 * D_MODEL // 3) + 127) // 128 * 128
IN_WIDTH = 3 * QKV_WIDTH + 3 * CONV_WIDTH + 2 * D_MODEL
NUM_BUCKETS = 32
MAX_DISTANCE = 2048
BLOCK = 128
N_SUB = 3
EPS = 1e-6
NEG_INF = -1e30

kernel_name = "hybrid_macaron_conv_dilated_attn"


def _t5_bucket(dist):
    exact = NUM_BUCKETS // 2
    d = np.maximum(dist, 1).astype(np.float32)
    large = exact + (np.log(d / exact) / np.log(MAX_DISTANCE / exact) * (NUM_BUCKETS - exact)).astype(np.int32)
    large = np.minimum(large, NUM_BUCKETS - 1)
    return np.where(dist < exact, dist, large).astype(np.int32)


def _rmsnorm(x, g):
    xf = x.astype(jnp.float32)
    y = xf * lax.rsqrt(jnp.mean(xf * xf, axis=-1, keepdims=True) + EPS) * g.astype(jnp.float32)
    return y.astype(x.dtype)


def _swiglu(h, w_gate, w_up, w_down):
    return (jax.nn.silu(h @ w_gate) * (h @ w_up)) @ w_down


def _causal_dwconv(u, w):
    rhs = w.astype(u.dtype).reshape(CONV_K, 1, u.shape[-1])
    return lax.conv_general_dilated(u, rhs, window_strides=(1,), padding=[(CONV_K - 1, 0)],
                                    dimension_numbers=("NWC", "WIO", "NWC"),
                                    feature_group_count=u.shape[-1])


def _dilated_window_attention(q, k, v, bias_tab, window, dilation):
    B, S, H, E = q.shape
    span = window // dilation
    L = S // dilation
    nb = -(-L // BLOCK)
    Lp = nb * BLOCK

    def to_sub(t):
        t = t.reshape(B, L, dilation, H, E)
        return jnp.pad(t, ((0, 0), (0, Lp - L), (0, 0), (0, 0), (0, 0)))

    qs, ks, vs = to_sub(q), to_sub(k), to_sub(v)
    qb = qs.reshape(B, nb, BLOCK, dilation, H, E)

    def key_blocks(t):
        tp = jnp.pad(t, ((0, 0), (BLOCK, 0), (0, 0), (0, 0), (0, 0)))
        prev = tp[:, :Lp].reshape(B, nb, BLOCK, dilation, H, E)
        cur = t.reshape(B, nb, BLOCK, dilation, H, E)
        return jnp.concatenate([prev, cur], axis=2)

    kb, vb = key_blocks(ks), key_blocks(vs)

    i = np.arange(BLOCK)[:, None]
    j = np.arange(2 * BLOCK)[None, :]
    rel = i - j + BLOCK
    band = (rel >= 0) & (rel <= span)
    first_ok = (np.arange(nb)[:, None, None] > 0) | (j[None] >= BLOCK)
    mask = jnp.asarray(band[None] & first_ok)[None, :, None, None]
    bucket = jnp.asarray(_t5_bucket(np.maximum(rel, 0) * dilation))
    bias = jnp.transpose(bias_tab[bucket].astype(jnp.float32), (2, 0, 1))

    logits = jnp.einsum("bnqrhe,bnkrhe->bnrhqk", qb, kb).astype(jnp.float32) * (HEAD_DIM ** -0.5)
    logits = jnp.where(mask, logits + bias, NEG_INF)
    m = jnp.max(logits, axis=-1, keepdims=True)
    p = jnp.exp(logits - m)
    s = jnp.sum(p, axis=-1)
    o = jnp.einsum("bnrhqk,bnkrhe->bnqrhe", p, vb.astype(jnp.float32))
    s_q = jnp.transpose(s, (0, 1, 4, 2, 3))
    o = o / s_q[..., None]
    lse = jnp.transpose(m[..., 0] + jnp.log(s), (0, 1, 4, 2, 3))
    o = o.reshape(B, Lp, dilation, H, E)[:, :L].reshape(B, S, H, E)
    lse = lse.reshape(B, Lp, dilation, H)[:, :L].reshape(B, S, H)
    return o, lse


def _mixer(h, w_in, conv_w, w_conv_out, w_attn_out, w_o, rel_bias):
    B, S, _ = h.shape
    u = h @ w_in
    splits = np.cumsum([QKV_WIDTH, QKV_WIDTH, QKV_WIDTH, CONV_WIDTH, CONV_WIDTH, CONV_WIDTH, D_MODEL])
    q, k, v, cb, cc, ch, g_conv, g_attn = jnp.split(u, splits, axis=-1)

    y_conv = (cb * _causal_dwconv(cc * ch, conv_w)) @ w_conv_out

    q = q.reshape(B, S, N_GROUPS, HEADS_PER_GROUP, HEAD_DIM)
    k = k.reshape(B, S, N_GROUPS, HEADS_PER_GROUP, HEAD_DIM)
    v = v.reshape(B, S, N_GROUPS, HEADS_PER_GROUP, HEAD_DIM)
    outs, lses = [], []
    for g, (window, dilation) in enumerate(DILATION_GROUPS):
        tab = rel_bias[:, g * HEADS_PER_GROUP:(g + 1) * HEADS_PER_GROUP]
        o_g, lse_g = _dilated_window_attention(q[:, :, g], k[:, :, g], v[:, :, g], tab, window, dilation)
        outs.append(o_g)
        lses.append(lse_g)
    alpha = jax.nn.softmax(jnp.stack(lses, axis=0), axis=0)
    o = jnp.sum(alpha[..., None] * jnp.stack(outs, axis=0), axis=0)
    y_attn = o.reshape(B, S, ATTN_OUT).astype(h.dtype) @ w_attn_out

    merged = jax.nn.sigmoid(g_conv) * y_conv + jax.nn.sigmoid(g_attn) * y_attn
    return merged @ w_o


def setup_inputs(seed: int = 0) -> dict:
    key = jax.random.key(seed)
    ks = jax.random.split(key, 16)
    f32 = jnp.float32

    def nrm(k, shape, fan_in):
        return jax.random.normal(k, shape, f32) * (fan_in ** -0.5)

    return {
        "x": jax.random.normal(ks[0], (BATCH, SEQ, D_MODEL), f32),
        "c": jax.random.normal(ks[1], (BATCH, D_MODEL), f32),
        "ada_w": nrm(ks[2], (DEPTH, D_MODEL, N_SUB * 3 * D_MODEL), D_MODEL),
        "ada_b": 0.02 * jax.random.normal(ks[3], (DEPTH, N_SUB * 3 * D_MODEL), f32),
        "norm_g": 1.0 + 0.05 * jax.random.normal(ks[4], (DEPTH, N_SUB, D_MODEL), f32),
        "ffn_w_gate": nrm(ks[5], (DEPTH, 2, D_MODEL, D_FF), D_MODEL),
        "ffn_w_up": nrm(ks[6], (DEPTH, 2, D_MODEL, D_FF), D_MODEL),
        "ffn_w_down": nrm(ks[7], (DEPTH, 2, D_FF, D_MODEL), D_FF),
        "w_in": nrm(ks[8], (DEPTH, D_MODEL, IN_WIDTH), D_MODEL),
        "conv_w": nrm(ks[9], (DEPTH, CONV_K, CONV_WIDTH), CONV_K),
        "w_conv_out": nrm(ks[10], (DEPTH, CONV_WIDTH, D_MODEL), CONV_WIDTH),
        "w_attn_out": nrm(ks[11], (DEPTH, ATTN_OUT, D_MODEL), ATTN_OUT),
        "w_o": nrm(ks[12], (DEPTH, D_MODEL, D_MODEL), D_MODEL),
        "rel_bias": 0.5 * jax.random.normal(ks[13], (NUM_BUCKETS, N_ATTN_HEADS), f32),
        "final_g": 1.0 + 0.05 * jax.random.normal(ks[14], (D_MODEL,), f32),
    }


def reference(x, c, ada_w, ada_b, norm_g, ffn_w_gate, ffn_w_up, ffn_w_down, w_in, conv_w,
              w_conv_out, w_attn_out, w_o, rel_bias, final_g):
    cs = jax.nn.silu(c)
    B = c.shape[0]
    for l in range(DEPTH):
        mod = (cs @ ada_w[l] + ada_b[l]).reshape(B, N_SUB, 3, D_MODEL)[:, :, :, None, :]
        h = _rmsnorm(x, norm_g[l, 0]) * (1.0 + mod[:, 0, 1]) + mod[:, 0, 0]
        x = x + 0.5 * mod[:, 0, 2] * _swiglu(h, ffn_w_gate[l, 0], ffn_w_up[l, 0], ffn_w_down[l, 0])
        h = _rmsnorm(x, norm_g[l, 1]) * (1.0 + mod[:, 1, 1]) + mod[:, 1, 0]
        x = x + mod[:, 1, 2] * _mixer(h, w_in[l], conv_w[l], w_conv_out[l], w_attn_out[l], w_o[l], rel_bias)
        h = _rmsnorm(x, norm_g[l, 2]) * (1.0 + mod[:, 2, 1]) + mod[:, 2, 0]
        x = x + 0.5 * mod[:, 2, 2] * _swiglu(h, ffn_w_gate[l, 1], ffn_w_up[l, 1], ffn_w_down[l, 1])
    return _rmsnorm(x, final_g)
```

```python
from contextlib import ExitStack

import numpy as np
import concourse.bass as bass
import concourse.mybir as mybir
from concourse.bass_utils import run_bass_kernel_spmd

F32 = mybir.dt.float32
BF16 = mybir.dt.bfloat16
AF = mybir.ActivationFunctionType
ALU = mybir.AluOpType

D = 1024
SEQ = 4096
DEPTH = 4
DFF = 2816
NFC = DFF // 128
INW = 9728
NCORES = 8
EPS = 1e-6
GROUPS = ((128, 1), (512, 4), (2048, 16))
NW = 3

ENGS = ("pe", "act", "dve", "pool", "sp")


class Op:
    __slots__ = ("eng", "fn", "deps", "dma", "sig", "sigval", "dmaval", "epoch", "ndma")

    def __init__(self, eng, fn, deps, dma, epoch, ndma):
        self.eng = eng
        self.fn = fn
        self.deps = deps
        self.dma = dma
        self.sig = False
        self.sigval = 0
        self.dmaval = 0
        self.epoch = epoch
        self.ndma = ndma


class Prog:
    def __init__(self):
        self.ops = {e: [] for e in ENGS}
        self.bw = {}
        self.br = {}
        self.epoch = 0
        self.planning = False
        self.dma_count = {}

    def add(self, eng, fn, reads=(), writes=(), dma=None, ndma=1, after=()):
        if self.planning:
            return None
        deps = set(a for a in after if a is not None)
        for k in reads:
            w = self.bw.get(k)
            if w is not None:
                deps.add(w)
        for k in writes:
            w = self.bw.get(k)
            if w is not None:
                deps.add(w)
            for r in self.br.get(k, {}).values():
                deps.add(r)
        idx = len(self.ops[eng])
        me = (eng, idx)
        op = Op(eng, fn, deps, dma, self.epoch, ndma)
        if dma is not None:
            c = self.dma_count.get(dma, 0) + ndma
            self.dma_count[dma] = c
            op.dmaval = 16 * c
        self.ops[eng].append(op)
        for k in reads:
            self.br.setdefault(k, {})[eng] = me
        for k in writes:
            self.bw[k] = me
            self.br[k] = {}
        return me

    def emit(self, nc):
        ops = self.ops
        raw_same = set()
        for e in ENGS:
            for op in ops[e]:
                nd = set()
                for (e2, i2) in op.deps:
                    t = ops[e2][i2]
                    if t.dma is None:
                        if e2 == e and e == "pe":
                            continue
                        t.sig = True
                    nd.add((e2, i2))
                op.deps = nd
        nepoch = self.epoch + 1
        for e in ENGS:
            cnt = [0] * nepoch
            for op in ops[e]:
                if op.dma is None and op.sig:
                    cnt[op.epoch] += 1
                    op.sigval = cnt[op.epoch]
        with ExitStack() as st:
            esem = {}
            for e in ENGS:
                for ep in range(nepoch):
                    if any((o.dma is None and o.sig and o.epoch == ep) for o in ops[e]):
                        esem[(e, ep)] = st.enter_context(nc.semaphore("s_%s_%d" % (e, ep)))
            dsem = {}
            for i, k in enumerate(sorted(self.dma_count.keys(), key=str)):
                dsem[k] = st.enter_context(nc.semaphore("d_%d" % i))
            self.nsems = len(esem) + len(dsem)
            block = st.enter_context(nc.Block())

            def run(ename, eng):
                known = {}
                for op in ops[ename]:
                    need = {}
                    for (e2, i2) in op.deps:
                        t = ops[e2][i2]
                        if t.dma is not None:
                            sk, val = ("d", t.dma), t.dmaval
                        else:
                            if e2 == ename and ename == "pe":
                                continue
                            sk, val = ("e", e2, t.epoch), t.sigval
                        if need.get(sk, 0) < val:
                            need[sk] = val
                    for sk, val in need.items():
                        if known.get(sk, 0) >= val:
                            continue
                        known[sk] = val
                        sem = dsem[sk[1]] if sk[0] == "d" else esem[(sk[1], sk[2])]
                        eng.wait_ge(sem, val)
                    if op.fn is None:
                        continue
                    r = op.fn(eng)
                    if op.dma is not None:
                        rl = r if isinstance(r, (list, tuple)) else [r]
                        assert len(rl) == op.ndma, (len(rl), op.ndma)
                        for ins in rl:
                            ins.then_inc(dsem[op.dma], 16)
                    elif op.sig:
                        ins = r[-1] if isinstance(r, (list, tuple)) else r
                        ins.then_inc(esem[(ename, op.epoch)], 1)

            @block.tensor
            def _(eng):
                run("pe", eng)

            @block.scalar
            def _(eng):
                run("act", eng)

            @block.vector
            def _(eng):
                run("dve", eng)

            @block.gpsimd
            def _(eng):
                run("pool", eng)

            @block.sync
            def _(eng):
                run("sp", eng)


def _t5_bucket(dist):
    exact = 16
    d = np.maximum(dist, 1).astype(np.float32)
    large = exact + (np.log(d / exact) / np.log(2048 / exact) * (32 - exact)).astype(np.int32)
    large = np.minimum(large, 31)
    return np.where(dist < exact, dist, large).astype(np.int32)


def _bias_index():
    j = np.arange(128)[:, None]
    n = np.arange(256)[None, :]
    i = np.where(n < 128, n, n - 128)
    rel = np.where(n < 128, i - j, i - j + 128)
    valid = (rel >= 0) & (rel <= 128)
    buckets = []
    for (_, dil) in GROUPS:
        buckets.append(_t5_bucket(np.maximum(rel, 0) * dil))
    return np.stack(buckets, 0), valid


class Builder:
    def __init__(self, S=SEQ, depth=DEPTH, stop_after=None, p2_level=9):
        self.p2_level = p2_level
        self.S = S
        self.NT = S // 512
        self.depth = depth
        self.stop_after = stop_after
        self.nc = bass.Bass("TRN2", target_bir_lowering=False)
        self.P = Prog()
        self.build()

    def dram_in(self, name, shape, dt=F32):
        return self.nc.dram_tensor(name, list(shape), dt, kind="ExternalInput").ap()

    def sb(self, name, shape, dt):
        return self._st.enter_context(self.nc.sbuf_tensor(name, list(shape), dt))

    def psum_get(self):
        b = self.ps_next
        self.ps_next = (b + 1) % 8
        return self.ps[b], ("ps", b)

    def wuse(self, desc):
        if self.P.planning:
            self.wplan.append(desc)
            return 0
        n = self.wpos
        assert self.wplan[n] == desc, (n, self.wplan[n], desc)
        self.wpos += 1
        self.wpending.append(n)
        return n % NW

    def wrelease(self):
        if self.P.planning:
            return
        for n0 in self.wpending:
            n = n0 + NW
            if n < len(self.wplan):
                self.wload(n)
        self.wpending = []

    def wload(self, n):
        desc = self.wplan[n]
        slot = n % NW
        src, rk = self.wsrc(desc)
        dst = self.wr[:, slot, :]
        nparts = len(src)
        views = []
        off = 0
        for (ap, nrow_chunks, ncols) in src:
            views.append((ap, off, nrow_chunks, ncols))
            off += nrow_chunks * ncols

        def fn(eng, views=views, slot=slot):
            res = []
            for (ap, off, nrc, ncols) in views:
                for c in range(nrc):
                    o = self.wr[:, slot, off + c * ncols: off + (c + 1) * ncols]
                    res.append(eng.dma_start(out=o, in_=ap[c * 128:(c + 1) * 128, :]))
            return res
        nd = sum(v[2] for v in views)
        grp = {"Ada": 0, "Wg": 1, "Wu": 1, "Wd": 1, "Win": 3, "Wco": 3, "Wao": 3, "Wo": 3}[rk[2]]
        if grp == 1 and rk[3] == 1:
            grp = 2
        self.P.add("sp", fn, reads=list(self._castkeys.get((rk[1], grp), [])), writes=[("w", slot)], dma=("w", slot), ndma=nd)

    def wsrc(self, desc):
        kind = desc[0]
        l = desc[1]
        if kind == "Wg" or kind == "Wu":
            which, fb = desc[2], desc[3]
            t = self.wgb if kind == "Wg" else self.wub
            c0 = fb * 512
            nc_ = min(512, DFF - c0)
            return [(t[l, which, :, c0:c0 + nc_], 8, nc_)], ("wbf", l, kind, which)
        if kind == "Wd":
            which, dg, fg = desc[2], desc[3], desc[4]
            f0 = fg * 8
            nf = min(8, NFC - f0)
            return [(self.wdb[l, which, f0 * 128:(f0 + nf) * 128, dg * 512:(dg + 1) * 512], nf, 512)], ("wbf", l, kind, which)
        if kind == "Win":
            c0 = desc[2]
            return [(self.winb[l, :, c0:c0 + 512], 8, 512)], ("wbf", l, kind)
        if kind == "Wco":
            db = desc[2]
            return [(self.wcob[l, :, db * 512:(db + 1) * 512], 8, 512)], ("wbf", l, kind)
        if kind == "Wo":
            db = desc[2]
            return [(self.wob[l, :, db * 512:(db + 1) * 512], 8, 512)], ("wbf", l, kind)
        if kind == "Ada":
            blk = desc[2]
            return [(self.adawb[l, :, blk * 512:(blk + 1) * 512], 8, 512)], ("wbf", l, kind)
        if kind == "Wao":
            return [(self.waob[l, :, :], 4, 1024)], ("wbf", l, kind)
        raise ValueError(desc)

    def build(self):
        nc = self.nc
        S = self.S
        dep = self.depth
        self.d_xT = self.dram_in("xT", [D, S])
        self.d_cT = self.dram_in("cT", [128, 8])
        self.d_adaw = self.dram_in("ada_w", [dep, D, 9 * D])
        self.d_adab = self.dram_in("ada_bT", [dep, 128, 72])
        self.d_ng = self.dram_in("ngT", [dep, 128, 24])
        self.d_wg = self.dram_in("ffn_w_gate", [dep, 2, D, DFF])
        self.d_wu = self.dram_in("ffn_w_up", [dep, 2, D, DFF])
        self.d_wd = self.dram_in("ffn_w_down", [dep, 2, DFF, D])
        self.d_win = self.dram_in("w_in", [dep, D, INW])
        self.d_cw = self.dram_in("cwT", [dep, 128, 24])
        self.d_wco = self.dram_in("w_conv_out", [dep, D, D])
        self.d_wao = self.dram_in("w_attn_out", [dep, 512, D])
        self.d_wo = self.dram_in("w_o", [dep, D, D])
        self.d_bias = self.dram_in("biasT", [128, 24 * 256])
        self.d_fg = self.dram_in("fgT", [128, 8])
        self.d_ident = self.dram_in("ident", [128, 128])
        self.d_out = nc.dram_tensor("outT", [D, S], F32, kind="ExternalOutput").ap()
        self.adawb = nc.dram_tensor("adawb", [DEPTH, D, 9 * D], BF16).ap()
        self.wgb = nc.dram_tensor("wgb", [DEPTH, 2, D, DFF], BF16).ap()
        self.wub = nc.dram_tensor("wub", [DEPTH, 2, D, DFF], BF16).ap()
        self.wdb = nc.dram_tensor("wdb", [DEPTH, 2, DFF, D], BF16).ap()
        self.winb = nc.dram_tensor("winb", [DEPTH, D, INW], BF16).ap()
        self.wcob = nc.dram_tensor("wcob", [DEPTH, D, D], BF16).ap()
        self.waob = nc.dram_tensor("waob", [DEPTH, 512, D], BF16).ap()
        self.wob = nc.dram_tensor("wob", [DEPTH, D, D], BF16).ap()
        self.d_qk = nc.dram_tensor("qk_s", [S, 6, 512], BF16).ap()
        self.d_v = nc.dram_tensor("v_s", [S, 3, 528], BF16).ap()
        self.d_U = nc.dram_tensor("U_s", [3, S, 520], F32).ap()
        self.d_yc = nc.dram_tensor("yc_s", [D, S], BF16).ap()
        self.d_sga = nc.dram_tensor("sga_s", [D, S], BF16).ap()
        self.d_E = nc.dram_tensor("E_s", [128, 24 * 256], BF16).ap()

        with ExitStack() as st:
            self._st = st
            self.xT = self.sb("xT_sb", [128, 8, S], F32)
            self.hT = self.sb("hT", [128, 2, 4096], BF16)
            self.wr = self.sb("wr", [128, NW, 4096], BF16)
            self.RA = self.sb("RA", [128, 4096], BF16)
            self.RB = self.sb("RB", [128, 4096 + 16], BF16)
            self.RC = self.sb("RC", [128, 4096], BF16)
            self.sq = self.sb("sq", [128, 2, 512], BF16)
            self.tmp = self.sb("tmpf", [128, 2, 512], F32)
            self.rstd = self.sb("rstd", [128, 512], F32)
            self.sg = self.sb("sg", [128, 2, 512], BF16)
            self.ident = self.sb("ident_sb", [128, 128], BF16)
            self.ones = self.sb("ones_sb", [128, 128], BF16)
            self.modT = self.sb("modT", [128, DEPTH, 72], F32)
            self.A32 = self.sb("A32", [128, DEPTH, 24], F32)
            self.Gc = self.sb("Gc", [128, DEPTH, 24], F32)
            self.ng = self.sb("ng", [128, DEPTH, 24], F32)
            self.cw = self.sb("cw", [128, DEPTH, 24], F32)
            self.fg = self.sb("fg", [128, 8], F32)
            self.fg32 = self.sb("fg32", [128, 8], F32)
            self.epsc = self.sb("epsc", [128, 1], F32)
            self.csT = self.sb("csT", [128, 8], BF16)
            self.cT = self.sb("cT_sb", [128, 8], F32)
            self.adab = self.sb("adab", [128, DEPTH, 72], F32)
            self.rs8 = self.sb("rs8", [128, 8], F32)
            self.bar = self.sb("bar", [128, 2], F32)
            self.ps = [st.enter_context(nc.psum_tensor("ps%d" % i, [128, 512], F32)) for i in range(8)]
            self.ps_next = 0
            self.last_dve = None
            self.cnt = {"h": 0, "sq": 0, "tmp": 0, "sg": 0, "stg": 0, "stgv": 0, "alt": 0}

            self.wplan = []
            self.P.planning = True
            self.traverse()
            self.P.planning = False
            self.wpos = 0
            self.wpending = []
            self.ps_next = 0
            self.cnt = {k: 0 for k in self.cnt}
            self.traverse()
            self.P.emit(nc)

    def traverse(self):
        P = self.P
        dep = self.depth
        P.epoch = 0
        self.prologue()
        done = False
        for l in range(dep):
            P.epoch = l + 1
            self.ffn(l, 0)
            if self.stop_after == (l, "ffn1"):
                done = True
                break
            if self.mixer(l):
                done = True
                break
            if self.stop_after == (l, "mix"):
                done = True
                break
            self.ffn(l, 1)
            if self.stop_after == (l, "ffn2"):
                done = True
                break
        self.epilogue(raw=done)

    def xkeys(self, t):
        return [("x", c, t) for c in range(8)]

    def cast_jobs(self, l):
        jobs = []
        for c in range(8):
            for h in range(4):
                c0 = h * 2304
                jobs.append((self.adawb[l, c * 128:(c + 1) * 128, c0:c0 + 2304], self.d_adaw[l, c * 128:(c + 1) * 128, c0:c0 + 2304], ("wbf", l, "Ada")))

        def ffn_jobs(which):
            for c in range(8):
                jobs.append((self.wgb[l, which, c * 128:(c + 1) * 128, :], self.d_wg[l, which, c * 128:(c + 1) * 128, :], ("wbf", l, "Wg", which)))
                jobs.append((self.wub[l, which, c * 128:(c + 1) * 128, :], self.d_wu[l, which, c * 128:(c + 1) * 128, :], ("wbf", l, "Wu", which)))
            for c in range(NFC):
                jobs.append((self.wdb[l, which, c * 128:(c + 1) * 128, :], self.d_wd[l, which, c * 128:(c + 1) * 128, :], ("wbf", l, "Wd", which)))
        ffn_jobs(0)
        for c in range(8):
            for h in range(4):
                c0 = h * 2432
                jobs.append((self.winb[l, c * 128:(c + 1) * 128, c0:c0 + 2432], self.d_win[l, c * 128:(c + 1) * 128, c0:c0 + 2432], ("wbf", l, "Win")))
        for c in range(8):
            jobs.append((self.wcob[l, c * 128:(c + 1) * 128, :], self.d_wco[l, c * 128:(c + 1) * 128, :], ("wbf", l, "Wco")))
        for c in range(4):
            jobs.append((self.waob[l, c * 128:(c + 1) * 128, :], self.d_wao[l, c * 128:(c + 1) * 128, :], ("wbf", l, "Wao")))
        for c in range(8):
            jobs.append((self.wob[l, c * 128:(c + 1) * 128, :], self.d_wo[l, c * 128:(c + 1) * 128, :], ("wbf", l, "Wo")))
        ffn_jobs(1)
        return jobs

    def cast_step(self, l, n, after=None):
        P = self.P
        if l >= self.depth:
            return
        if l not in self._castjobs:
            self._castjobs[l] = self.cast_jobs(l)
            self._castpos[l] = 0
        jobs = self._castjobs[l]
        lo = self._castpos[l]
        hi = len(jobs) if n is None else min(len(jobs), lo + n)
        self._castpos[l] = hi
        for i in range(lo, hi):
            o, i_, key = jobs[i]
            grp = {"Ada": 0, "Wg": 1, "Wu": 1, "Wd": 1, "Win": 3, "Wco": 3, "Wao": 3, "Wo": 3}[key[2]]
            if grp == 1 and key[3] == 1:
                grp = 2
            me = P.add("pool", lambda eng, o=o, i_=i_: eng.dma_start(out=o, in_=i_), dma=("cast", l, grp), after=[after])
            if not P.planning:
                uk = ("castjob", l, i)
                P.bw[uk] = me
                self._castkeys.setdefault((l, grp), []).append(uk)

    def ada_stream(self, l):
        P = self.P
        pst, pk = self.psum_get()
        for blk in range(18):
            ws = self.wuse(("Ada", l, blk))
            for j in range(4):
                oc = blk * 4 + j

                def mm(eng, oc=oc, j=j, ws=ws, pst=pst):
                    r = None
                    for kc in range(8):
                        r = eng.matmul(pst[:, oc:oc + 1], self.wr[:, ws, kc * 512 + j * 128: kc * 512 + (j + 1) * 128],
                                       self.csT[:, kc:kc + 1], start=(kc == 0), stop=(kc == 7))
                    return r
                P.add("pe", mm, reads=[("w", ws), ("csT",)], writes=[pk])
            self.wrelease()
        P.add("dve", lambda eng, pst=pst: eng.tensor_tensor(out=self.modT[:, l, :], in0=pst[:, 0:72], in1=self.adab[:, l, :], op=ALU.add),
              reads=[pk, ("adab", l)], writes=[("mod", l)])
        for sub in range(3):
            sc = self.modT[:, l, (sub * 3 + 1) * 8:(sub * 3 + 1) * 8 + 8]
            gt = self.modT[:, l, (sub * 3 + 2) * 8:(sub * 3 + 2) * 8 + 8]
            P.add("dve", lambda eng, sub=sub, sc=sc: eng.scalar_tensor_tensor(
                out=self.A32[:, l, sub * 8:sub * 8 + 8], in0=sc, scalar=1.0, in1=self.ng[:, l, sub * 8:sub * 8 + 8],
                op0=ALU.add, op1=ALU.mult), reads=[("mod", l), ("ng", l)], writes=[("A32", l, sub)])
            P.add("dve", lambda eng, sub=sub, gt=gt: eng.tensor_scalar(
                out=self.Gc[:, l, sub * 8:sub * 8 + 8], in0=gt, scalar1=(1.0 if sub == 1 else 0.5), scalar2=None,
                op0=ALU.mult), reads=[("mod", l)], writes=[("Gc", l, sub)])

    def prologue(self):
        P = self.P
        nc = self.nc
        S = self.S
        self._castkeys = {}
        self._castjobs = {}
        self._castpos = {}
        for c in range(8):
            P.add("sp", lambda eng, c=c: eng.dma_start(out=self.xT[:, c, :], in_=self.d_xT[c * 128:(c + 1) * 128, :]),
                  writes=[("x", c, t) for t in range(self.NT)], dma=("xin", c))
        P.add("sp", lambda eng: [eng.dma_start(out=self.cT[:], in_=self.d_cT),
                                 eng.dma_start(out=self.fg[:], in_=self.d_fg)], writes=[("cT",), ("fg",)], dma=("c0",), ndma=2)
        for l in range(self.depth):
            P.add("sp", lambda eng, l=l: [eng.dma_start(out=self.adab[:, l, :], in_=self.d_adab[l]),
                                          eng.dma_start(out=self.ng[:, l, :], in_=self.d_ng[l]),
                                          eng.dma_start(out=self.cw[:, l, :], in_=self.d_cw[l])],
                  writes=[("adab", l), ("ng", l), ("cw", l)], dma=("c1", l), ndma=3)
        P.add("pool", lambda eng: eng.dma_start(out=self.ident[:], in_=self.d_ident), writes=[("ident",)], dma=("c2",))
        P.add("pool", lambda eng: eng.memset(self.ones[:], 1.0), writes=[("ones",)])
        P.add("pool", lambda eng: eng.memset(self.epsc[:], EPS), writes=[("epsc",)])
        self.cast_step(0, 32 + 38)
        P.add("dve", lambda eng: eng.tensor_scalar(out=self.fg32[:], in0=self.fg[:], scalar1=1.0, scalar2=None, op0=ALU.mult),
              reads=[("fg",)], writes=[("fg32",)])
        P.add("act", lambda eng: eng.activation(out=self.csT[:], in_=self.cT[:], func=AF.Silu), reads=[("cT",)], writes=[("csT",)])
        for i in range(6):
            k = ("w", i % NW)

            def ld(eng, i=i):
                return eng.dma_start(out=self.wr[:, i % NW, :].bitcast(F32)[:, 0:1024], in_=self.d_bias[:, i * 1024:(i + 1) * 1024])
            P.add("sp", ld, writes=[k], dma=("w", i % NW))

            def ex(eng, i=i):
                return eng.activation(out=self.RA[:, 0:1024], in_=self.wr[:, i % NW, :].bitcast(F32)[:, 0:1024], func=AF.Exp)
            P.add("act", ex, reads=[k], writes=[("RA", 0), ("RA", 1)])
            P.add("sp", lambda eng, i=i: eng.dma_start(out=self.d_E[:, i * 1024:(i + 1) * 1024], in_=self.RA[:, 0:1024]),
                  reads=[("RA", 0), ("RA", 1)], writes=[("Ed", i)], dma=("Est",))
        if not P.planning:
            for n in range(min(NW, len(self.wplan))):
                self.wload(n)
        self.ada_stream(0)

    def norm(self, l, sub, t):
        P = self.P
        hs = self.cnt["h"] % 2
        self.cnt["h"] += 1
        tc_ = slice(t * 512, (t + 1) * 512)
        pst, pk = self.psum_get()
        for c in range(8):
            ss = self.cnt["sq"] % 2
            self.cnt["sq"] += 1
            P.add("act", lambda eng, c=c, ss=ss: eng.activation(out=self.sq[:, ss, :], in_=self.xT[:, c, tc_], func=AF.Square),
                  reads=[("x", c, t)], writes=[("sq", ss)])
            P.add("pe", lambda eng, c=c, ss=ss, pst=pst: eng.matmul(pst[:], self.ones[:], self.sq[:, ss, :], start=(c == 0), stop=(c == 7)),
                  reads=[("sq", ss), ("ones",)], writes=[pk])
        P.add("act", lambda eng, pst=pst: eng.activation(out=self.rstd[:], in_=pst[:], func=AF.Sqrt, scale=1.0 / D, bias=self.epsc[:, 0:1]),
              reads=[pk, ("epsc",)], writes=[("rstd0",)])
        P.add("dve", lambda eng: eng.reciprocal(out=self.rstd[:], in_=self.rstd[:]), reads=[("rstd0",)], writes=[("rstd",), ("rstd0",)])
        for c in range(8):
            ts_ = self.cnt["tmp"] % 2
            self.cnt["tmp"] += 1
            P.add("dve", lambda eng, c=c, ts_=ts_: eng.tensor_tensor(out=self.tmp[:, ts_, :], in0=self.xT[:, c, tc_], in1=self.rstd[:], op=ALU.mult),
                  reads=[("x", c, t), ("rstd",)], writes=[("tmp", ts_)])
            P.add("act", lambda eng, c=c, ts_=ts_, hs=hs: eng.activation(
                out=self.hT[:, hs, c * 512:(c + 1) * 512], in_=self.tmp[:, ts_, :], func=AF.Identity,
                scale=self.A32[:, l, sub * 8 + c: sub * 8 + c + 1], bias=self.modT[:, l, sub * 24 + c: sub * 24 + c + 1]),
                reads=[("tmp", ts_), ("A32", l, sub), ("mod", l)], writes=[("h", hs, c)])
        return hs

    def ffn(self, l, which):
        P = self.P
        sub = 0 if which == 0 else 2
        NT = self.NT
        hs = self.norm(l, sub, 0)
        ncast = NT
        for t in range(NT):
            hk = [("h", hs, c) for c in range(8)]
            tc_ = slice(t * 512, (t + 1) * 512)
            for fb in range(6):
                nch = min(4, NFC - fb * 4)
                sg_ = self.wuse(("Wg", l, which, fb))
                su_ = self.wuse(("Wu", l, which, fb))
                ncols = nch * 128
                for j in range(nch):
                    f = fb * 4 + j
                    psg, kg = self.psum_get()
                    psu, ku = self.psum_get()

                    def mmg(eng, j=j, sg_=sg_, psg=psg, hs=hs, ncols=ncols):
                        r = None
                        for kc in range(8):
                            r = eng.matmul(psg[:], self.wr[:, sg_, kc * ncols + j * 128: kc * ncols + (j + 1) * 128],
                                           self.hT[:, hs, kc * 512:(kc + 1) * 512], start=(kc == 0), stop=(kc == 7))
                        return r
                    P.add("pe", mmg, reads=hk + [("w", sg_)], writes=[kg])

                    def mmu(eng, j=j, su_=su_, psu=psu, hs=hs, ncols=ncols):
                        r = None
                        for kc in range(8):
                            r = eng.matmul(psu[:], self.wr[:, su_, kc * ncols + j * 128: kc * ncols + (j + 1) * 128],
                                           self.hT[:, hs, kc * 512:(kc + 1) * 512], start=(kc == 0), stop=(kc == 7))
                        return r
                    P.add("pe", mmu, reads=hk + [("w", su_)], writes=[ku])
                    sgs = self.cnt["sg"] % 2
                    self.cnt["sg"] += 1
                    P.add("act", lambda eng, psg=psg, sgs=sgs: eng.activation(out=self.sg[:, sgs, :], in_=psg[:], func=AF.Silu),
                          reads=[kg], writes=[("sg", sgs)])
                    abuf, akey = self.abuf(f)
                    self.last_dve = P.add("dve", lambda eng, psu=psu, sgs=sgs, abuf=abuf: eng.tensor_tensor(out=abuf, in0=self.sg[:, sgs, :], in1=psu[:], op=ALU.mult),
                                          reads=[ku, ("sg", sgs)], writes=[akey])
                self.wrelease()
            if t + 1 < NT:
                hs_next = self.norm(l, sub, t + 1)
            if which == 0:
                if l == 0:
                    self.cast_step(0, (None if t == NT - 1 else -(-90 // NT)), after=self.last_dve)
                self.cast_step(l + 1, -(-32 // NT), after=self.last_dve)
            for dg in range(2):
                pss = [self.psum_get() for _ in range(4)]
                for fg in range(3):
                    nf = min(8, NFC - fg * 8)
                    sd = self.wuse(("Wd", l, which, dg, fg))
                    for j in range(nf):
                        f = fg * 8 + j
                        abuf, akey = self.abuf(f)

                        def mmd(eng, j=j, f=f, sd=sd, abuf=abuf, pss=pss):
                            r = None
                            for d4 in range(4):
                                r = eng.matmul(pss[d4][0][:], self.wr[:, sd, j * 512 + d4 * 128: j * 512 + (d4 + 1) * 128], abuf,
                                               start=(f == 0), stop=(f == NFC - 1))
                            return r
                        P.add("pe", mmd, reads=[akey, ("w", sd)], writes=[k for (_, k) in pss])
                    self.wrelease()
                for d4 in range(4):
                    d = dg * 4 + d4
                    pt, pk = pss[d4]
                    P.add("dve", lambda eng, pt=pt, d=d, tc_=tc_: eng.scalar_tensor_tensor(
                        out=self.xT[:, d, tc_], in0=pt[:], scalar=self.Gc[:, l, sub * 8 + d: sub * 8 + d + 1], in1=self.xT[:, d, tc_],
                        op0=ALU.mult, op1=ALU.add), reads=[pk, ("x", d, t), ("Gc", l, sub)], writes=[("x", d, t)])
            if t + 1 < NT:
                hs = hs_next

    def abuf(self, f):
        if f < 8:
            return self.RA[:, f * 512:(f + 1) * 512], ("RA", f)
        if f < 16:
            return self.RB[:, (f - 8) * 512:(f - 7) * 512], ("RB", f - 8)
        return self.RC[:, (f - 16) * 512:(f - 15) * 512], ("RC", f - 16)

    def mixer(self, l):
        self.mixer_p1(l)
        if self.stop_after == (l, "p1"):
            return True
        self.mixer_p2(l)
        if self.stop_after == (l, "p2"):
            return True
        self.mixer_p3(l)
        return False

    def evac_engine(self):
        self.cnt["alt"] += 1
        return "act" if self.cnt["alt"] % 2 == 0 else "dve"

    def copy_op(self, ename, out, in_):
        if ename == "act":
            return lambda eng: eng.activation(out=out, in_=in_, func=AF.Identity)
        return lambda eng: eng.tensor_copy(out=out, in_=in_)

    def mixer_p1(self, l):
        P = self.P
        NT = self.NT
        S = self.S
        sub = 1
        def stg_slot(kind="a"):
            if kind == "a":
                s = self.cnt["stg"] % 4
                self.cnt["stg"] += 1
                return s, [("RC", s)], s * 512
            s = self.cnt["stgv"] % 2
            self.cnt["stgv"] += 1
            return 4 + s, [("RC", 4 + 2 * s), ("RC", 5 + 2 * s)], 2048 + s * 1024
        uT = self.RB[:, 0:8 * 514].rearrange("p (c n) -> p c n", n=514)
        ukeys = [("RB", c) for c in range(8)] + [("RB", 8)]
        P.add("pool", lambda eng: eng.memset(uT[:, :, 0:2], 0.0), writes=ukeys)
        P.add("pool", lambda eng: eng.memset(self.RC[:], 1.0), writes=[("RC", c) for c in range(8)])
        hs = self.norm(l, sub, 0)
        for t in range(NT):
            hk = [("h", hs, c) for c in range(8)]
            tc_ = slice(t * 512, (t + 1) * 512)
            CB0, CC0, CH0, GC0, GA0 = 4608, 5632, 6656, 7680, 8704
            for half in range(2):
                ws = self.wuse(("Win", l, CC0 + half * 512))
                for j in range(4):
                    c = half * 4 + j
                    pt, pk = self.psum_get()
                    P.add("pe", self.fm_mm(pt, ws, j, hs), reads=hk + [("w", ws)], writes=[pk])
                    P.add("act", self.copy_op("act", self.RA[:, c * 512:(c + 1) * 512], pt[:]), reads=[pk], writes=[("RA", c)])
                self.wrelease()
            for half in range(2):
                ws = self.wuse(("Win", l, CH0 + half * 512))
                for j in range(4):
                    c = half * 4 + j
                    pt, pk = self.psum_get()
                    P.add("pe", self.fm_mm(pt, ws, j, hs), reads=hk + [("w", ws)], writes=[pk])
                    P.add("dve", lambda eng, c=c, pt=pt: eng.tensor_tensor(out=uT[:, c, 2:514], in0=self.RA[:, c * 512:(c + 1) * 512], in1=pt[:], op=ALU.mult),
                          reads=[pk, ("RA", c)], writes=ukeys_c(c))
                    tA = self.tmp[:, 0, :]
                    tB = self.tmp[:, 1, :]
                    kA, kB = ("tmp", 0), ("tmp", 1)
                    uk = ukeys_c(c)
                    w0 = self.cw[:, l, 0 * 8 + c: 0 * 8 + c + 1]
                    w1 = self.cw[:, l, 1 * 8 + c: 1 * 8 + c + 1]
                    w2 = self.cw[:, l, 2 * 8 + c: 2 * 8 + c + 1]
                    P.add("act", lambda eng, c=c, tA=tA, w2=w2: eng.activation(out=tA, in_=uT[:, c, 2:514], func=AF.Identity, scale=w2),
                          reads=uk + [("cw", l)], writes=[kA])
                    P.add("act", lambda eng, c=c, tB=tB, w1=w1: eng.activation(out=tB, in_=uT[:, c, 1:513], func=AF.Identity, scale=w1),
                          reads=uk + [("cw", l)], writes=[kB])
                    P.add("dve", lambda eng, tA=tA, tB=tB: eng.tensor_tensor(out=tA, in0=tA, in1=tB, op=ALU.add), reads=[kA, kB], writes=[kA])
                    P.add("act", lambda eng, c=c, tB=tB, w0=w0: eng.activation(out=tB, in_=uT[:, c, 0:512], func=AF.Identity, scale=w0),
                          reads=uk + [("cw", l)], writes=[kB])
                    P.add("dve", lambda eng, c=c, tA=tA, tB=tB: eng.tensor_tensor(out=self.RA[:, c * 512:(c + 1) * 512], in0=tA, in1=tB, op=ALU.add),
                          reads=[kA, kB], writes=[("RA", c)])
                self.wrelease()
            P.add("pool", lambda eng: eng.tensor_copy(out=uT[:, :, 0:2], in_=uT[:, :, 512:514]), reads=ukeys, writes=ukeys)
            for blk in range(9):
                col0 = blk * 512
                ws = self.wuse(("Win", l, col0))
                for s in range(4):
                    pt, pk = self.psum_get()

                    def mm(eng, s=s, ws=ws, pt=pt, hs=hs):
                        r = None
                        for kc in range(8):
                            r = eng.matmul(pt[:], self.hT[:, hs, kc * 512 + s * 128: kc * 512 + (s + 1) * 128],
                                           self.wr[:, ws, kc * 512:(kc + 1) * 512], start=(kc == 0), stop=(kc == 7))
                        return r
                    P.add("pe", mm, reads=hk + [("w", ws)], writes=[pk])
                    ss, sk, so = stg_slot("a" if blk < 6 else "v")
                    en = self.evac_engine()
                    r0 = t * 512 + s * 128
                    if blk < 6:
                        o = self.RC[:, so: so + 512]
                        P.add(en, self.copy_op(en, o, pt[:]), reads=[pk], writes=sk)
                        P.add("pool", lambda eng, o=o, blk=blk, r0=r0: eng.dma_start(out=self.d_qk[r0:r0 + 128, blk, :], in_=o),
                              reads=sk, writes=[("qk", blk, r0 // 128)], dma=("stg", ss))
                    else:
                        g = blk - 6
                        o = self.RC[:, so: so + 528].rearrange("p (h e) -> p h e", e=66)
                        P.add(en, self.copy_op(en, o[:, :, 0:64], pt[:].rearrange("p (h e) -> p h e", e=64)), reads=[pk], writes=sk)
                        o2 = self.RC[:, so: so + 528]
                        P.add("pool", lambda eng, o2=o2, g=g, r0=r0: eng.dma_start(out=self.d_v[r0:r0 + 128, g, :], in_=o2),
                              reads=sk, writes=[("v", g, r0 // 128)], dma=("stg", ss))
                self.wrelease()
            for half in range(2):
                ws = self.wuse(("Win", l, CB0 + half * 512))
                for j in range(4):
                    c = half * 4 + j
                    pt, pk = self.psum_get()
                    P.add("pe", self.fm_mm(pt, ws, j, hs), reads=hk + [("w", ws)], writes=[pk])
                    P.add("dve", lambda eng, c=c, pt=pt: eng.tensor_tensor(out=self.RA[:, c * 512:(c + 1) * 512], in0=self.RA[:, c * 512:(c + 1) * 512], in1=pt[:], op=ALU.mult),
                          reads=[pk, ("RA", c)], writes=[("RA", c)])
                self.wrelease()
            zk = [("RA", c) for c in range(8)]
            for db in range(2):
                wsg = self.wuse(("Win", l, GC0 + db * 512))
                wsc = self.wuse(("Wco", l, db))
                for j in range(4):
                    d = db * 4 + j
                    pg, kg = self.psum_get()
                    pc, kc_ = self.psum_get()
                    P.add("pe", self.fm_mm(pg, wsg, j, hs), reads=hk + [("w", wsg)], writes=[kg])

                    def mmc(eng, j=j, wsc=wsc, pc=pc):
                        r = None
                        for c in range(8):
                            r = eng.matmul(pc[:], self.wr[:, wsc, c * 512 + j * 128: c * 512 + (j + 1) * 128], self.RA[:, c * 512:(c + 1) * 512],
                                           start=(c == 0), stop=(c == 7))
                        return r
                    P.add("pe", mmc, reads=zk + [("w", wsc)], writes=[kc_])
                    sgs = self.cnt["sg"] % 2
                    self.cnt["sg"] += 1
                    P.add("act", lambda eng, pg=pg, sgs=sgs: eng.activation(out=self.sg[:, sgs, :], in_=pg[:], func=AF.Sigmoid),
                          reads=[kg], writes=[("sg", sgs)])
                    ss, sk, so = stg_slot()
                    o = self.RC[:, so: so + 512]
                    P.add("dve", lambda eng, o=o, pc=pc, sgs=sgs: eng.tensor_tensor(out=o, in0=self.sg[:, sgs, :], in1=pc[:], op=ALU.mult),
                          reads=[kc_, ("sg", sgs)], writes=sk)
                    P.add("pool", lambda eng, o=o, d=d, tc_=tc_: eng.dma_start(out=self.d_yc[d * 128:(d + 1) * 128, tc_], in_=o),
                          reads=sk, writes=[("yc", d, t)], dma=("stg", ss))
                self.wrelease()
            for db in range(2):
                ws = self.wuse(("Win", l, GA0 + db * 512))
                for j in range(4):
                    d = db * 4 + j
                    pt, pk = self.psum_get()
                    P.add("pe", self.fm_mm(pt, ws, j, hs), reads=hk + [("w", ws)], writes=[pk])
                    ss, sk, so = stg_slot()
                    o = self.RC[:, so: so + 512]
                    hnd = P.add("act", lambda eng, o=o, pt=pt: eng.activation(out=o, in_=pt[:], func=AF.Sigmoid), reads=[pk], writes=sk)
                    P.add("pool", lambda eng, o=o, d=d, tc_=tc_: eng.dma_start(out=self.d_sga[d * 128:(d + 1) * 128, tc_], in_=o),
                          reads=sk, writes=[("sga", d, t)], dma=("stg", ss))
                self.wrelease()
            self.cast_step(l + 1, -(-64 // NT), after=hnd)
            if t + 1 < NT:
                hs = self.norm(l, sub, t + 1)

    def fm_mm(self, pt, ws, j, hs):
        def mm(eng):
            r = None
            for kc in range(8):
                r = eng.matmul(pt[:], self.wr[:, ws, kc * 512 + j * 128: kc * 512 + (j + 1) * 128],
                               self.hT[:, hs, kc * 512:(kc + 1) * 512], start=(kc == 0), stop=(kc == 7))
            return r
        return mm

    def mixer_p2(self, l):
        P = self.P
        S = self.S
        if l + 1 < self.depth:
            self.ada_stream(l + 1)
        hflat = self.hT[:].rearrange("p a b -> p (a b)")
        ekeys = [("h", 0, c) for c in range(8)] + [("h", 1, c) for c in range(4)]
        P.add("sp", lambda eng: eng.dma_start(out=hflat[:, 0:6144], in_=self.d_E[:, :]), reads=[("Ed", i) for i in range(6)], writes=ekeys, dma=("Eld",))
        uo_base = 6144
        units = []
        for g, (win, dil) in enumerate(GROUPS):
            nb = (S // dil) // 128
            for r in range(dil):
                for b in range(nb):
                    units.append((g, dil, r, b))
        loaded = {}
        trd = {}
        rbk = [("RB", c) for c in range(9)]
        p2priv = [("p2V", i) for i in range(4)] + [("p2K", i) for i in range(3)]
        self.barrier(reads=[], writes=rbk + p2priv)

        def rec_loads(n):
            g, dil, r, b = units[n]
            row0 = dil * 128 * b + r
            rows = slice(row0, row0 + dil * 127 + 1, dil)
            blks = list(range(dil * b, dil * (b + 1)))
            qs = n % 2
            qkbuf = self.RA[:, qs * 1024:(qs + 1) * 1024]
            qkkeys = [("RA", 2 * qs), ("RA", 2 * qs + 1)]
            P.add("sp", lambda eng, qkbuf=qkbuf, rows=rows, g=g: [
                eng.dma_start(out=qkbuf[:, 0:512], in_=self.d_qk[rows, g, :]),
                eng.dma_start(out=qkbuf[:, 512:1024], in_=self.d_qk[rows, 3 + g, :])],
                reads=[("qk", g, bb) for bb in blks] + [("qk", 3 + g, bb) for bb in blks], writes=qkkeys, dma=("qkld", qs), ndma=2)
            vs = n % 4
            vbuf = self.RB[:, vs * 528: vs * 528 + 528]
            vkeys = [("p2V", vs)]
            P.add("sp", lambda eng, vbuf=vbuf, rows=rows, g=g: eng.dma_start(out=vbuf, in_=self.d_v[rows, g, :]),
                  reads=[("v", g, bb) for bb in blks], writes=vkeys, dma=("vld", vs))
            loaded[n] = (qkbuf, qkkeys, vbuf, vkeys, rows)

        def rec_tr(n):
            qkbuf, qkkeys, vbuf, vkeys, rows = loaded[n]
            pq, kq = self.psum_get()
            pkk, kk = self.psum_get()

            def trq(eng, qkbuf=qkbuf, pq=pq):
                r = None
                for hp in range(4):
                    r = eng.matmul(pq[:, hp * 128:(hp + 1) * 128], qkbuf[:, hp * 128:(hp + 1) * 128], self.ident[:], start=True, stop=True)
                return r
            P.add("pe", trq, reads=qkkeys + [("ident",)], writes=[kq])

            def trk(eng, qkbuf=qkbuf, pkk=pkk):
                r = None
                for hp in range(4):
                    r = eng.matmul(pkk[:, hp * 128:(hp + 1) * 128], qkbuf[:, 512 + hp * 128: 512 + (hp + 1) * 128], self.ident[:], start=True, stop=True)
                return r
            P.add("pe", trk, reads=qkkeys + [("ident",)], writes=[kk])
            qts = n % 2
            QT = self.RA[:, 2048 + qts * 1024: 2048 + (qts + 1) * 1024].rearrange("p (h n) -> p h n", n=512)
            qtk = ("RA", 4 + 2 * qts)
            qtk2 = ("RA", 5 + 2 * qts)
            P.add("act", self.copy_op("act", QT[0:64, 0, :], pq[0:64, :]), reads=[kq], writes=[qtk])
            P.add("act", self.copy_op("act", QT[64:128, 1, :], pq[64:128, :]), reads=[kq], writes=[qtk2])
            ks = n % 3
            KT = self.RB[:, 2112 + ks * 512: 2112 + (ks + 1) * 512]
            ktk = ("p2K", ks)
            P.add("dve", self.copy_op("dve", KT, pkk[:]), reads=[kk], writes=[ktk])
            trd[n] = (QT, qtk, qtk2, KT, ktk)

        P.add("pool", lambda eng: eng.memset(self.RA[:, 2048:4096], 0.0), writes=[("RA", c) for c in range(4, 8)])
        rec_loads(0)
        if len(units) > 1:
            rec_loads(1)
        rec_tr(0)
        prev = None
        for unit in range(len(units)):
            if True:
                if True:
                    g, dil, r, b = units[unit]
                    if b == 0:
                        prev = None
                    if unit + 2 < len(units):
                        rec_loads(unit + 2)
                    if unit + 1 < len(units):
                        rec_tr(unit + 1)
                    qkbuf, qkkeys, vbuf, vkeys, rows = loaded.pop(unit)
                    QT, qtk, qtk2, KT, ktk = trd.pop(unit)
                    pslot = unit
                    psl = pslot % 2
                    Pbuf = self.RC[:, psl * 2048:(psl + 1) * 2048]
                    pU = [self.psum_get(), self.psum_get()]
                    ncol = 256 if b > 0 else 128

                    def S_ops(hp, psS, kS, KT=KT, QT=QT, prev=prev, ktk=ktk, qtk=qtk, qtk2=qtk2):
                        def mm(eng):
                            r = None
                            for hh in range(2):
                                r = eng.matmul(psS[:, hh * 256: hh * 256 + 128], KT[:, hp * 128:(hp + 1) * 128], QT[:, hh, hp * 128:(hp + 1) * 128], start=True, stop=True)
                                if prev is not None:
                                    r = eng.matmul(psS[:, hh * 256 + 128: hh * 256 + 256], prev[0][:, hp * 128:(hp + 1) * 128], QT[:, hh, hp * 128:(hp + 1) * 128], start=True, stop=True)
                            return r
                        rd = [ktk, qtk, qtk2] + ([prev[1]] if prev is not None else [])
                        P.add("pe", mm, reads=rd, writes=[kS])

                    def E_ops(hp, psS, kS, Pbuf=Pbuf, psl=psl, ncol=ncol, g=g):
                        pk_ = ("RC", psl * 4 + hp)
                        src = psS[:].rearrange("p (h n) -> p h n", n=256)[:, :, 0:ncol]
                        dst = Pbuf[:, hp * 512:(hp + 1) * 512].rearrange("p (h n) -> p h n", n=256)[:, :, 0:ncol]
                        e0 = (g * 8 + 2 * hp) * 256
                        Ev = hflat[:, e0:e0 + 512].rearrange("p (h n) -> p h n", n=256)[:, :, 0:ncol]
                        if self.p2_level >= 1.7:
                            P.add("act", lambda eng: eng.activation(out=dst, in_=src, func=AF.Exp, scale=0.125), reads=[kS], writes=[pk_])
                        if self.p2_level >= 2:
                            P.add("dve", lambda eng: eng.tensor_tensor(out=dst, in0=dst, in1=Ev, op=ALU.mult), reads=[pk_] + ekeys, writes=[pk_])

                    def V_ops(hp, Pbuf=Pbuf, psl=psl, vbuf=vbuf, vkeys=vkeys, prev=prev, pU=pU):
                        pk_ = ("RC", psl * 4 + hp)
                        put, puk = pU[hp // 2]

                        def mm(eng):
                            r = None
                            for hh in range(2):
                                h = 2 * hp + hh
                                o = put[:, (h % 4) * 66: (h % 4) * 66 + 65]
                                r = eng.matmul(o, Pbuf[:, h * 256: h * 256 + 128], vbuf[:, h * 66: h * 66 + 65], start=True, stop=(prev is None))
                                if prev is not None:
                                    r = eng.matmul(o, Pbuf[:, h * 256 + 128: h * 256 + 256], prev[2][:, h * 66: h * 66 + 65], start=False, stop=True)
                            return r
                        rd = [pk_] + vkeys + (prev[3] if prev is not None else [])
                        P.add("pe", mm, reads=rd, writes=[puk])

                    if self.p2_level < 1.5:
                        prev = (KT, ktk, vbuf, vkeys)
                        continue
                    if self.p2_level < 3:
                        V_ops = lambda hp: None
                    psS = [None] * 4
                    psS[0] = self.psum_get()
                    S_ops(0, *psS[0])
                    E_ops(0, *psS[0])
                    psS[1] = self.psum_get()
                    S_ops(1, *psS[1])
                    E_ops(1, *psS[1])
                    for hp in range(4):
                        V_ops(hp)
                        if hp + 2 < 4:
                            psS[hp + 2] = self.psum_get()
                            S_ops(hp + 2, *psS[hp + 2])
                            E_ops(hp + 2, *psS[hp + 2])
                    if self.p2_level < 4:
                        prev = (KT, ktk, vbuf, vkeys)
                        continue
                    Uo = hflat[:, uo_base: uo_base + 1040].bitcast(F32)
                    uok = [("h", 1, 4), ("h", 1, 5), ("h", 1, 6)]
                    for half in range(2):
                        put, puk = pU[half]
                        src_ = put[:, 0:264].rearrange("p (h e) -> p h e", e=66)[:, :, 0:65]
                        dst_ = Uo[:, half * 260:(half + 1) * 260].rearrange("p (h e) -> p h e", e=65)
                        hnd = P.add("dve", self.copy_op("dve", dst_, src_), reads=[puk], writes=uok)
                    P.add("pool", lambda eng, Uo=Uo, rows=rows, g=g: eng.dma_start(out=self.d_U[g, rows, :], in_=Uo),
                          reads=uok, writes=[("U", g, r, b)], dma=("ust", 0))
                    self.cast_step(l + 1, 1, after=hnd)
                    prev = (KT, ktk, vbuf, vkeys)
        self.cast_step(l + 1, None)
        self.barrier(reads=[], writes=rbk + p2priv)

    def barrier(self, reads, writes):
        self.P.add("pool", lambda eng: eng.memset(self.bar[:, 0:1], 0.0), reads=list(reads), writes=list(writes) + [("bar",)])

    def mixer_p3(self, l):
        P = self.P
        S = self.S
        NT = self.NT
        sub = 1
        hflat = self.hT[:].rearrange("p a b -> p (a b)")
        hall = [("h", s_, c) for s_ in range(2) for c in range(8)]

        def f32v(off_bf, n):
            return hflat[:, off_bf: off_bf + 2 * n].bitcast(F32)
        Ust = [[f32v((k * 3 + g) * 1040, 520) for g in range(3)] for k in range(2)]
        Ustk = [[("Ust", k, g) for g in range(3)] for k in range(2)]
        obuf = [hflat[:, 6240 + i * 512: 6240 + (i + 1) * 512] for i in range(2)]
        obk = [("obuf", 0), ("obuf", 1)]
        priv = [k_ for ks in Ustk for k_ in ks] + obk
        self.barrier(reads=[], writes=hall + priv)
        def oT(fc):
            return (self.sq if fc < 2 else self.sg)[:, fc % 2, :]
        oTk = [("sq", 0), ("sq", 1), ("sg", 0), ("sg", 1)]
        state = {"blk": 0}

        def stageA1(t, s):
            k = s % 2
            r0 = t * 512 + s * 128
            for g, (win, dil) in enumerate(GROUPS):
                b = r0 // (dil * 128)
                rk = [("U", g, r, b) for r in range(dil)]
                P.add("sp", lambda eng, g=g, r0=r0, k=k: eng.dma_start(out=Ust[k][g], in_=self.d_U[g, r0:r0 + 128, :]),
                      reads=rk, writes=[Ustk[k][g]], dma=("uld", k, g))
            U0, U1, U2 = Ust[k]
            P.add("dve", lambda eng, U0=U0, U1=U1: eng.tensor_tensor(out=U0, in0=U0, in1=U1, op=ALU.add), reads=[Ustk[k][0], Ustk[k][1]], writes=[Ustk[k][0]])
            P.add("dve", lambda eng, U0=U0, U2=U2: eng.tensor_tensor(out=U0, in0=U0, in1=U2, op=ALU.add), reads=[Ustk[k][0], Ustk[k][2]], writes=[Ustk[k][0]])
            U3 = U0.rearrange("p (h e) -> p h e", e=65)
            P.add("dve", lambda eng, U3=U3: eng.reciprocal(out=self.rs8[:], in_=U3[:, :, 64]), reads=[Ustk[k][0]], writes=[("rs8",)])
            P.add("dve", lambda eng, U3=U3, k=k: eng.tensor_tensor(
                out=obuf[k].rearrange("p (h e) -> p h e", e=64), in0=U3[:, :, 0:64],
                in1=self.rs8[:].unsqueeze(2).to_broadcast([128, 8, 64]), op=ALU.mult),
                reads=[Ustk[k][0], ("rs8",)], writes=[obk[k]])

        def stageA2(t, s):
            k = s % 2
            pt, pk = self.psum_get()

            def tro(eng, k=k, pt=pt):
                r = None
                for fc in range(4):
                    r = eng.matmul(pt[:, fc * 128:(fc + 1) * 128], obuf[k][:, fc * 128:(fc + 1) * 128], self.ident[:], start=True, stop=True)
                return r
            P.add("pe", tro, reads=[obk[k], ("ident",)], writes=[pk])
            P.add("act", self.copy_op("act", self.sq[:, :, s * 128:(s + 1) * 128], pt[:, 0:256].rearrange("p (f t) -> p f t", t=128)),
                  reads=[pk], writes=oTk[0:2])
            P.add("act", self.copy_op("act", self.sg[:, :, s * 128:(s + 1) * 128], pt[:, 256:512].rearrange("p (f t) -> p f t", t=128)),
                  reads=[pk], writes=oTk[2:4])

        for s in range(4):
            stageA1(0, s)
            stageA2(0, s)
        for t in range(NT):
            tc_ = slice(t * 512, (t + 1) * 512)
            def ld_chunk(tt, d):
                tcc = slice(tt * 512, (tt + 1) * 512)
                P.add("sp", lambda eng, d=d, tcc=tcc: [
                    eng.dma_start(out=self.RA[:, d * 512:(d + 1) * 512], in_=self.d_yc[d * 128:(d + 1) * 128, tcc]),
                    eng.dma_start(out=self.RB[:, d * 512:(d + 1) * 512], in_=self.d_sga[d * 128:(d + 1) * 128, tcc])],
                    reads=[("yc", d, tt), ("sga", d, tt)], writes=[("RA", d), ("RB", d)], dma=("ycld", d), ndma=2)
            if t == 0:
                for d in range(8):
                    ld_chunk(0, d)
            wsa = self.wuse(("Wao", l))
            pts = []
            for d in range(8):
                pt, pk = self.psum_get()

                def mma(eng, d=d, pt=pt, wsa=wsa):
                    r = None
                    for fc in range(4):
                        r = eng.matmul(pt[:], self.wr[:, wsa, fc * 1024 + d * 128: fc * 1024 + (d + 1) * 128], oT(fc), start=(fc == 0), stop=(fc == 3))
                    return r
                P.add("pe", mma, reads=oTk + [("w", wsa)], writes=[pk])
                ts_ = self.cnt["tmp"] % 2
                self.cnt["tmp"] += 1
                tv = self.tmp[:, ts_, :]
                tk = ("tmp", ts_)
                P.add("dve", lambda eng, d=d, pt=pt, tv=tv: eng.tensor_tensor(out=tv, in0=pt[:], in1=self.RB[:, d * 512:(d + 1) * 512], op=ALU.mult),
                      reads=[pk, ("RB", d)], writes=[tk])
                P.add("dve", lambda eng, d=d, tv=tv: eng.tensor_tensor(out=self.RC[:, d * 512:(d + 1) * 512], in0=tv, in1=self.RA[:, d * 512:(d + 1) * 512], op=ALU.add),
                      reads=[tk, ("RA", d)], writes=[("RC", d)])
                if t + 1 < NT:
                    ld_chunk(t + 1, d)
                if d in (1, 3) and t + 1 < NT:
                    stageA1(t + 1, d // 2)
            self.wrelease()
            if t + 1 < NT:
                stageA2(t + 1, 0)
                stageA2(t + 1, 1)
                stageA1(t + 1, 2)
                stageA1(t + 1, 3)
            mk = [("RC", d) for d in range(8)]
            for db in range(2):
                if t + 1 < NT:
                    stageA2(t + 1, 2 + db)
                ws = self.wuse(("Wo", l, db))
                for j in range(4):
                    d = db * 4 + j
                    pt, pk = self.psum_get()

                    def mmo(eng, j=j, ws=ws, pt=pt):
                        r = None
                        for c in range(8):
                            r = eng.matmul(pt[:], self.wr[:, ws, c * 512 + j * 128: c * 512 + (j + 1) * 128], self.RC[:, c * 512:(c + 1) * 512],
                                           start=(c == 0), stop=(c == 7))
                        return r
                    P.add("pe", mmo, reads=mk + [("w", ws)], writes=[pk])
                    P.add("dve", lambda eng, pt=pt, d=d, tc_=tc_: eng.scalar_tensor_tensor(
                        out=self.xT[:, d, tc_], in0=pt[:], scalar=self.Gc[:, l, sub * 8 + d: sub * 8 + d + 1], in1=self.xT[:, d, tc_],
                        op0=ALU.mult, op1=ALU.add), reads=[pk, ("x", d, t), ("Gc", l, sub)], writes=[("x", d, t)])
                self.wrelease()
        self.barrier(reads=[], writes=hall + priv)

    def epilogue(self, raw=False):
        P = self.P
        NT = self.NT
        outk = []
        for t in range(NT):
            tc_ = slice(t * 512, (t + 1) * 512)
            if not raw:
                pst, pk = self.psum_get()
                for c in range(8):
                    ss = self.cnt["sq"] % 2
                    self.cnt["sq"] += 1
                    P.add("act", lambda eng, c=c, ss=ss, tc_=tc_: eng.activation(out=self.sq[:, ss, :], in_=self.xT[:, c, tc_], func=AF.Square),
                          reads=[("x", c, t)], writes=[("sq", ss)])
                    P.add("pe", lambda eng, c=c, ss=ss, pst=pst: eng.matmul(pst[:], self.ones[:], self.sq[:, ss, :], start=(c == 0), stop=(c == 7)),
                          reads=[("sq", ss), ("ones",)], writes=[pk])
                P.add("act", lambda eng, pst=pst: eng.activation(out=self.rstd[:], in_=pst[:], func=AF.Sqrt, scale=1.0 / D, bias=self.epsc[:, 0:1]),
                      reads=[pk, ("epsc",)], writes=[("rstd0",)])
                P.add("dve", lambda eng: eng.reciprocal(out=self.rstd[:], in_=self.rstd[:]), reads=[("rstd0",)], writes=[("rstd",), ("rstd0",)])
            for c in range(8):
                if raw:
                    P.add("sp", lambda eng, c=c, tc_=tc_: eng.dma_start(out=self.d_out[c * 128:(c + 1) * 128, tc_], in_=self.xT[:, c, tc_]),
                          reads=[("x", c, t)], writes=[("out", c, t)], dma=("ost", 0))
                    outk.append(("out", c, t))
                    continue
                ts_ = self.cnt["tmp"] % 2
                self.cnt["tmp"] += 1
                tk = ("tmp", ts_)
                P.add("dve", lambda eng, c=c, ts_=ts_, tc_=tc_: eng.tensor_tensor(out=self.tmp[:, ts_, :], in0=self.xT[:, c, tc_], in1=self.rstd[:], op=ALU.mult),
                      reads=[("x", c, t), ("rstd",)], writes=[tk])
                P.add("act", lambda eng, c=c, ts_=ts_: eng.activation(out=self.tmp[:, ts_, :], in_=self.tmp[:, ts_, :], func=AF.Identity,
                                                                      scale=self.fg32[:, c:c + 1]), reads=[tk, ("fg32",)], writes=[tk])
                P.add("sp", lambda eng, c=c, ts_=ts_, tc_=tc_: eng.dma_start(out=self.d_out[c * 128:(c + 1) * 128, tc_], in_=self.tmp[:, ts_, :]),
                      reads=[tk], writes=[("out", c, t)], dma=("ost", ts_))
                outk.append(("out", c, t))
        P.add("sp", None, reads=outk)


def ukeys_c(c):
    lo = (c * 514) // 512
    hi = ((c + 1) * 514 - 1) // 512
    return [("RB", k) for k in range(lo, hi + 1)]


_CACHE = {}


def get_builder(S=SEQ, depth=DEPTH, stop_after=None):
    key = (S, depth, stop_after)
    if key not in _CACHE:
        _CACHE[key] = Builder(S, depth, stop_after)
    return _CACHE[key]


def make_in_maps(inputs, S=SEQ, cores=NCORES, depth=DEPTH):
    f32 = np.float32
    x = np.asarray(inputs["x"], f32)
    c = np.asarray(inputs["c"], f32)
    ada_b = np.asarray(inputs["ada_b"], f32)
    norm_g = np.asarray(inputs["norm_g"], f32)
    conv_w = np.asarray(inputs["conv_w"], f32)
    rel_bias = np.asarray(inputs["rel_bias"], f32)
    final_g = np.asarray(inputs["final_g"], f32)
    ada_bT = np.ascontiguousarray(ada_b.reshape(DEPTH, 72, 128).transpose(0, 2, 1))
    ngT = np.ascontiguousarray(norm_g.reshape(DEPTH, 3, 8, 128).transpose(0, 3, 1, 2).reshape(DEPTH, 128, 24))
    cwT = np.ascontiguousarray(conv_w.reshape(DEPTH, 3, 8, 128).transpose(0, 3, 1, 2).reshape(DEPTH, 128, 24))
    fgT = np.ascontiguousarray(final_g.reshape(8, 128).T)
    bucket, valid = _bias_index()
    biasT = np.full((128, 24, 256), -30000.0, f32)
    for g in range(3):
        for h in range(8):
            vals = rel_bias[bucket[g], g * 8 + h]
            biasT[:, g * 8 + h, :] = np.where(valid, vals, f32(-30000.0))
    biasT = np.ascontiguousarray(biasT.reshape(128, 24 * 256))
    ident = np.eye(128, dtype=f32)
    shared = {
        "ada_w": np.ascontiguousarray(np.asarray(inputs["ada_w"], f32)),
        "ada_bT": ada_bT, "ngT": ngT, "cwT": cwT, "fgT": fgT, "biasT": biasT, "ident": ident,
        "ffn_w_gate": np.ascontiguousarray(np.asarray(inputs["ffn_w_gate"], f32)),
        "ffn_w_up": np.ascontiguousarray(np.asarray(inputs["ffn_w_up"], f32)),
        "ffn_w_down": np.ascontiguousarray(np.asarray(inputs["ffn_w_down"], f32)),
        "w_in": np.ascontiguousarray(np.asarray(inputs["w_in"], f32)),
        "w_conv_out": np.ascontiguousarray(np.asarray(inputs["w_conv_out"], f32)),
        "w_attn_out": np.ascontiguousarray(np.asarray(inputs["w_attn_out"], f32)),
        "w_o": np.ascontiguousarray(np.asarray(inputs["w_o"], f32)),
    }
    if depth != DEPTH:
        for k in ("ada_w", "ada_bT", "ngT", "cwT", "ffn_w_gate", "ffn_w_up", "ffn_w_down", "w_in", "w_conv_out", "w_attn_out", "w_o"):
            shared[k] = np.ascontiguousarray(shared[k][:depth])
    maps = []
    for b in range(cores):
        m = dict(shared)
        m["xT"] = np.ascontiguousarray(x[b, :S].T)
        m["cT"] = np.ascontiguousarray(c[b].reshape(8, 128).T)
        maps.append(m)
    return maps


def kernel(x, c, ada_w, ada_b, norm_g, ffn_w_gate, ffn_w_up, ffn_w_down, w_in, conv_w,
           w_conv_out, w_attn_out, w_o, rel_bias, final_g):
    inputs = dict(x=x, c=c, ada_w=ada_w, ada_b=ada_b, norm_g=norm_g, ffn_w_gate=ffn_w_gate, ffn_w_up=ffn_w_up,
                  ffn_w_down=ffn_w_down, w_in=w_in, conv_w=conv_w, w_conv_out=w_conv_out, w_attn_out=w_attn_out,
                  w_o=w_o, rel_bias=rel_bias, final_g=final_g)
    bld = get_builder()
    maps = make_in_maps(inputs)
    res = run_bass_kernel_spmd(bld.nc, maps, core_ids=list(range(NCORES)))
    out = np.stack([np.ascontiguousarray(r["outT"].T) for r in res.results], axis=0)
    return out.astype(np.float32)
```

```python
from contextlib import ExitStack

import numpy as np
import concourse.bass as bass
import concourse.mybir as mybir
from concourse.bass_utils import run_bass_kernel_spmd

F32 = mybir.dt.float32
BF16 = mybir.dt.bfloat16
AF = mybir.ActivationFunctionType
ALU = mybir.AluOpType

D = 1024
SEQ = 4096
DEPTH = 4
DFF = 2816
NFC = DFF // 128
INW = 9728
NCORES = 8
EPS = 1e-6
GROUPS = ((128, 1), (512, 4), (2048, 16))
NW = 3

ENGS = ("pe", "act", "dve", "pool", "sp")


class Op:
    __slots__ = ("eng", "fn", "deps", "dma", "sig", "sigval", "dmaval", "epoch", "ndma")

    def __init__(self, eng, fn, deps, dma, epoch, ndma):
        self.eng = eng
        self.fn = fn
        self.deps = deps
        self.dma = dma
        self.sig = False
        self.sigval = 0
        self.dmaval = 0
        self.epoch = epoch
        self.ndma = ndma


class Prog:
    def __init__(self):
        self.ops = {e: [] for e in ENGS}
        self.bw = {}
        self.br = {}
        self.epoch = 0
        self.planning = False
        self.dma_count = {}

    def add(self, eng, fn, reads=(), writes=(), dma=None, ndma=1, after=()):
        if self.planning:
            return None
        deps = set(a for a in after if a is not None)
        for k in reads:
            w = self.bw.get(k)
            if w is not None:
                deps.add(w)
        for k in writes:
            w = self.bw.get(k)
            if w is not None:
                deps.add(w)
            for r in self.br.get(k, {}).values():
                deps.add(r)
        idx = len(self.ops[eng])
        me = (eng, idx)
        op = Op(eng, fn, deps, dma, self.epoch, ndma)
        if dma is not None:
            c = self.dma_count.get(dma, 0) + ndma
            self.dma_count[dma] = c
            op.dmaval = 16 * c
        self.ops[eng].append(op)
        for k in reads:
            self.br.setdefault(k, {})[eng] = me
        for k in writes:
            self.bw[k] = me
            self.br[k] = {}
        return me

    def emit(self, nc):
        ops = self.ops
        raw_same = set()
        for e in ENGS:
            for op in ops[e]:
                nd = set()
                for (e2, i2) in op.deps:
                    t = ops[e2][i2]
                    if t.dma is None:
                        if e2 == e and e == "pe":
                            continue
                        t.sig = True
                    nd.add((e2, i2))
                op.deps = nd
        nepoch = self.epoch + 1
        for e in ENGS:
            cnt = [0] * nepoch
            for op in ops[e]:
                if op.dma is None and op.sig:
                    cnt[op.epoch] += 1
                    op.sigval = cnt[op.epoch]
        with ExitStack() as st:
            esem = {}
            for e in ENGS:
                for ep in range(nepoch):
                    if any((o.dma is None and o.sig and o.epoch == ep) for o in ops[e]):
                        esem[(e, ep)] = st.enter_context(nc.semaphore("s_%s_%d" % (e, ep)))
            dsem = {}
            for i, k in enumerate(sorted(self.dma_count.keys(), key=str)):
                dsem[k] = st.enter_context(nc.semaphore("d_%d" % i))
            self.nsems = len(esem) + len(dsem)
            block = st.enter_context(nc.Block())

            def run(ename, eng):
                known = {}
                for op in ops[ename]:
                    need = {}
                    for (e2, i2) in op.deps:
                        t = ops[e2][i2]
                        if t.dma is not None:
                            sk, val = ("d", t.dma), t.dmaval
                        else:
                            if e2 == ename and ename == "pe":
                                continue
                            sk, val = ("e", e2, t.epoch), t.sigval
                        if need.get(sk, 0) < val:
                            need[sk] = val
                    for sk, val in need.items():
                        if known.get(sk, 0) >= val:
                            continue
                        known[sk] = val
                        sem = dsem[sk[1]] if sk[0] == "d" else esem[(sk[1], sk[2])]
                        eng.wait_ge(sem, val)
                    if op.fn is None:
                        continue
                    r = op.fn(eng)
                    if op.dma is not None:
                        rl = r if isinstance(r, (list, tuple)) else [r]
                        assert len(rl) == op.ndma, (len(rl), op.ndma)
                        for ins in rl:
                            ins.then_inc(dsem[op.dma], 16)
                    elif op.sig:
                        ins = r[-1] if isinstance(r, (list, tuple)) else r
                        ins.then_inc(esem[(ename, op.epoch)], 1)

            @block.tensor
            def _(eng):
                run("pe", eng)

            @block.scalar
            def _(eng):
                run("act", eng)

            @block.vector
            def _(eng):
                run("dve", eng)

            @block.gpsimd
            def _(eng):
                run("pool", eng)

            @block.sync
            def _(eng):
                run("sp", eng)


def _t5_bucket(dist):
    exact = 16
    d = np.maximum(dist, 1).astype(np.float32)
    large = exact + (np.log(d / exact) / np.log(2048 / exact) * (32 - exact)).astype(np.int32)
    large = np.minimum(large, 31)
    return np.where(dist < exact, dist, large).astype(np.int32)


def _bias_index():
    j = np.arange(128)[:, None]
    n = np.arange(256)[None, :]
    i = np.where(n < 128, n, n - 128)
    rel = np.where(n < 128, i - j, i - j + 128)
    valid = (rel >= 0) & (rel <= 128)
    buckets = []
    for (_, dil) in GROUPS:
        buckets.append(_t5_bucket(np.maximum(rel, 0) * dil))
    return np.stack(buckets, 0), valid


class Builder:
    def __init__(self, S=SEQ, depth=DEPTH, stop_after=None, p2_level=9):
        self.p2_level = p2_level
        self.S = S
        self.NT = S // 512
        self.depth = depth
        self.stop_after = stop_after
        self.nc = bass.Bass("TRN2", target_bir_lowering=False)
        self.P = Prog()
        self.build()

    def dram_in(self, name, shape, dt=F32):
        return self.nc.dram_tensor(name, list(shape), dt, kind="ExternalInput").ap()

    def sb(self, name, shape, dt):
        return self._st.enter_context(self.nc.sbuf_tensor(name, list(shape), dt))

    def psum_get(self):
        b = self.ps_next
        self.ps_next = (b + 1) % 8
        return self.ps[b], ("ps", b)

    def wuse(self, desc):
        if self.P.planning:
            self.wplan.append(desc)
            return 0
        n = self.wpos
        assert self.wplan[n] == desc, (n, self.wplan[n], desc)
        self.wpos += 1
        self.wpending.append(n)
        return n % NW

    def wrelease(self):
        if self.P.planning:
            return
        for n0 in self.wpending:
            n = n0 + NW
            if n < len(self.wplan):
                self.wload(n)
        self.wpending = []

    def wload(self, n):
        desc = self.wplan[n]
        slot = n % NW
        src, rk = self.wsrc(desc)
        dst = self.wr[:, slot, :]
        nparts = len(src)
        views = []
        off = 0
        for (ap, nrow_chunks, ncols) in src:
            views.append((ap, off, nrow_chunks, ncols))
            off += nrow_chunks * ncols

        def fn(eng, views=views, slot=slot):
            res = []
            for (ap, off, nrc, ncols) in views:
                for c in range(nrc):
                    o = self.wr[:, slot, off + c * ncols: off + (c + 1) * ncols]
                    res.append(eng.dma_start(out=o, in_=ap[c * 128:(c + 1) * 128, :]))
            return res
        nd = sum(v[2] for v in views)
        grp = {"Ada": 0, "Wg": 1, "Wu": 1, "Wd": 1, "Win": 3, "Wco": 3, "Wao": 3, "Wo": 3}[rk[2]]
        if grp == 1 and rk[3] == 1:
            grp = 2
        self.P.add("sp", fn, reads=list(self._castkeys.get((rk[1], grp), [])), writes=[("w", slot)], dma=("w", slot), ndma=nd)

    def wsrc(self, desc):
        kind = desc[0]
        l = desc[1]
        if kind == "Wg" or kind == "Wu":
            which, fb = desc[2], desc[3]
            t = self.wgb if kind == "Wg" else self.wub
            c0 = fb * 512
            nc_ = min(512, DFF - c0)
            return [(t[l, which, :, c0:c0 + nc_], 8, nc_)], ("wbf", l, kind, which)
        if kind == "Wd":
            which, dg, fg = desc[2], desc[3], desc[4]
            f0 = fg * 8
            nf = min(8, NFC - f0)
            return [(self.wdb[l, which, f0 * 128:(f0 + nf) * 128, dg * 512:(dg + 1) * 512], nf, 512)], ("wbf", l, kind, which)
        if kind == "Win":
            c0 = desc[2]
            return [(self.winb[l, :, c0:c0 + 512], 8, 512)], ("wbf", l, kind)
        if kind == "Wco":
            db = desc[2]
            return [(self.wcob[l, :, db * 512:(db + 1) * 512], 8, 512)], ("wbf", l, kind)
        if kind == "Wo":
            db = desc[2]
            return [(self.wob[l, :, db * 512:(db + 1) * 512], 8, 512)], ("wbf", l, kind)
        if kind == "Ada":
            blk = desc[2]
            return [(self.adawb[l, :, blk * 512:(blk + 1) * 512], 8, 512)], ("wbf", l, kind)
        if kind == "Wao":
            return [(self.waob[l, :, :], 4, 1024)], ("wbf", l, kind)
        raise ValueError(desc)

    def build(self):
        nc = self.nc
        S = self.S
        dep = self.depth
        self.d_xT = self.dram_in("xT", [D, S])
        self.d_cT = self.dram_in("cT", [128, 8])
        self.d_adaw = self.dram_in("ada_w", [dep, D, 9 * D])
        self.d_adab = self.dram_in("ada_bT", [dep, 128, 72])
        self.d_ng = self.dram_in("ngT", [dep, 128, 24])
        self.d_wg = self.dram_in("ffn_w_gate", [dep, 2, D, DFF])
        self.d_wu = self.dram_in("ffn_w_up", [dep, 2, D, DFF])
        self.d_wd = self.dram_in("ffn_w_down", [dep, 2, DFF, D])
        self.d_win = self.dram_in("w_in", [dep, D, INW])
        self.d_cw = self.dram_in("cwT", [dep, 128, 24])
        self.d_wco = self.dram_in("w_conv_out", [dep, D, D])
        self.d_wao = self.dram_in("w_attn_out", [dep, 512, D])
        self.d_wo = self.dram_in("w_o", [dep, D, D])
        self.d_bias = self.dram_in("biasT", [128, 24 * 256])
        self.d_fg = self.dram_in("fgT", [128, 8])
        self.d_ident = self.dram_in("ident", [128, 128])
        self.d_out = nc.dram_tensor("outT", [D, S], F32, kind="ExternalOutput").ap()
        self.adawb = nc.dram_tensor("adawb", [DEPTH, D, 9 * D], BF16).ap()
        self.wgb = nc.dram_tensor("wgb", [DEPTH, 2, D, DFF], BF16).ap()
        self.wub = nc.dram_tensor("wub", [DEPTH, 2, D, DFF], BF16).ap()
        self.wdb = nc.dram_tensor("wdb", [DEPTH, 2, DFF, D], BF16).ap()
        self.winb = nc.dram_tensor("winb", [DEPTH, D, INW], BF16).ap()
        self.wcob = nc.dram_tensor("wcob", [DEPTH, D, D], BF16).ap()
        self.waob = nc.dram_tensor("waob", [DEPTH, 512, D], BF16).ap()
        self.wob = nc.dram_tensor("wob", [DEPTH, D, D], BF16).ap()
        self.d_qk = nc.dram_tensor("qk_s", [S, 6, 512], BF16).ap()
        self.d_v = nc.dram_tensor("v_s", [S, 3, 528], BF16).ap()
        self.d_U = nc.dram_tensor("U_s", [3, S, 520], F32).ap()
        self.d_yc = nc.dram_tensor("yc_s", [D, S], BF16).ap()
        self.d_sga = nc.dram_tensor("sga_s", [D, S], BF16).ap()
        self.d_E = nc.dram_tensor("E_s", [128, 24 * 256], BF16).ap()

        with ExitStack() as st:
            self._st = st
            self.xT = self.sb("xT_sb", [128, 8, S], F32)
            self.hT = self.sb("hT", [128, 2, 4096], BF16)
            self.wr = self.sb("wr", [128, NW, 4096], BF16)
            self.RA = self.sb("RA", [128, 4096], BF16)
            self.RB = self.sb("RB", [128, 4096 + 16], BF16)
            self.RC = self.sb("RC", [128, 4096], BF16)
            self.sq = self.sb("sq", [128, 2, 512], BF16)
            self.tmp = self.sb("tmpf", [128, 2, 512], F32)
            self.rstd = self.sb("rstd", [128, 512], F32)
            self.sg = self.sb("sg", [128, 2, 512], BF16)
            self.ident = self.sb("ident_sb", [128, 128], BF16)
            self.ones = self.sb("ones_sb", [128, 128], BF16)
            self.modT = self.sb("modT", [128, DEPTH, 72], F32)
            self.A32 = self.sb("A32", [128, DEPTH, 24], F32)
            self.Gc = self.sb("Gc", [128, DEPTH, 24], F32)
            self.ng = self.sb("ng", [128, DEPTH, 24], F32)
            self.cw = self.sb("cw", [128, DEPTH, 24], F32)
            self.fg = self.sb("fg", [128, 8], F32)
            self.fg32 = self.sb("fg32", [128, 8], F32)
            self.epsc = self.sb("epsc", [128, 1], F32)
            self.csT = self.sb("csT", [128, 8], BF16)
            self.cT = self.sb("cT_sb", [128, 8], F32)
            self.adab = self.sb("adab", [128, DEPTH, 72], F32)
            self.rs8 = self.sb("rs8", [128, 8], F32)
            self.bar = self.sb("bar", [128, 2], F32)
            self.ps = [st.enter_context(nc.psum_tensor("ps%d" % i, [128, 512], F32)) for i in range(8)]
            self.ps_next = 0
            self.last_dve = None
            self.cnt = {"h": 0, "sq": 0, "tmp": 0, "sg": 0, "stg": 0, "stgv": 0, "alt": 0}

            self.wplan = []
            self.P.planning = True
            self.traverse()
            self.P.planning = False
            self.wpos = 0
            self.wpending = []
            self.ps_next = 0
            self.cnt = {k: 0 for k in self.cnt}
            self.traverse()
            self.P.emit(nc)

    def traverse(self):
        P = self.P
        dep = self.depth
        P.epoch = 0
        self.prologue()
        done = False
        for l in range(dep):
            P.epoch = l + 1
            self.ffn(l, 0)
            if self.stop_after == (l, "ffn1"):
                done = True
                break
            if self.mixer(l):
                done = True
                break
            if self.stop_after == (l, "mix"):
                done = True
                break
            self.ffn(l, 1)
            if self.stop_after == (l, "ffn2"):
                done = True
                break
        self.epilogue(raw=done)

    def xkeys(self, t):
        return [("x", c, t) for c in range(8)]

    def cast_jobs(self, l):
        jobs = []
        for c in range(8):
            for h in range(4):
                c0 = h * 2304
                jobs.append((self.adawb[l, c * 128:(c + 1) * 128, c0:c0 + 2304], self.d_adaw[l, c * 128:(c + 1) * 128, c0:c0 + 2304], ("wbf", l, "Ada")))

        def ffn_jobs(which):
            for c in range(8):
                jobs.append((self.wgb[l, which, c * 128:(c + 1) * 128, :], self.d_wg[l, which, c * 128:(c + 1) * 128, :], ("wbf", l, "Wg", which)))
                jobs.append((self.wub[l, which, c * 128:(c + 1) * 128, :], self.d_wu[l, which, c * 128:(c + 1) * 128, :], ("wbf", l, "Wu", which)))
            for c in range(NFC):
                jobs.append((self.wdb[l, which, c * 128:(c + 1) * 128, :], self.d_wd[l, which, c * 128:(c + 1) * 128, :], ("wbf", l, "Wd", which)))
        ffn_jobs(0)
        for c in range(8):
            for h in range(4):
                c0 = h * 2432
                jobs.append((self.winb[l, c * 128:(c + 1) * 128, c0:c0 + 2432], self.d_win[l, c * 128:(c + 1) * 128, c0:c0 + 2432], ("wbf", l, "Win")))
        for c in range(8):
            jobs.append((self.wcob[l, c * 128:(c + 1) * 128, :], self.d_wco[l, c * 128:(c + 1) * 128, :], ("wbf", l, "Wco")))
        for c in range(4):
            jobs.append((self.waob[l, c * 128:(c + 1) * 128, :], self.d_wao[l, c * 128:(c + 1) * 128, :], ("wbf", l, "Wao")))
        for c in range(8):
            jobs.append((self.wob[l, c * 128:(c + 1) * 128, :], self.d_wo[l, c * 128:(c + 1) * 128, :], ("wbf", l, "Wo")))
        ffn_jobs(1)
        return jobs

    def cast_step(self, l, n, after=None):
        P = self.P
        if l >= self.depth:
            return
        if l not in self._castjobs:
            self._castjobs[l] = self.cast_jobs(l)
            self._castpos[l] = 0
        jobs = self._castjobs[l]
        lo = self._castpos[l]
        hi = len(jobs) if n is None else min(len(jobs), lo + n)
        self._castpos[l] = hi
        for i in range(lo, hi):
            o, i_, key = jobs[i]
            grp = {"Ada": 0, "Wg": 1, "Wu": 1, "Wd": 1, "Win": 3, "Wco": 3, "Wao": 3, "Wo": 3}[key[2]]
            if grp == 1 and key[3] == 1:
                grp = 2
            me = P.add("pool", lambda eng, o=o, i_=i_: eng.dma_start(out=o, in_=i_), dma=("cast", l, grp), after=[after])
            if not P.planning:
                uk = ("castjob", l, i)
                P.bw[uk] = me
                self._castkeys.setdefault((l, grp), []).append(uk)

    def ada_stream(self, l):
        for blk in range(18):
            self.ada_block(l, blk)
        self.ada_finish(l)

    def ada_block(self, l, blk):
        P = self.P
        pst, pk = self.psum_get()
        ws = self.wuse(("Ada", l, blk))
        for j in range(4):
            def mm(eng, j=j, ws=ws, pst=pst):
                r = None
                for kc in range(8):
                    r = eng.matmul(pst[:, j:j + 1], self.wr[:, ws, kc * 512 + j * 128: kc * 512 + (j + 1) * 128],
                                   self.csT[:, kc:kc + 1], start=(kc == 0), stop=(kc == 7))
                return r
            P.add("pe", mm, reads=[("w", ws), ("csT",)], writes=[pk])
        self.wrelease()
        oc0 = blk * 4
        P.add("dve", lambda eng, pst=pst, oc0=oc0: eng.tensor_tensor(out=self.modT[:, l, oc0:oc0 + 4], in0=pst[:, 0:4], in1=self.adab[:, l, oc0:oc0 + 4], op=ALU.add),
              reads=[pk, ("adab", l)], writes=[("modblk", l, blk)])

    def ada_finish(self, l):
        P = self.P
        P.add("dve", lambda eng: eng.tensor_copy(out=self.bar[:, 1:2], in_=self.bar[:, 1:2]),
              reads=[("modblk", l, blk) for blk in range(18)], writes=[("mod", l), ("bar1",)])
        for sub in range(3):
            sc = self.modT[:, l, (sub * 3 + 1) * 8:(sub * 3 + 1) * 8 + 8]
            gt = self.modT[:, l, (sub * 3 + 2) * 8:(sub * 3 + 2) * 8 + 8]
            P.add("dve", lambda eng, sub=sub, sc=sc: eng.scalar_tensor_tensor(
                out=self.A32[:, l, sub * 8:sub * 8 + 8], in0=sc, scalar=1.0, in1=self.ng[:, l, sub * 8:sub * 8 + 8],
                op0=ALU.add, op1=ALU.mult), reads=[("mod", l), ("ng", l)], writes=[("A32", l, sub)])
            P.add("dve", lambda eng, sub=sub, gt=gt: eng.tensor_scalar(
                out=self.Gc[:, l, sub * 8:sub * 8 + 8], in0=gt, scalar1=(1.0 if sub == 1 else 0.5), scalar2=None,
                op0=ALU.mult), reads=[("mod", l)], writes=[("Gc", l, sub)])

    def prologue(self):
        P = self.P
        nc = self.nc
        S = self.S
        self._castkeys = {}
        self._castjobs = {}
        self._castpos = {}
        for c in range(8):
            P.add("sp", lambda eng, c=c: eng.dma_start(out=self.xT[:, c, :], in_=self.d_xT[c * 128:(c + 1) * 128, :]),
                  writes=[("x", c, t) for t in range(self.NT)], dma=("xin", c))
        P.add("sp", lambda eng: [eng.dma_start(out=self.cT[:], in_=self.d_cT),
                                 eng.dma_start(out=self.fg[:], in_=self.d_fg)], writes=[("cT",), ("fg",)], dma=("c0",), ndma=2)
        for l in range(self.depth):
            P.add("sp", lambda eng, l=l: [eng.dma_start(out=self.adab[:, l, :], in_=self.d_adab[l]),
                                          eng.dma_start(out=self.ng[:, l, :], in_=self.d_ng[l]),
                                          eng.dma_start(out=self.cw[:, l, :], in_=self.d_cw[l])],
                  writes=[("adab", l), ("ng", l), ("cw", l)], dma=("c1", l), ndma=3)
        P.add("pool", lambda eng: eng.dma_start(out=self.ident[:], in_=self.d_ident), writes=[("ident",)], dma=("c2",))
        P.add("pool", lambda eng: eng.memset(self.ones[:], 1.0), writes=[("ones",)])
        P.add("pool", lambda eng: eng.memset(self.epsc[:], EPS), writes=[("epsc",)])
        self.cast_step(0, 32 + 38)
        P.add("dve", lambda eng: eng.tensor_scalar(out=self.fg32[:], in0=self.fg[:], scalar1=1.0, scalar2=None, op0=ALU.mult),
              reads=[("fg",)], writes=[("fg32",)])
        P.add("act", lambda eng: eng.activation(out=self.csT[:], in_=self.cT[:], func=AF.Silu), reads=[("cT",)], writes=[("csT",)])
        for i in range(6):
            k = ("w", i % NW)

            def ld(eng, i=i):
                return eng.dma_start(out=self.wr[:, i % NW, :].bitcast(F32)[:, 0:1024], in_=self.d_bias[:, i * 1024:(i + 1) * 1024])
            P.add("sp", ld, writes=[k], dma=("w", i % NW))

            def ex(eng, i=i):
                return eng.activation(out=self.RA[:, 0:1024], in_=self.wr[:, i % NW, :].bitcast(F32)[:, 0:1024], func=AF.Exp)
            P.add("act", ex, reads=[k], writes=[("RA", 0), ("RA", 1)])
            P.add("sp", lambda eng, i=i: eng.dma_start(out=self.d_E[:, i * 1024:(i + 1) * 1024], in_=self.RA[:, 0:1024]),
                  reads=[("RA", 0), ("RA", 1)], writes=[("Ed", i)], dma=("Est",))
        if not P.planning:
            for n in range(min(NW, len(self.wplan))):
                self.wload(n)
        self.ada_stream(0)

    def norm(self, l, sub, t):
        P = self.P
        hs = self.cnt["h"] % 2
        self.cnt["h"] += 1
        tc_ = slice(t * 512, (t + 1) * 512)
        pst, pk = self.psum_get()
        for c in range(8):
            ss = self.cnt["sq"] % 2
            self.cnt["sq"] += 1
            P.add("act", lambda eng, c=c, ss=ss: eng.activation(out=self.sq[:, ss, :], in_=self.xT[:, c, tc_], func=AF.Square),
                  reads=[("x", c, t)], writes=[("sq", ss)])
            P.add("pe", lambda eng, c=c, ss=ss, pst=pst: eng.matmul(pst[:], self.ones[:], self.sq[:, ss, :], start=(c == 0), stop=(c == 7)),
                  reads=[("sq", ss), ("ones",)], writes=[pk])
        P.add("act", lambda eng, pst=pst: eng.activation(out=self.rstd[:], in_=pst[:], func=AF.Sqrt, scale=1.0 / D, bias=self.epsc[:, 0:1]),
              reads=[pk, ("epsc",)], writes=[("rstd0",)])
        P.add("dve", lambda eng: eng.reciprocal(out=self.rstd[:], in_=self.rstd[:]), reads=[("rstd0",)], writes=[("rstd",), ("rstd0",)])
        for c in range(8):
            ts_ = self.cnt["tmp"] % 2
            self.cnt["tmp"] += 1
            P.add("dve", lambda eng, c=c, ts_=ts_: eng.tensor_tensor(out=self.tmp[:, ts_, :], in0=self.xT[:, c, tc_], in1=self.rstd[:], op=ALU.mult),
                  reads=[("x", c, t), ("rstd",)], writes=[("tmp", ts_)])
            P.add("act", lambda eng, c=c, ts_=ts_, hs=hs: eng.activation(
                out=self.hT[:, hs, c * 512:(c + 1) * 512], in_=self.tmp[:, ts_, :], func=AF.Identity,
                scale=self.A32[:, l, sub * 8 + c: sub * 8 + c + 1], bias=self.modT[:, l, sub * 24 + c: sub * 24 + c + 1]),
                reads=[("tmp", ts_), ("A32", l, sub), ("mod", l)], writes=[("h", hs, c)])
        return hs

    def ffn(self, l, which):
        P = self.P
        sub = 0 if which == 0 else 2
        NT = self.NT
        hs = self.norm(l, sub, 0)
        ncast = NT
        for t in range(NT):
            hk = [("h", hs, c) for c in range(8)]
            tc_ = slice(t * 512, (t + 1) * 512)
            for fb in range(6):
                nch = min(4, NFC - fb * 4)
                sg_ = self.wuse(("Wg", l, which, fb))
                su_ = self.wuse(("Wu", l, which, fb))
                ncols = nch * 128
                for j in range(nch):
                    f = fb * 4 + j
                    psg, kg = self.psum_get()
                    psu, ku = self.psum_get()

                    def mmg(eng, j=j, sg_=sg_, psg=psg, hs=hs, ncols=ncols):
                        r = None
                        for kc in range(8):
                            r = eng.matmul(psg[:], self.wr[:, sg_, kc * ncols + j * 128: kc * ncols + (j + 1) * 128],
                                           self.hT[:, hs, kc * 512:(kc + 1) * 512], start=(kc == 0), stop=(kc == 7))
                        return r
                    P.add("pe", mmg, reads=hk + [("w", sg_)], writes=[kg])

                    def mmu(eng, j=j, su_=su_, psu=psu, hs=hs, ncols=ncols):
                        r = None
                        for kc in range(8):
                            r = eng.matmul(psu[:], self.wr[:, su_, kc * ncols + j * 128: kc * ncols + (j + 1) * 128],
                                           self.hT[:, hs, kc * 512:(kc + 1) * 512], start=(kc == 0), stop=(kc == 7))
                        return r
                    P.add("pe", mmu, reads=hk + [("w", su_)], writes=[ku])
                    sgs = self.cnt["sg"] % 2
                    self.cnt["sg"] += 1
                    P.add("act", lambda eng, psg=psg, sgs=sgs: eng.activation(out=self.sg[:, sgs, :], in_=psg[:], func=AF.Silu),
                          reads=[kg], writes=[("sg", sgs)])
                    abuf, akey = self.abuf(f)
                    self.last_dve = P.add("dve", lambda eng, psu=psu, sgs=sgs, abuf=abuf: eng.tensor_tensor(out=abuf, in0=self.sg[:, sgs, :], in1=psu[:], op=ALU.mult),
                                          reads=[ku, ("sg", sgs)], writes=[akey])
                self.wrelease()
            if t + 1 < NT:
                hs_next = self.norm(l, sub, t + 1)
            if which == 0:
                if l == 0:
                    self.cast_step(0, -(-52 // NT), after=self.last_dve)
                self.cast_step(l + 1, -(-32 // NT), after=self.last_dve)
            for dg in range(2):
                pss = [self.psum_get() for _ in range(4)]
                for fg in range(3):
                    nf = min(8, NFC - fg * 8)
                    sd = self.wuse(("Wd", l, which, dg, fg))
                    for j in range(nf):
                        f = fg * 8 + j
                        abuf, akey = self.abuf(f)

                        def mmd(eng, j=j, f=f, sd=sd, abuf=abuf, pss=pss):
                            r = None
                            for d4 in range(4):
                                r = eng.matmul(pss[d4][0][:], self.wr[:, sd, j * 512 + d4 * 128: j * 512 + (d4 + 1) * 128], abuf,
                                               start=(f == 0), stop=(f == NFC - 1))
                            return r
                        P.add("pe", mmd, reads=[akey, ("w", sd)], writes=[k for (_, k) in pss])
                    self.wrelease()
                for d4 in range(4):
                    d = dg * 4 + d4
                    pt, pk = pss[d4]
                    P.add("dve", lambda eng, pt=pt, d=d, tc_=tc_: eng.scalar_tensor_tensor(
                        out=self.xT[:, d, tc_], in0=pt[:], scalar=self.Gc[:, l, sub * 8 + d: sub * 8 + d + 1], in1=self.xT[:, d, tc_],
                        op0=ALU.mult, op1=ALU.add), reads=[pk, ("x", d, t), ("Gc", l, sub)], writes=[("x", d, t)])
            if t + 1 < NT:
                hs = hs_next

    def abuf(self, f):
        if f < 8:
            return self.RA[:, f * 512:(f + 1) * 512], ("RA", f)
        if f < 16:
            return self.RB[:, (f - 8) * 512:(f - 7) * 512], ("RB", f - 8)
        return self.RC[:, (f - 16) * 512:(f - 15) * 512], ("RC", f - 16)

    def mixer(self, l):
        self.mixer_p1(l)
        if self.stop_after == (l, "p1"):
            return True
        self.mixer_p2(l)
        if self.stop_after == (l, "p2"):
            return True
        self.mixer_p3(l)
        return False

    def evac_engine(self):
        self.cnt["alt"] += 1
        return "act" if self.cnt["alt"] % 2 == 0 else "dve"

    def copy_op(self, ename, out, in_):
        if ename == "act":
            return lambda eng: eng.activation(out=out, in_=in_, func=AF.Identity)
        return lambda eng: eng.tensor_copy(out=out, in_=in_)

    def mixer_p1(self, l):
        P = self.P
        NT = self.NT
        S = self.S
        sub = 1
        def stg_slot(kind="a"):
            if kind == "a":
                s = self.cnt["stg"] % 4
                self.cnt["stg"] += 1
                return s, [("RC", s)], s * 512
            s = self.cnt["stgv"] % 2
            self.cnt["stgv"] += 1
            return 4 + s, [("RC", 4 + 2 * s), ("RC", 5 + 2 * s)], 2048 + s * 1024
        uT = self.RB[:, 0:8 * 514].rearrange("p (c n) -> p c n", n=514)
        ukeys = [("RB", c) for c in range(8)] + [("RB", 8)]
        P.add("pool", lambda eng: eng.memset(uT[:, :, 0:2], 0.0), writes=ukeys)
        P.add("pool", lambda eng: eng.memset(self.RC[:], 1.0), writes=[("RC", c) for c in range(8)])
        hs = self.norm(l, sub, 0)
        for t in range(NT):
            hk = [("h", hs, c) for c in range(8)]
            tc_ = slice(t * 512, (t + 1) * 512)
            CB0, CC0, CH0, GC0, GA0 = 4608, 5632, 6656, 7680, 8704
            for half in range(2):
                ws = self.wuse(("Win", l, CC0 + half * 512))
                for j in range(4):
                    c = half * 4 + j
                    pt, pk = self.psum_get()
                    P.add("pe", self.fm_mm(pt, ws, j, hs), reads=hk + [("w", ws)], writes=[pk])
                    P.add("act", self.copy_op("act", self.RA[:, c * 512:(c + 1) * 512], pt[:]), reads=[pk], writes=[("RA", c)])
                self.wrelease()
            for half in range(2):
                ws = self.wuse(("Win", l, CH0 + half * 512))
                for j in range(4):
                    c = half * 4 + j
                    pt, pk = self.psum_get()
                    P.add("pe", self.fm_mm(pt, ws, j, hs), reads=hk + [("w", ws)], writes=[pk])
                    P.add("dve", lambda eng, c=c, pt=pt: eng.tensor_tensor(out=uT[:, c, 2:514], in0=self.RA[:, c * 512:(c + 1) * 512], in1=pt[:], op=ALU.mult),
                          reads=[pk, ("RA", c)], writes=ukeys_c(c))
                    tA = self.tmp[:, 0, :]
                    tB = self.tmp[:, 1, :]
                    kA, kB = ("tmp", 0), ("tmp", 1)
                    uk = ukeys_c(c)
                    w0 = self.cw[:, l, 0 * 8 + c: 0 * 8 + c + 1]
                    w1 = self.cw[:, l, 1 * 8 + c: 1 * 8 + c + 1]
                    w2 = self.cw[:, l, 2 * 8 + c: 2 * 8 + c + 1]
                    P.add("act", lambda eng, c=c, tA=tA, w2=w2: eng.activation(out=tA, in_=uT[:, c, 2:514], func=AF.Identity, scale=w2),
                          reads=uk + [("cw", l)], writes=[kA])
                    P.add("act", lambda eng, c=c, tB=tB, w1=w1: eng.activation(out=tB, in_=uT[:, c, 1:513], func=AF.Identity, scale=w1),
                          reads=uk + [("cw", l)], writes=[kB])
                    P.add("dve", lambda eng, tA=tA, tB=tB: eng.tensor_tensor(out=tA, in0=tA, in1=tB, op=ALU.add), reads=[kA, kB], writes=[kA])
                    P.add("act", lambda eng, c=c, tB=tB, w0=w0: eng.activation(out=tB, in_=uT[:, c, 0:512], func=AF.Identity, scale=w0),
                          reads=uk + [("cw", l)], writes=[kB])
                    P.add("dve", lambda eng, c=c, tA=tA, tB=tB: eng.tensor_tensor(out=self.RA[:, c * 512:(c + 1) * 512], in0=tA, in1=tB, op=ALU.add),
                          reads=[kA, kB], writes=[("RA", c)])
                self.wrelease()
            P.add("pool", lambda eng: eng.tensor_copy(out=uT[:, :, 0:2], in_=uT[:, :, 512:514]), reads=ukeys, writes=ukeys)
            for blk in range(9):
                col0 = blk * 512
                ws = self.wuse(("Win", l, col0))
                for s in range(4):
                    pt, pk = self.psum_get()

                    def mm(eng, s=s, ws=ws, pt=pt, hs=hs):
                        r = None
                        for kc in range(8):
                            r = eng.matmul(pt[:], self.hT[:, hs, kc * 512 + s * 128: kc * 512 + (s + 1) * 128],
                                           self.wr[:, ws, kc * 512:(kc + 1) * 512], start=(kc == 0), stop=(kc == 7))
                        return r
                    P.add("pe", mm, reads=hk + [("w", ws)], writes=[pk])
                    ss, sk, so = stg_slot("a" if blk < 6 else "v")
                    en = self.evac_engine()
                    r0 = t * 512 + s * 128
                    if blk < 6:
                        o = self.RC[:, so: so + 512]
                        P.add(en, self.copy_op(en, o, pt[:]), reads=[pk], writes=sk)
                        P.add("pool", lambda eng, o=o, blk=blk, r0=r0: eng.dma_start(out=self.d_qk[r0:r0 + 128, blk, :], in_=o),
                              reads=sk, writes=[("qk", blk, r0 // 128)], dma=("stg", ss))
                    else:
                        g = blk - 6
                        o = self.RC[:, so: so + 528].rearrange("p (h e) -> p h e", e=66)
                        P.add(en, self.copy_op(en, o[:, :, 0:64], pt[:].rearrange("p (h e) -> p h e", e=64)), reads=[pk], writes=sk)
                        o2 = self.RC[:, so: so + 528]
                        P.add("pool", lambda eng, o2=o2, g=g, r0=r0: eng.dma_start(out=self.d_v[r0:r0 + 128, g, :], in_=o2),
                              reads=sk, writes=[("v", g, r0 // 128)], dma=("stg", ss))
                self.wrelease()
            for half in range(2):
                ws = self.wuse(("Win", l, CB0 + half * 512))
                for j in range(4):
                    c = half * 4 + j
                    pt, pk = self.psum_get()
                    P.add("pe", self.fm_mm(pt, ws, j, hs), reads=hk + [("w", ws)], writes=[pk])
                    P.add("dve", lambda eng, c=c, pt=pt: eng.tensor_tensor(out=self.RA[:, c * 512:(c + 1) * 512], in0=self.RA[:, c * 512:(c + 1) * 512], in1=pt[:], op=ALU.mult),
                          reads=[pk, ("RA", c)], writes=[("RA", c)])
                self.wrelease()
            zk = [("RA", c) for c in range(8)]
            for db in range(2):
                wsg = self.wuse(("Win", l, GC0 + db * 512))
                wsc = self.wuse(("Wco", l, db))
                for j in range(4):
                    d = db * 4 + j
                    pg, kg = self.psum_get()
                    pc, kc_ = self.psum_get()
                    P.add("pe", self.fm_mm(pg, wsg, j, hs), reads=hk + [("w", wsg)], writes=[kg])

                    def mmc(eng, j=j, wsc=wsc, pc=pc):
                        r = None
                        for c in range(8):
                            r = eng.matmul(pc[:], self.wr[:, wsc, c * 512 + j * 128: c * 512 + (j + 1) * 128], self.RA[:, c * 512:(c + 1) * 512],
                                           start=(c == 0), stop=(c == 7))
                        return r
                    P.add("pe", mmc, reads=zk + [("w", wsc)], writes=[kc_])
                    sgs = self.cnt["sg"] % 2
                    self.cnt["sg"] += 1
                    P.add("act", lambda eng, pg=pg, sgs=sgs: eng.activation(out=self.sg[:, sgs, :], in_=pg[:], func=AF.Sigmoid),
                          reads=[kg], writes=[("sg", sgs)])
                    ss, sk, so = stg_slot()
                    o = self.RC[:, so: so + 512]
                    P.add("dve", lambda eng, o=o, pc=pc, sgs=sgs: eng.tensor_tensor(out=o, in0=self.sg[:, sgs, :], in1=pc[:], op=ALU.mult),
                          reads=[kc_, ("sg", sgs)], writes=sk)
                    P.add("pool", lambda eng, o=o, d=d, tc_=tc_: eng.dma_start(out=self.d_yc[d * 128:(d + 1) * 128, tc_], in_=o),
                          reads=sk, writes=[("yc", d, t)], dma=("stg", ss))
                self.wrelease()
            for db in range(2):
                ws = self.wuse(("Win", l, GA0 + db * 512))
                for j in range(4):
                    d = db * 4 + j
                    pt, pk = self.psum_get()
                    P.add("pe", self.fm_mm(pt, ws, j, hs), reads=hk + [("w", ws)], writes=[pk])
                    ss, sk, so = stg_slot()
                    o = self.RC[:, so: so + 512]
                    hnd = P.add("act", lambda eng, o=o, pt=pt: eng.activation(out=o, in_=pt[:], func=AF.Sigmoid), reads=[pk], writes=sk)
                    P.add("pool", lambda eng, o=o, d=d, tc_=tc_: eng.dma_start(out=self.d_sga[d * 128:(d + 1) * 128, tc_], in_=o),
                          reads=sk, writes=[("sga", d, t)], dma=("stg", ss))
                self.wrelease()
            if l == 0:
                self.cast_step(0, (None if t == NT - 1 else -(-38 // NT)), after=hnd)
            self.cast_step(l + 1, -(-64 // NT), after=hnd)
            if t + 1 < NT:
                hs = self.norm(l, sub, t + 1)

    def fm_mm(self, pt, ws, j, hs):
        def mm(eng):
            r = None
            for kc in range(8):
                r = eng.matmul(pt[:], self.wr[:, ws, kc * 512 + j * 128: kc * 512 + (j + 1) * 128],
                               self.hT[:, hs, kc * 512:(kc + 1) * 512], start=(kc == 0), stop=(kc == 7))
            return r
        return mm

    def mixer_p2(self, l):
        P = self.P
        S = self.S
        ada_next = list(range(18)) if l + 1 < self.depth else []
        hflat = self.hT[:].rearrange("p a b -> p (a b)")
        ekeys = [("h", 0, c) for c in range(8)] + [("h", 1, c) for c in range(4)]
        P.add("sp", lambda eng: eng.dma_start(out=hflat[:, 0:6144], in_=self.d_E[:, :]), reads=[("Ed", i) for i in range(6)], writes=ekeys, dma=("Eld",))
        uo_base = 6144
        units = []
        for g, (win, dil) in enumerate(GROUPS):
            nb = (S // dil) // 128
            for r in range(dil):
                for b in range(nb):
                    units.append((g, dil, r, b))
        loaded = {}
        trd = {}
        rbk = [("RB", c) for c in range(9)]
        p2priv = [("p2V", i) for i in range(4)] + [("p2K", i) for i in range(3)]
        self.barrier(reads=[], writes=rbk + p2priv)

        def rec_loads(n):
            g, dil, r, b = units[n]
            row0 = dil * 128 * b + r
            rows = slice(row0, row0 + dil * 127 + 1, dil)
            blks = list(range(dil * b, dil * (b + 1)))
            qs = n % 2
            qkbuf = self.RA[:, qs * 1024:(qs + 1) * 1024]
            qkkeys = [("RA", 2 * qs), ("RA", 2 * qs + 1)]
            P.add("sp", lambda eng, qkbuf=qkbuf, rows=rows, g=g: [
                eng.dma_start(out=qkbuf[:, 0:512], in_=self.d_qk[rows, g, :]),
                eng.dma_start(out=qkbuf[:, 512:1024], in_=self.d_qk[rows, 3 + g, :])],
                reads=[("qk", g, bb) for bb in blks] + [("qk", 3 + g, bb) for bb in blks], writes=qkkeys, dma=("qkld", qs), ndma=2)
            vs = n % 4
            vbuf = self.RB[:, vs * 528: vs * 528 + 528]
            vkeys = [("p2V", vs)]
            P.add("sp", lambda eng, vbuf=vbuf, rows=rows, g=g: eng.dma_start(out=vbuf, in_=self.d_v[rows, g, :]),
                  reads=[("v", g, bb) for bb in blks], writes=vkeys, dma=("vld", vs))
            loaded[n] = (qkbuf, qkkeys, vbuf, vkeys, rows)

        def rec_tr(n):
            qkbuf, qkkeys, vbuf, vkeys, rows = loaded[n]
            pq, kq = self.psum_get()
            pkk, kk = self.psum_get()

            def trq(eng, qkbuf=qkbuf, pq=pq):
                r = None
                for hp in range(4):
                    r = eng.matmul(pq[:, hp * 128:(hp + 1) * 128], qkbuf[:, hp * 128:(hp + 1) * 128], self.ident[:], start=True, stop=True)
                return r
            P.add("pe", trq, reads=qkkeys + [("ident",)], writes=[kq])

            def trk(eng, qkbuf=qkbuf, pkk=pkk):
                r = None
                for hp in range(4):
                    r = eng.matmul(pkk[:, hp * 128:(hp + 1) * 128], qkbuf[:, 512 + hp * 128: 512 + (hp + 1) * 128], self.ident[:], start=True, stop=True)
                return r
            P.add("pe", trk, reads=qkkeys + [("ident",)], writes=[kk])
            qts = n % 2
            QT = self.RA[:, 2048 + qts * 1024: 2048 + (qts + 1) * 1024].rearrange("p (h n) -> p h n", n=512)
            qtk = ("RA", 4 + 2 * qts)
            qtk2 = ("RA", 5 + 2 * qts)
            P.add("act", self.copy_op("act", QT[0:64, 0, :], pq[0:64, :]), reads=[kq], writes=[qtk])
            P.add("act", self.copy_op("act", QT[64:128, 1, :], pq[64:128, :]), reads=[kq], writes=[qtk2])
            ks = n % 3
            KT = self.RB[:, 2112 + ks * 512: 2112 + (ks + 1) * 512]
            ktk = ("p2K", ks)
            P.add("dve", self.copy_op("dve", KT, pkk[:]), reads=[kk], writes=[ktk])
            trd[n] = (QT, qtk, qtk2, KT, ktk)

        P.add("pool", lambda eng: eng.memset(self.RA[:, 2048:4096], 0.0), writes=[("RA", c) for c in range(4, 8)])
        ada_stride = max(1, len(units) // 18)
        rec_loads(0)
        if len(units) > 1:
            rec_loads(1)
        rec_tr(0)
        prev = None
        for unit in range(len(units)):
            if True:
                if True:
                    g, dil, r, b = units[unit]
                    if b == 0:
                        prev = None
                    if unit + 2 < len(units):
                        rec_loads(unit + 2)
                    if unit + 1 < len(units):
                        rec_tr(unit + 1)
                    qkbuf, qkkeys, vbuf, vkeys, rows = loaded.pop(unit)
                    QT, qtk, qtk2, KT, ktk = trd.pop(unit)
                    pslot = unit
                    psl = pslot % 2
                    Pbuf = self.RC[:, psl * 2048:(psl + 1) * 2048]
                    pU = [self.psum_get(), self.psum_get()]
                    ncol = 256 if b > 0 else 128

                    def S_ops(hp, psS, kS, KT=KT, QT=QT, prev=prev, ktk=ktk, qtk=qtk, qtk2=qtk2):
                        def mm(eng):
                            r = None
                            for hh in range(2):
                                r = eng.matmul(psS[:, hh * 256: hh * 256 + 128], KT[:, hp * 128:(hp + 1) * 128], QT[:, hh, hp * 128:(hp + 1) * 128], start=True, stop=True)
                                if prev is not None:
                                    r = eng.matmul(psS[:, hh * 256 + 128: hh * 256 + 256], prev[0][:, hp * 128:(hp + 1) * 128], QT[:, hh, hp * 128:(hp + 1) * 128], start=True, stop=True)
                            return r
                        rd = [ktk, qtk, qtk2] + ([prev[1]] if prev is not None else [])
                        P.add("pe", mm, reads=rd, writes=[kS])

                    def E_ops(hp, psS, kS, Pbuf=Pbuf, psl=psl, ncol=ncol, g=g):
                        pk_ = ("RC", psl * 4 + hp)
                        src = psS[:].rearrange("p (h n) -> p h n", n=256)[:, :, 0:ncol]
                        dst = Pbuf[:, hp * 512:(hp + 1) * 512].rearrange("p (h n) -> p h n", n=256)[:, :, 0:ncol]
                        e0 = (g * 8 + 2 * hp) * 256
                        Ev = hflat[:, e0:e0 + 512].rearrange("p (h n) -> p h n", n=256)[:, :, 0:ncol]
                        if self.p2_level >= 1.7:
                            P.add("act", lambda eng: eng.activation(out=dst, in_=src, func=AF.Exp, scale=0.125), reads=[kS], writes=[pk_])
                        if self.p2_level >= 2:
                            P.add("dve", lambda eng: eng.tensor_tensor(out=dst, in0=dst, in1=Ev, op=ALU.mult), reads=[pk_] + ekeys, writes=[pk_])

                    def V_ops(hp, Pbuf=Pbuf, psl=psl, vbuf=vbuf, vkeys=vkeys, prev=prev, pU=pU):
                        pk_ = ("RC", psl * 4 + hp)
                        put, puk = pU[hp // 2]

                        def mm(eng):
                            r = None
                            for hh in range(2):
                                h = 2 * hp + hh
                                o = put[:, (h % 4) * 66: (h % 4) * 66 + 65]
                                r = eng.matmul(o, Pbuf[:, h * 256: h * 256 + 128], vbuf[:, h * 66: h * 66 + 65], start=True, stop=(prev is None))
                                if prev is not None:
                                    r = eng.matmul(o, Pbuf[:, h * 256 + 128: h * 256 + 256], prev[2][:, h * 66: h * 66 + 65], start=False, stop=True)
                            return r
                        rd = [pk_] + vkeys + (prev[3] if prev is not None else [])
                        P.add("pe", mm, reads=rd, writes=[puk])

                    if self.p2_level < 1.5:
                        prev = (KT, ktk, vbuf, vkeys)
                        continue
                    if self.p2_level < 3:
                        V_ops = lambda hp: None
                    psS = [None] * 4
                    psS[0] = self.psum_get()
                    S_ops(0, *psS[0])
                    E_ops(0, *psS[0])
                    psS[1] = self.psum_get()
                    S_ops(1, *psS[1])
                    E_ops(1, *psS[1])
                    for hp in range(4):
                        V_ops(hp)
                        if hp + 2 < 4:
                            psS[hp + 2] = self.psum_get()
                            S_ops(hp + 2, *psS[hp + 2])
                            E_ops(hp + 2, *psS[hp + 2])
                    if self.p2_level < 4:
                        prev = (KT, ktk, vbuf, vkeys)
                        continue
                    Uo = hflat[:, uo_base: uo_base + 1040].bitcast(F32)
                    uok = [("h", 1, 4), ("h", 1, 5), ("h", 1, 6)]
                    for half in range(2):
                        put, puk = pU[half]
                        src_ = put[:, 0:264].rearrange("p (h e) -> p h e", e=66)[:, :, 0:65]
                        dst_ = Uo[:, half * 260:(half + 1) * 260].rearrange("p (h e) -> p h e", e=65)
                        hnd = P.add("dve", self.copy_op("dve", dst_, src_), reads=[puk], writes=uok)
                    P.add("pool", lambda eng, Uo=Uo, rows=rows, g=g: eng.dma_start(out=self.d_U[g, rows, :], in_=Uo),
                          reads=uok, writes=[("U", g, r, b)], dma=("ust", 0))
                    self.cast_step(l + 1, 1, after=hnd)
                    prev = (KT, ktk, vbuf, vkeys)
                    if ada_next and unit % ada_stride == 0:
                        self.ada_block(l + 1, ada_next.pop(0))
        while ada_next:
            self.ada_block(l + 1, ada_next.pop(0))
        if l + 1 < self.depth:
            self.ada_finish(l + 1)
        self.cast_step(l + 1, None)
        self.barrier(reads=[], writes=rbk + p2priv)

    def barrier(self, reads, writes):
        self.P.add("pool", lambda eng: eng.memset(self.bar[:, 0:1], 0.0), reads=list(reads), writes=list(writes) + [("bar",)])

    def mixer_p3(self, l):
        P = self.P
        S = self.S
        NT = self.NT
        sub = 1
        hflat = self.hT[:].rearrange("p a b -> p (a b)")
        hall = [("h", s_, c) for s_ in range(2) for c in range(8)]

        def f32v(off_bf, n):
            return hflat[:, off_bf: off_bf + 2 * n].bitcast(F32)
        Ust = [[f32v((k * 3 + g) * 1040, 520) for g in range(3)] for k in range(2)]
        Ustk = [[("Ust", k, g) for g in range(3)] for k in range(2)]
        obuf = [hflat[:, 6240 + i * 512: 6240 + (i + 1) * 512] for i in range(2)]
        obk = [("obuf", 0), ("obuf", 1)]
        priv = [k_ for ks in Ustk for k_ in ks] + obk
        p3x = [("RC", c) for c in range(8)] + [("mT", c, h) for c in range(8) for h in range(2)] + [("tmp", 0), ("tmp", 1)] + [("tmpH", i) for i in range(4)]
        self.barrier(reads=[], writes=hall + priv + p3x)
        def oT(fc):
            return (self.sq if fc < 2 else self.sg)[:, fc % 2, :]
        oTk = [("sq", 0), ("sq", 1), ("sg", 0), ("sg", 1)]
        state = {"blk": 0}

        def stageA1(t, s):
            stageA1_load(t, s)
            stageA1_comp(t, s)

        def stageA1_load(t, s):
            k = s % 2
            r0 = t * 512 + s * 128
            for g, (win, dil) in enumerate(GROUPS):
                b = r0 // (dil * 128)
                rk = [("U", g, r, b) for r in range(dil)]
                P.add("sp", lambda eng, g=g, r0=r0, k=k: eng.dma_start(out=Ust[k][g], in_=self.d_U[g, r0:r0 + 128, :]),
                      reads=rk, writes=[Ustk[k][g]], dma=("uld", k, g))

        def stageA1_comp(t, s):
            k = s % 2
            U0, U1, U2 = Ust[k]
            P.add("dve", lambda eng, U0=U0, U1=U1: eng.tensor_tensor(out=U0, in0=U0, in1=U1, op=ALU.add), reads=[Ustk[k][0], Ustk[k][1]], writes=[Ustk[k][0]])
            P.add("dve", lambda eng, U0=U0, U2=U2: eng.tensor_tensor(out=U0, in0=U0, in1=U2, op=ALU.add), reads=[Ustk[k][0], Ustk[k][2]], writes=[Ustk[k][0]])
            U3 = U0.rearrange("p (h e) -> p h e", e=65)
            P.add("dve", lambda eng, U3=U3: eng.reciprocal(out=self.rs8[:], in_=U3[:, :, 64]), reads=[Ustk[k][0]], writes=[("rs8",)])
            P.add("dve", lambda eng, U3=U3, k=k: eng.tensor_tensor(
                out=obuf[k].rearrange("p (h e) -> p h e", e=64), in0=U3[:, :, 0:64],
                in1=self.rs8[:].unsqueeze(2).to_broadcast([128, 8, 64]), op=ALU.mult),
                reads=[Ustk[k][0], ("rs8",)], writes=[obk[k]])

        def stageA2(t, s):
            k = s % 2
            pt, pk = self.psum_get()

            def tro(eng, k=k, pt=pt):
                r = None
                for fc in range(4):
                    r = eng.matmul(pt[:, fc * 128:(fc + 1) * 128], obuf[k][:, fc * 128:(fc + 1) * 128], self.ident[:], start=True, stop=True)
                return r
            P.add("pe", tro, reads=[obk[k], ("ident",)], writes=[pk])
            P.add("act", self.copy_op("act", self.sq[:, :, s * 128:(s + 1) * 128], pt[:, 0:256].rearrange("p (f t) -> p f t", t=128)),
                  reads=[pk], writes=oTk[0:2])
            P.add("act", self.copy_op("act", self.sg[:, :, s * 128:(s + 1) * 128], pt[:, 256:512].rearrange("p (f t) -> p f t", t=128)),
                  reads=[pk], writes=oTk[2:4])

        for s in range(4):
            stageA1(0, s)
            stageA2(0, s)
        for t in range(NT):
            tc_ = slice(t * 512, (t + 1) * 512)
            def ld_chunk(tt, d):
                tcc = slice(tt * 512, (tt + 1) * 512)
                P.add("sp", lambda eng, d=d, tcc=tcc: [
                    eng.dma_start(out=self.RA[:, d * 512:(d + 1) * 512], in_=self.d_yc[d * 128:(d + 1) * 128, tcc]),
                    eng.dma_start(out=self.RB[:, d * 512:(d + 1) * 512], in_=self.d_sga[d * 128:(d + 1) * 128, tcc])],
                    reads=[("yc", d, tt), ("sga", d, tt)], writes=[("RA", d), ("RB", d)], dma=("ycld", d), ndma=2)
            if t == 0:
                for d in range(8):
                    ld_chunk(0, d)
            wsa = self.wuse(("Wao", l))
            for h in range(2):
                hc = slice(h * 256, (h + 1) * 256)
                for d in range(8):
                    pt, pk = self.psum_get()

                    def mma(eng, d=d, pt=pt, wsa=wsa, hc=hc):
                        r = None
                        for fc in range(4):
                            r = eng.matmul(pt[:, 0:256], self.wr[:, wsa, fc * 1024 + d * 128: fc * 1024 + (d + 1) * 128], oT(fc)[:, hc],
                                           start=(fc == 0), stop=(fc == 3))
                        return r
                    P.add("pe", mma, reads=oTk + [("w", wsa)], writes=[pk])
                    th = self.cnt["tmp"] % 4
                    self.cnt["tmp"] += 1
                    tv = self.tmp[:, th // 2, (th % 2) * 256:(th % 2) * 256 + 256]
                    tk = ("tmpH", th)
                    P.add("dve", lambda eng, d=d, pt=pt, tv=tv, h=h: eng.tensor_tensor(out=tv, in0=pt[:, 0:256], in1=self.RB[:, d * 512 + h * 256: d * 512 + (h + 1) * 256], op=ALU.mult),
                          reads=[pk, ("RB", d)], writes=[tk])
                    P.add("dve", lambda eng, d=d, tv=tv, h=h: eng.tensor_tensor(out=self.RC[:, d * 512 + h * 256: d * 512 + (h + 1) * 256], in0=tv,
                                                                                 in1=self.RA[:, d * 512 + h * 256: d * 512 + (h + 1) * 256], op=ALU.add),
                          reads=[tk, ("RA", d)], writes=[("mT", d, h)])
                    if h == 1 and t + 1 < NT:
                        ld_chunk(t + 1, d)
                    if h == 0 and d in (1, 3) and t + 1 < NT:
                        stageA1(t + 1, d // 2)
                    if h == 0 and d in (5, 7) and t + 1 < NT:
                        stageA1_load(t + 1, d // 2)
            self.wrelease()
            if t + 1 < NT:
                stageA2(t + 1, 0)
                stageA1_comp(t + 1, 2)
                stageA2(t + 1, 1)
                stageA1_comp(t + 1, 3)
            for db in range(2):
                if t + 1 < NT:
                    stageA2(t + 1, 2 + db)
                ws = self.wuse(("Wo", l, db))
                for h in range(2):
                    mk = [("mT", c, h) for c in range(8)]
                    for j in range(4):
                        d = db * 4 + j
                        pt, pk = self.psum_get()

                        def mmo(eng, j=j, ws=ws, pt=pt, h=h):
                            r = None
                            for c in range(8):
                                r = eng.matmul(pt[:, 0:256], self.wr[:, ws, c * 512 + j * 128: c * 512 + (j + 1) * 128],
                                               self.RC[:, c * 512 + h * 256: c * 512 + (h + 1) * 256], start=(c == 0), stop=(c == 7))
                            return r
                        P.add("pe", mmo, reads=mk + [("w", ws)], writes=[pk])
                        tch = slice(t * 512 + h * 256, t * 512 + (h + 1) * 256)
                        P.add("dve", lambda eng, pt=pt, d=d, tch=tch: eng.scalar_tensor_tensor(
                            out=self.xT[:, d, tch], in0=pt[:, 0:256], scalar=self.Gc[:, l, sub * 8 + d: sub * 8 + d + 1], in1=self.xT[:, d, tch],
                            op0=ALU.mult, op1=ALU.add), reads=[pk, ("x", d, t), ("Gc", l, sub)], writes=[("x", d, t)])
                self.wrelease()
        self.barrier(reads=[], writes=hall + priv + p3x)

    def epilogue(self, raw=False):
        P = self.P
        NT = self.NT
        outk = []
        for t in range(NT):
            tc_ = slice(t * 512, (t + 1) * 512)
            if not raw:
                pst, pk = self.psum_get()
                for c in range(8):
                    ss = self.cnt["sq"] % 2
                    self.cnt["sq"] += 1
                    P.add("act", lambda eng, c=c, ss=ss, tc_=tc_: eng.activation(out=self.sq[:, ss, :], in_=self.xT[:, c, tc_], func=AF.Square),
                          reads=[("x", c, t)], writes=[("sq", ss)])
                    P.add("pe", lambda eng, c=c, ss=ss, pst=pst: eng.matmul(pst[:], self.ones[:], self.sq[:, ss, :], start=(c == 0), stop=(c == 7)),
                          reads=[("sq", ss), ("ones",)], writes=[pk])
                P.add("act", lambda eng, pst=pst: eng.activation(out=self.rstd[:], in_=pst[:], func=AF.Sqrt, scale=1.0 / D, bias=self.epsc[:, 0:1]),
                      reads=[pk, ("epsc",)], writes=[("rstd0",)])
                P.add("dve", lambda eng: eng.reciprocal(out=self.rstd[:], in_=self.rstd[:]), reads=[("rstd0",)], writes=[("rstd",), ("rstd0",)])
            for c in range(8):
                if raw:
                    P.add("sp", lambda eng, c=c, tc_=tc_: eng.dma_start(out=self.d_out[c * 128:(c + 1) * 128, tc_], in_=self.xT[:, c, tc_]),
                          reads=[("x", c, t)], writes=[("out", c, t)], dma=("ost", 0))
                    outk.append(("out", c, t))
                    continue
                ts_ = self.cnt["tmp"] % 2
                self.cnt["tmp"] += 1
                tk = ("tmp", ts_)
                P.add("dve", lambda eng, c=c, ts_=ts_, tc_=tc_: eng.tensor_tensor(out=self.tmp[:, ts_, :], in0=self.xT[:, c, tc_], in1=self.rstd[:], op=ALU.mult),
                      reads=[("x", c, t), ("rstd",)], writes=[tk])
                P.add("act", lambda eng, c=c, ts_=ts_: eng.activation(out=self.tmp[:, ts_, :], in_=self.tmp[:, ts_, :], func=AF.Identity,
                                                                      scale=self.fg32[:, c:c + 1]), reads=[tk, ("fg32",)], writes=[tk])
                P.add("sp", lambda eng, c=c, ts_=ts_, tc_=tc_: eng.dma_start(out=self.d_out[c * 128:(c + 1) * 128, tc_], in_=self.tmp[:, ts_, :]),
                      reads=[tk], writes=[("out", c, t)], dma=("ost", ts_))
                outk.append(("out", c, t))
        P.add("sp", None, reads=outk)


def ukeys_c(c):
    lo = (c * 514) // 512
    hi = ((c + 1) * 514 - 1) // 512
    return [("RB", k) for k in range(lo, hi + 1)]


_CACHE = {}


def get_builder(S=SEQ, depth=DEPTH, stop_after=None):
    key = (S, depth, stop_after)
    if key not in _CACHE:
        _CACHE[key] = Builder(S, depth, stop_after)
    return _CACHE[key]


def make_in_maps(inputs, S=SEQ, cores=NCORES, depth=DEPTH):
    f32 = np.float32
    x = np.asarray(inputs["x"], f32)
    c = np.asarray(inputs["c"], f32)
    ada_b = np.asarray(inputs["ada_b"], f32)
    norm_g = np.asarray(inputs["norm_g"], f32)
    conv_w = np.asarray(inputs["conv_w"], f32)
    rel_bias = np.asarray(inputs["rel_bias"], f32)
    final_g = np.asarray(inputs["final_g"], f32)
    ada_bT = np.ascontiguousarray(ada_b.reshape(DEPTH, 72, 128).transpose(0, 2, 1))
    ngT = np.ascontiguousarray(norm_g.reshape(DEPTH, 3, 8, 128).transpose(0, 3, 1, 2).reshape(DEPTH, 128, 24))
    cwT = np.ascontiguousarray(conv_w.reshape(DEPTH, 3, 8, 128).transpose(0, 3, 1, 2).reshape(DEPTH, 128, 24))
    fgT = np.ascontiguousarray(final_g.reshape(8, 128).T)
    bucket, valid = _bias_index()
    biasT = np.full((128, 24, 256), -30000.0, f32)
    for g in range(3):
        for h in range(8):
            vals = rel_bias[bucket[g], g * 8 + h]
            biasT[:, g * 8 + h, :] = np.where(valid, vals, f32(-30000.0))
    biasT = np.ascontiguousarray(biasT.reshape(128, 24 * 256))
    ident = np.eye(128, dtype=f32)
    shared = {
        "ada_w": np.ascontiguousarray(np.asarray(inputs["ada_w"], f32)),
        "ada_bT": ada_bT, "ngT": ngT, "cwT": cwT, "fgT": fgT, "biasT": biasT, "ident": ident,
        "ffn_w_gate": np.ascontiguousarray(np.asarray(inputs["ffn_w_gate"], f32)),
        "ffn_w_up": np.ascontiguousarray(np.asarray(inputs["ffn_w_up"], f32)),
        "ffn_w_down": np.ascontiguousarray(np.asarray(inputs["ffn_w_down"], f32)),
        "w_in": np.ascontiguousarray(np.asarray(inputs["w_in"], f32)),
        "w_conv_out": np.ascontiguousarray(np.asarray(inputs["w_conv_out"], f32)),
        "w_attn_out": np.ascontiguousarray(np.asarray(inputs["w_attn_out"], f32)),
        "w_o": np.ascontiguousarray(np.asarray(inputs["w_o"], f32)),
    }
    if depth != DEPTH:
        for k in ("ada_w", "ada_bT", "ngT", "cwT", "ffn_w_gate", "ffn_w_up", "ffn_w_down", "w_in", "w_conv_out", "w_attn_out", "w_o"):
            shared[k] = np.ascontiguousarray(shared[k][:depth])
    maps = []
    for b in range(cores):
        m = dict(shared)
        m["xT"] = np.ascontiguousarray(x[b, :S].T)
        m["cT"] = np.ascontiguousarray(c[b].reshape(8, 128).T)
        maps.append(m)
    return maps


def kernel(x, c, ada_w, ada_b, norm_g, ffn_w_gate, ffn_w_up, ffn_w_down, w_in, conv_w,
           w_conv_out, w_attn_out, w_o, rel_bias, final_g):
    inputs = dict(x=x, c=c, ada_w=ada_w, ada_b=ada_b, norm_g=norm_g, ffn_w_gate=ffn_w_gate, ffn_w_up=ffn_w_up,
                  ffn_w_down=ffn_w_down, w_in=w_in, conv_w=conv_w, w_conv_out=w_conv_out, w_attn_out=w_attn_out,
                  w_o=w_o, rel_bias=rel_bias, final_g=final_g)
    bld = get_builder()
    maps = make_in_maps(inputs)
    res = run_bass_kernel_spmd(bld.nc, maps, core_ids=list(range(NCORES)))
    out = np.stack([np.ascontiguousarray(r["outT"].T) for r in res.results], axis=0)
    return out.astype(np.float32)
```

```python
from contextlib import ExitStack

import numpy as np
import concourse.bass as bass
import concourse.mybir as mybir
from concourse.bass_utils import run_bass_kernel_spmd

F32 = mybir.dt.float32
BF16 = mybir.dt.bfloat16
AF = mybir.ActivationFunctionType
ALU = mybir.AluOpType

D = 1024
SEQ = 4096
DEPTH = 4
DFF = 2816
NFC = DFF // 128
INW = 9728
NCORES = 8
EPS = 1e-6
GROUPS = ((128, 1), (512, 4), (2048, 16))
NW = 3

ENGS = ("pe", "act", "dve", "pool", "sp")


class Op:
    __slots__ = ("eng", "fn", "deps", "dma", "sig", "sigval", "dmaval", "epoch", "ndma")

    def __init__(self, eng, fn, deps, dma, epoch, ndma):
        self.eng = eng
        self.fn = fn
        self.deps = deps
        self.dma = dma
        self.sig = False
        self.sigval = 0
        self.dmaval = 0
        self.epoch = epoch
        self.ndma = ndma


class Prog:
    def __init__(self):
        self.ops = {e: [] for e in ENGS}
        self.bw = {}
        self.br = {}
        self.epoch = 0
        self.planning = False
        self.dma_count = {}

    def add(self, eng, fn, reads=(), writes=(), dma=None, ndma=1, after=()):
        if self.planning:
            return None
        deps = set(a for a in after if a is not None)
        for k in reads:
            w = self.bw.get(k)
            if w is not None:
                deps.add(w)
        for k in writes:
            w = self.bw.get(k)
            if w is not None:
                deps.add(w)
            for r in self.br.get(k, {}).values():
                deps.add(r)
        idx = len(self.ops[eng])
        me = (eng, idx)
        op = Op(eng, fn, deps, dma, self.epoch, ndma)
        if dma is not None:
            c = self.dma_count.get(dma, 0) + ndma
            self.dma_count[dma] = c
            op.dmaval = 16 * c
        self.ops[eng].append(op)
        for k in reads:
            self.br.setdefault(k, {})[eng] = me
        for k in writes:
            self.bw[k] = me
            self.br[k] = {}
        return me

    def emit(self, nc):
        ops = self.ops
        raw_same = set()
        for e in ENGS:
            for op in ops[e]:
                nd = set()
                for (e2, i2) in op.deps:
                    t = ops[e2][i2]
                    if t.dma is None:
                        if e2 == e and e == "pe":
                            continue
                        t.sig = True
                    nd.add((e2, i2))
                op.deps = nd
        nepoch = self.epoch + 1
        for e in ENGS:
            cnt = [0] * nepoch
            for op in ops[e]:
                if op.dma is None and op.sig:
                    cnt[op.epoch] += 1
                    op.sigval = cnt[op.epoch]
        with ExitStack() as st:
            esem = {}
            for e in ENGS:
                for ep in range(nepoch):
                    if any((o.dma is None and o.sig and o.epoch == ep) for o in ops[e]):
                        esem[(e, ep)] = st.enter_context(nc.semaphore("s_%s_%d" % (e, ep)))
            dsem = {}
            for i, k in enumerate(sorted(self.dma_count.keys(), key=str)):
                dsem[k] = st.enter_context(nc.semaphore("d_%d" % i))
            self.nsems = len(esem) + len(dsem)
            block = st.enter_context(nc.Block())

            def run(ename, eng):
                known = {}
                for op in ops[ename]:
                    need = {}
                    for (e2, i2) in op.deps:
                        t = ops[e2][i2]
                        if t.dma is not None:
                            sk, val = ("d", t.dma), t.dmaval
                        else:
                            if e2 == ename and ename == "pe":
                                continue
                            sk, val = ("e", e2, t.epoch), t.sigval
                        if need.get(sk, 0) < val:
                            need[sk] = val
                    for sk, val in need.items():
                        if known.get(sk, 0) >= val:
                            continue
                        known[sk] = val
                        sem = dsem[sk[1]] if sk[0] == "d" else esem[(sk[1], sk[2])]
                        eng.wait_ge(sem, val)
                    if op.fn is None:
                        continue
                    r = op.fn(eng)
                    if op.dma is not None:
                        rl = r if isinstance(r, (list, tuple)) else [r]
                        assert len(rl) == op.ndma, (len(rl), op.ndma)
                        for ins in rl:
                            ins.then_inc(dsem[op.dma], 16)
                    elif op.sig:
                        ins = r[-1] if isinstance(r, (list, tuple)) else r
                        ins.then_inc(esem[(ename, op.epoch)], 1)

            @block.tensor
            def _(eng):
                run("pe", eng)

            @block.scalar
            def _(eng):
                run("act", eng)

            @block.vector
            def _(eng):
                run("dve", eng)

            @block.gpsimd
            def _(eng):
                run("pool", eng)

            @block.sync
            def _(eng):
                run("sp", eng)


def _t5_bucket(dist):
    exact = 16
    d = np.maximum(dist, 1).astype(np.float32)
    large = exact + (np.log(d / exact) / np.log(2048 / exact) * (32 - exact)).astype(np.int32)
    large = np.minimum(large, 31)
    return np.where(dist < exact, dist, large).astype(np.int32)


def _bias_index():
    j = np.arange(128)[:, None]
    n = np.arange(256)[None, :]
    i = np.where(n < 128, n, n - 128)
    rel = np.where(n < 128, i - j, i - j + 128)
    valid = (rel >= 0) & (rel <= 128)
    buckets = []
    for (_, dil) in GROUPS:
        buckets.append(_t5_bucket(np.maximum(rel, 0) * dil))
    return np.stack(buckets, 0), valid


class Builder:
    def __init__(self, S=SEQ, depth=DEPTH, stop_after=None, p2_level=9):
        self.p2_level = p2_level
        self.S = S
        self.NT = S // 512
        self.depth = depth
        self.stop_after = stop_after
        self.nc = bass.Bass("TRN2", target_bir_lowering=False)
        self.P = Prog()
        self.build()

    def dram_in(self, name, shape, dt=F32):
        return self.nc.dram_tensor(name, list(shape), dt, kind="ExternalInput").ap()

    def sb(self, name, shape, dt):
        return self._st.enter_context(self.nc.sbuf_tensor(name, list(shape), dt))

    def psum_get(self):
        b = self.ps_next
        self.ps_next = (b + 1) % 8
        return self.ps[b], ("ps", b)

    def wuse(self, desc):
        if self.P.planning:
            self.wplan.append(desc)
            return 0
        n = self.wpos
        assert self.wplan[n] == desc, (n, self.wplan[n], desc)
        self.wpos += 1
        self.wpending.append(n)
        return n % NW

    def wrelease(self):
        if self.P.planning:
            return
        for n0 in self.wpending:
            n = n0 + NW
            if n < len(self.wplan):
                self.wload(n)
        self.wpending = []

    def wload(self, n):
        desc = self.wplan[n]
        slot = n % NW
        src, rk = self.wsrc(desc)
        dst = self.wr[:, slot, :]
        nparts = len(src)
        views = []
        off = 0
        for (ap, nrow_chunks, ncols) in src:
            views.append((ap, off, nrow_chunks, ncols))
            off += nrow_chunks * ncols

        def fn(eng, views=views, slot=slot):
            res = []
            for (ap, off, nrc, ncols) in views:
                for c in range(nrc):
                    o = self.wr[:, slot, off + c * ncols: off + (c + 1) * ncols]
                    res.append(eng.dma_start(out=o, in_=ap[c * 128:(c + 1) * 128, :]))
            return res
        nd = sum(v[2] for v in views)
        grp = {"Ada": 0, "Wg": 1, "Wu": 1, "Wd": 1, "Win": 3, "Wco": 3, "Wao": 3, "Wo": 3}[rk[2]]
        if grp == 1 and rk[3] == 1:
            grp = 2
        self.P.add("sp", fn, reads=list(self._castkeys.get((rk[1], grp), [])), writes=[("w", slot)], dma=("w", slot), ndma=nd)

    def wsrc(self, desc):
        kind = desc[0]
        l = desc[1]
        if kind == "Wg" or kind == "Wu":
            which, fb = desc[2], desc[3]
            t = self.wgb if kind == "Wg" else self.wub
            c0 = fb * 512
            nc_ = min(512, DFF - c0)
            return [(t[l, which, :, c0:c0 + nc_], 8, nc_)], ("wbf", l, kind, which)
        if kind == "Wd":
            which, dg, fg = desc[2], desc[3], desc[4]
            f0 = fg * 8
            nf = min(8, NFC - f0)
            return [(self.wdb[l, which, f0 * 128:(f0 + nf) * 128, dg * 512:(dg + 1) * 512], nf, 512)], ("wbf", l, kind, which)
        if kind == "Win":
            c0 = desc[2]
            return [(self.winb[l, :, c0:c0 + 512], 8, 512)], ("wbf", l, kind)
        if kind == "Wco":
            db = desc[2]
            return [(self.wcob[l, :, db * 512:(db + 1) * 512], 8, 512)], ("wbf", l, kind)
        if kind == "Wo":
            db = desc[2]
            return [(self.wob[l, :, db * 512:(db + 1) * 512], 8, 512)], ("wbf", l, kind)
        if kind == "Ada":
            blk = desc[2]
            return [(self.adawb[l, :, blk * 512:(blk + 1) * 512], 8, 512)], ("wbf", l, kind)
        if kind == "Wao":
            return [(self.waob[l, :, :], 4, 1024)], ("wbf", l, kind)
        raise ValueError(desc)

    def build(self):
        nc = self.nc
        S = self.S
        dep = self.depth
        self.d_xT = self.dram_in("xT", [D, S])
        self.d_cT = self.dram_in("cT", [128, 8])
        self.d_adaw = self.dram_in("ada_w", [dep, D, 9 * D])
        self.d_adab = self.dram_in("ada_bT", [dep, 128, 72])
        self.d_ng = self.dram_in("ngT", [dep, 128, 24])
        self.d_wg = self.dram_in("ffn_w_gate", [dep, 2, D, DFF])
        self.d_wu = self.dram_in("ffn_w_up", [dep, 2, D, DFF])
        self.d_wd = self.dram_in("ffn_w_down", [dep, 2, DFF, D])
        self.d_win = self.dram_in("w_in", [dep, D, INW])
        self.d_cw = self.dram_in("cwT", [dep, 128, 24])
        self.d_wco = self.dram_in("w_conv_out", [dep, D, D])
        self.d_wao = self.dram_in("w_attn_out", [dep, 512, D])
        self.d_wo = self.dram_in("w_o", [dep, D, D])
        self.d_bias = self.dram_in("biasT", [128, 24 * 256])
        self.d_fg = self.dram_in("fgT", [128, 8])
        self.d_ident = self.dram_in("ident", [128, 128])
        self.d_out = nc.dram_tensor("outT", [D, S], F32, kind="ExternalOutput").ap()
        self.adawb = nc.dram_tensor("adawb", [DEPTH, D, 9 * D], BF16).ap()
        self.wgb = nc.dram_tensor("wgb", [DEPTH, 2, D, DFF], BF16).ap()
        self.wub = nc.dram_tensor("wub", [DEPTH, 2, D, DFF], BF16).ap()
        self.wdb = nc.dram_tensor("wdb", [DEPTH, 2, DFF, D], BF16).ap()
        self.winb = nc.dram_tensor("winb", [DEPTH, D, INW], BF16).ap()
        self.wcob = nc.dram_tensor("wcob", [DEPTH, D, D], BF16).ap()
        self.waob = nc.dram_tensor("waob", [DEPTH, 512, D], BF16).ap()
        self.wob = nc.dram_tensor("wob", [DEPTH, D, D], BF16).ap()
        self.d_qk = nc.dram_tensor("qk_s", [S, 6, 512], BF16).ap()
        self.d_v = nc.dram_tensor("v_s", [S, 3, 528], BF16).ap()
        self.d_U = nc.dram_tensor("U_s", [3, S, 520], F32).ap()
        self.d_yc = nc.dram_tensor("yc_s", [D, S], BF16).ap()
        self.d_sga = nc.dram_tensor("sga_s", [D, S], BF16).ap()
        self.d_E = nc.dram_tensor("E_s", [128, 24 * 256], BF16).ap()

        with ExitStack() as st:
            self._st = st
            self.xT = self.sb("xT_sb", [128, 8, S], F32)
            self.hT = self.sb("hT", [128, 2, 4096], BF16)
            self.wr = self.sb("wr", [128, NW, 4096], BF16)
            self.RA = self.sb("RA", [128, 4096], BF16)
            self.RB = self.sb("RB", [128, 4096 + 16], BF16)
            self.RC = self.sb("RC", [128, 4096], BF16)
            self.sq = self.sb("sq", [128, 2, 512], BF16)
            self.tmp = self.sb("tmpf", [128, 2, 512], F32)
            self.rstd = self.sb("rstd", [128, 512], F32)
            self.sg = self.sb("sg", [128, 2, 512], BF16)
            self.ident = self.sb("ident_sb", [128, 128], BF16)
            self.ones = self.sb("ones_sb", [128, 128], BF16)
            self.modT = self.sb("modT", [128, DEPTH, 72], F32)
            self.A32 = self.sb("A32", [128, DEPTH, 24], F32)
            self.Gc = self.sb("Gc", [128, DEPTH, 24], F32)
            self.ng = self.sb("ng", [128, DEPTH, 24], F32)
            self.cw = self.sb("cw", [128, DEPTH, 24], F32)
            self.fg = self.sb("fg", [128, 8], F32)
            self.fg32 = self.sb("fg32", [128, 8], F32)
            self.epsc = self.sb("epsc", [128, 1], F32)
            self.csT = self.sb("csT", [128, 8], BF16)
            self.cT = self.sb("cT_sb", [128, 8], F32)
            self.adab = self.sb("adab", [128, DEPTH, 72], F32)
            self.rs8 = self.sb("rs8", [128, 8], F32)
            self.bar = self.sb("bar", [128, 2], F32)
            self.ps = [st.enter_context(nc.psum_tensor("ps%d" % i, [128, 512], F32)) for i in range(8)]
            self.ps_next = 0
            self.last_dve = None
            self.cnt = {"h": 0, "sq": 0, "tmp": 0, "sg": 0, "stg": 0, "stgv": 0, "alt": 0}

            self.wplan = []
            self.P.planning = True
            self.traverse()
            self.P.planning = False
            self.wpos = 0
            self.wpending = []
            self.ps_next = 0
            self.cnt = {k: 0 for k in self.cnt}
            self.traverse()
            self.P.emit(nc)

    def traverse(self):
        P = self.P
        dep = self.depth
        P.epoch = 0
        self.prologue()
        done = False
        for l in range(dep):
            P.epoch = l + 1
            self.ffn(l, 0)
            if self.stop_after == (l, "ffn1"):
                done = True
                break
            if self.mixer(l):
                done = True
                break
            if self.stop_after == (l, "mix"):
                done = True
                break
            self.ffn(l, 1)
            if self.stop_after == (l, "ffn2"):
                done = True
                break
        self.epilogue(raw=done)

    def xkeys(self, t):
        return [("x", c, t) for c in range(8)]

    def cast_jobs(self, l):
        jobs = []
        for c in range(8):
            for h in range(4):
                c0 = h * 2304
                jobs.append((self.adawb[l, c * 128:(c + 1) * 128, c0:c0 + 2304], self.d_adaw[l, c * 128:(c + 1) * 128, c0:c0 + 2304], ("wbf", l, "Ada")))

        def ffn_jobs(which):
            for c in range(8):
                jobs.append((self.wgb[l, which, c * 128:(c + 1) * 128, :], self.d_wg[l, which, c * 128:(c + 1) * 128, :], ("wbf", l, "Wg", which)))
                jobs.append((self.wub[l, which, c * 128:(c + 1) * 128, :], self.d_wu[l, which, c * 128:(c + 1) * 128, :], ("wbf", l, "Wu", which)))
            for c in range(NFC):
                jobs.append((self.wdb[l, which, c * 128:(c + 1) * 128, :], self.d_wd[l, which, c * 128:(c + 1) * 128, :], ("wbf", l, "Wd", which)))
        ffn_jobs(0)
        for c in range(8):
            for h in range(4):
                c0 = h * 2432
                jobs.append((self.winb[l, c * 128:(c + 1) * 128, c0:c0 + 2432], self.d_win[l, c * 128:(c + 1) * 128, c0:c0 + 2432], ("wbf", l, "Win")))
        for c in range(8):
            jobs.append((self.wcob[l, c * 128:(c + 1) * 128, :], self.d_wco[l, c * 128:(c + 1) * 128, :], ("wbf", l, "Wco")))
        for c in range(4):
            jobs.append((self.waob[l, c * 128:(c + 1) * 128, :], self.d_wao[l, c * 128:(c + 1) * 128, :], ("wbf", l, "Wao")))
        for c in range(8):
            jobs.append((self.wob[l, c * 128:(c + 1) * 128, :], self.d_wo[l, c * 128:(c + 1) * 128, :], ("wbf", l, "Wo")))
        ffn_jobs(1)
        return jobs

    def cast_step(self, l, n, after=None):
        P = self.P
        if l >= self.depth:
            return
        if l not in self._castjobs:
            self._castjobs[l] = self.cast_jobs(l)
            self._castpos[l] = 0
        jobs = self._castjobs[l]
        lo = self._castpos[l]
        hi = len(jobs) if n is None else min(len(jobs), lo + n)
        self._castpos[l] = hi
        for i in range(lo, hi):
            o, i_, key = jobs[i]
            grp = {"Ada": 0, "Wg": 1, "Wu": 1, "Wd": 1, "Win": 3, "Wco": 3, "Wao": 3, "Wo": 3}[key[2]]
            if grp == 1 and key[3] == 1:
                grp = 2
            me = P.add("pool", lambda eng, o=o, i_=i_: eng.dma_start(out=o, in_=i_), dma=("cast", l, grp), after=[after])
            if not P.planning:
                uk = ("castjob", l, i)
                P.bw[uk] = me
                self._castkeys.setdefault((l, grp), []).append(uk)

    def ada_stream(self, l):
        P = self.P
        pst, pk = self.psum_get()
        for blk in range(18):
            ws = self.wuse(("Ada", l, blk))
            for j in range(4):
                oc = blk * 4 + j

                def mm(eng, oc=oc, j=j, ws=ws, pst=pst):
                    r = None
                    for kc in range(8):
                        r = eng.matmul(pst[:, oc:oc + 1], self.wr[:, ws, kc * 512 + j * 128: kc * 512 + (j + 1) * 128],
                                       self.csT[:, kc:kc + 1], start=(kc == 0), stop=(kc == 7))
                    return r
                P.add("pe", mm, reads=[("w", ws), ("csT",)], writes=[pk])
            self.wrelease()
        P.add("dve", lambda eng, pst=pst: eng.tensor_tensor(out=self.modT[:, l, :], in0=pst[:, 0:72], in1=self.adab[:, l, :], op=ALU.add),
              reads=[pk, ("adab", l)], writes=[("mod", l)])
        for sub in range(3):
            sc = self.modT[:, l, (sub * 3 + 1) * 8:(sub * 3 + 1) * 8 + 8]
            gt = self.modT[:, l, (sub * 3 + 2) * 8:(sub * 3 + 2) * 8 + 8]
            P.add("dve", lambda eng, sub=sub, sc=sc: eng.scalar_tensor_tensor(
                out=self.A32[:, l, sub * 8:sub * 8 + 8], in0=sc, scalar=1.0, in1=self.ng[:, l, sub * 8:sub * 8 + 8],
                op0=ALU.add, op1=ALU.mult), reads=[("mod", l), ("ng", l)], writes=[("A32", l, sub)])
            P.add("dve", lambda eng, sub=sub, gt=gt: eng.tensor_scalar(
                out=self.Gc[:, l, sub * 8:sub * 8 + 8], in0=gt, scalar1=(1.0 if sub == 1 else 0.5), scalar2=None,
                op0=ALU.mult), reads=[("mod", l)], writes=[("Gc", l, sub)])

    def prologue(self):
        P = self.P
        nc = self.nc
        S = self.S
        self._castkeys = {}
        self._castjobs = {}
        self._castpos = {}
        for c in range(8):
            P.add("sp", lambda eng, c=c: eng.dma_start(out=self.xT[:, c, :], in_=self.d_xT[c * 128:(c + 1) * 128, :]),
                  writes=[("x", c, t) for t in range(self.NT)], dma=("xin", c))
        P.add("sp", lambda eng: [eng.dma_start(out=self.cT[:], in_=self.d_cT),
                                 eng.dma_start(out=self.fg[:], in_=self.d_fg)], writes=[("cT",), ("fg",)], dma=("c0",), ndma=2)
        for l in range(self.depth):
            P.add("sp", lambda eng, l=l: [eng.dma_start(out=self.adab[:, l, :], in_=self.d_adab[l]),
                                          eng.dma_start(out=self.ng[:, l, :], in_=self.d_ng[l]),
                                          eng.dma_start(out=self.cw[:, l, :], in_=self.d_cw[l])],
                  writes=[("adab", l), ("ng", l), ("cw", l)], dma=("c1", l), ndma=3)
        P.add("pool", lambda eng: eng.dma_start(out=self.ident[:], in_=self.d_ident), writes=[("ident",)], dma=("c2",))
        P.add("pool", lambda eng: eng.memset(self.ones[:], 1.0), writes=[("ones",)])
        P.add("pool", lambda eng: eng.memset(self.epsc[:], EPS), writes=[("epsc",)])
        self.cast_step(0, 32 + 38)
        P.add("dve", lambda eng: eng.tensor_scalar(out=self.fg32[:], in0=self.fg[:], scalar1=1.0, scalar2=None, op0=ALU.mult),
              reads=[("fg",)], writes=[("fg32",)])
        P.add("act", lambda eng: eng.activation(out=self.csT[:], in_=self.cT[:], func=AF.Silu), reads=[("cT",)], writes=[("csT",)])
        for i in range(6):
            k = ("w", i % NW)

            def ld(eng, i=i):
                return eng.dma_start(out=self.wr[:, i % NW, :].bitcast(F32)[:, 0:1024], in_=self.d_bias[:, i * 1024:(i + 1) * 1024])
            P.add("sp", ld, writes=[k], dma=("w", i % NW))

            def ex(eng, i=i):
                return eng.activation(out=self.RA[:, 0:1024], in_=self.wr[:, i % NW, :].bitcast(F32)[:, 0:1024], func=AF.Exp)
            P.add("act", ex, reads=[k], writes=[("RA", 0), ("RA", 1)])
            P.add("sp", lambda eng, i=i: eng.dma_start(out=self.d_E[:, i * 1024:(i + 1) * 1024], in_=self.RA[:, 0:1024]),
                  reads=[("RA", 0), ("RA", 1)], writes=[("Ed", i)], dma=("Est",))
        if not P.planning:
            for n in range(min(NW, len(self.wplan))):
                self.wload(n)
        self.ada_stream(0)

    def norm(self, l, sub, t):
        P = self.P
        hs = self.cnt["h"] % 2
        self.cnt["h"] += 1
        tc_ = slice(t * 512, (t + 1) * 512)
        pst, pk = self.psum_get()
        for c in range(8):
            ss = self.cnt["sq"] % 2
            self.cnt["sq"] += 1
            P.add("act", lambda eng, c=c, ss=ss: eng.activation(out=self.sq[:, ss, :], in_=self.xT[:, c, tc_], func=AF.Square),
                  reads=[("x", c, t)], writes=[("sq", ss)])
            P.add("pe", lambda eng, c=c, ss=ss, pst=pst: eng.matmul(pst[:], self.ones[:], self.sq[:, ss, :], start=(c == 0), stop=(c == 7)),
                  reads=[("sq", ss), ("ones",)], writes=[pk])
        P.add("act", lambda eng, pst=pst: eng.activation(out=self.rstd[:], in_=pst[:], func=AF.Sqrt, scale=1.0 / D, bias=self.epsc[:, 0:1]),
              reads=[pk, ("epsc",)], writes=[("rstd0",)])
        P.add("dve", lambda eng: eng.reciprocal(out=self.rstd[:], in_=self.rstd[:]), reads=[("rstd0",)], writes=[("rstd",), ("rstd0",)])
        for c in range(8):
            ts_ = self.cnt["tmp"] % 2
            self.cnt["tmp"] += 1
            P.add("dve", lambda eng, c=c, ts_=ts_: eng.tensor_tensor(out=self.tmp[:, ts_, :], in0=self.xT[:, c, tc_], in1=self.rstd[:], op=ALU.mult),
                  reads=[("x", c, t), ("rstd",)], writes=[("tmp", ts_)])
            P.add("act", lambda eng, c=c, ts_=ts_, hs=hs: eng.activation(
                out=self.hT[:, hs, c * 512:(c + 1) * 512], in_=self.tmp[:, ts_, :], func=AF.Identity,
                scale=self.A32[:, l, sub * 8 + c: sub * 8 + c + 1], bias=self.modT[:, l, sub * 24 + c: sub * 24 + c + 1]),
                reads=[("tmp", ts_), ("A32", l, sub), ("mod", l)], writes=[("h", hs, c)])
        return hs

    def ffn(self, l, which):
        P = self.P
        sub = 0 if which == 0 else 2
        NT = self.NT
        hs = self.norm(l, sub, 0)
        ncast = NT
        for t in range(NT):
            hk = [("h", hs, c) for c in range(8)]
            tc_ = slice(t * 512, (t + 1) * 512)
            for fb in range(6):
                nch = min(4, NFC - fb * 4)
                sg_ = self.wuse(("Wg", l, which, fb))
                su_ = self.wuse(("Wu", l, which, fb))
                ncols = nch * 128
                for j in range(nch):
                    f = fb * 4 + j
                    psg, kg = self.psum_get()
                    psu, ku = self.psum_get()

                    def mmg(eng, j=j, sg_=sg_, psg=psg, hs=hs, ncols=ncols):
                        r = None
                        for kc in range(8):
                            r = eng.matmul(psg[:], self.wr[:, sg_, kc * ncols + j * 128: kc * ncols + (j + 1) * 128],
                                           self.hT[:, hs, kc * 512:(kc + 1) * 512], start=(kc == 0), stop=(kc == 7))
                        return r
                    P.add("pe", mmg, reads=hk + [("w", sg_)], writes=[kg])

                    def mmu(eng, j=j, su_=su_, psu=psu, hs=hs, ncols=ncols):
                        r = None
                        for kc in range(8):
                            r = eng.matmul(psu[:], self.wr[:, su_, kc * ncols + j * 128: kc * ncols + (j + 1) * 128],
                                           self.hT[:, hs, kc * 512:(kc + 1) * 512], start=(kc == 0), stop=(kc == 7))
                        return r
                    P.add("pe", mmu, reads=hk + [("w", su_)], writes=[ku])
                    sgs = self.cnt["sg"] % 2
                    self.cnt["sg"] += 1
                    P.add("act", lambda eng, psg=psg, sgs=sgs: eng.activation(out=self.sg[:, sgs, :], in_=psg[:], func=AF.Silu),
                          reads=[kg], writes=[("sg", sgs)])
                    abuf, akey = self.abuf(f)
                    self.last_dve = P.add("dve", lambda eng, psu=psu, sgs=sgs, abuf=abuf: eng.tensor_tensor(out=abuf, in0=self.sg[:, sgs, :], in1=psu[:], op=ALU.mult),
                                          reads=[ku, ("sg", sgs)], writes=[akey])
                self.wrelease()
            if t + 1 < NT:
                hs_next = self.norm(l, sub, t + 1)
            if which == 0:
                if l == 0:
                    self.cast_step(0, -(-52 // NT), after=self.last_dve)
                if l > 0:
                    self.cast_step(l + 1, -(-32 // NT), after=self.last_dve)
            for dg in range(2):
                pss = [self.psum_get() for _ in range(4)]
                for fg in range(3):
                    nf = min(8, NFC - fg * 8)
                    sd = self.wuse(("Wd", l, which, dg, fg))
                    for j in range(nf):
                        f = fg * 8 + j
                        abuf, akey = self.abuf(f)

                        def mmd(eng, j=j, f=f, sd=sd, abuf=abuf, pss=pss):
                            r = None
                            for d4 in range(4):
                                r = eng.matmul(pss[d4][0][:], self.wr[:, sd, j * 512 + d4 * 128: j * 512 + (d4 + 1) * 128], abuf,
                                               start=(f == 0), stop=(f == NFC - 1))
                            return r
                        P.add("pe", mmd, reads=[akey, ("w", sd)], writes=[k for (_, k) in pss])
                    self.wrelease()
                for d4 in range(4):
                    d = dg * 4 + d4
                    pt, pk = pss[d4]
                    P.add("dve", lambda eng, pt=pt, d=d, tc_=tc_: eng.scalar_tensor_tensor(
                        out=self.xT[:, d, tc_], in0=pt[:], scalar=self.Gc[:, l, sub * 8 + d: sub * 8 + d + 1], in1=self.xT[:, d, tc_],
                        op0=ALU.mult, op1=ALU.add), reads=[pk, ("x", d, t), ("Gc", l, sub)], writes=[("x", d, t)])
            if t + 1 < NT:
                hs = hs_next

    def abuf(self, f):
        if f < 8:
            return self.RA[:, f * 512:(f + 1) * 512], ("RA", f)
        if f < 16:
            return self.RB[:, (f - 8) * 512:(f - 7) * 512], ("RB", f - 8)
        return self.RC[:, (f - 16) * 512:(f - 15) * 512], ("RC", f - 16)

    def mixer(self, l):
        self.mixer_p1(l)
        if self.stop_after == (l, "p1"):
            return True
        self.mixer_p2(l)
        if self.stop_after == (l, "p2"):
            return True
        self.mixer_p3(l)
        return False

    def evac_engine(self):
        self.cnt["alt"] += 1
        return "act" if self.cnt["alt"] % 2 == 0 else "dve"

    def copy_op(self, ename, out, in_):
        if ename == "act":
            return lambda eng: eng.activation(out=out, in_=in_, func=AF.Identity)
        return lambda eng: eng.tensor_copy(out=out, in_=in_)

    def mixer_p1(self, l):
        P = self.P
        NT = self.NT
        S = self.S
        sub = 1
        def stg_slot(kind="a"):
            if kind == "a":
                s = self.cnt["stg"] % 4
                self.cnt["stg"] += 1
                return s, [("RC", s)], s * 512
            s = self.cnt["stgv"] % 2
            self.cnt["stgv"] += 1
            return 4 + s, [("RC", 4 + 2 * s), ("RC", 5 + 2 * s)], 2048 + s * 1024
        uT = self.RB[:, 0:8 * 514].rearrange("p (c n) -> p c n", n=514)
        ukeys = [("RB", c) for c in range(8)] + [("RB", 8)]
        P.add("pool", lambda eng: eng.memset(uT[:, :, 0:2], 0.0), writes=ukeys)
        P.add("pool", lambda eng: eng.memset(self.RC[:], 1.0), writes=[("RC", c) for c in range(8)])
        hs = self.norm(l, sub, 0)
        for t in range(NT):
            hk = [("h", hs, c) for c in range(8)]
            tc_ = slice(t * 512, (t + 1) * 512)
            CB0, CC0, CH0, GC0, GA0 = 4608, 5632, 6656, 7680, 8704
            for half in range(2):
                ws = self.wuse(("Win", l, CC0 + half * 512))
                for j in range(4):
                    c = half * 4 + j
                    pt, pk = self.psum_get()
                    P.add("pe", self.fm_mm(pt, ws, j, hs), reads=hk + [("w", ws)], writes=[pk])
                    P.add("act", self.copy_op("act", self.RA[:, c * 512:(c + 1) * 512], pt[:]), reads=[pk], writes=[("RA", c)])
                self.wrelease()
            for half in range(2):
                ws = self.wuse(("Win", l, CH0 + half * 512))
                for j in range(4):
                    c = half * 4 + j
                    pt, pk = self.psum_get()
                    P.add("pe", self.fm_mm(pt, ws, j, hs), reads=hk + [("w", ws)], writes=[pk])
                    P.add("dve", lambda eng, c=c, pt=pt: eng.tensor_tensor(out=uT[:, c, 2:514], in0=self.RA[:, c * 512:(c + 1) * 512], in1=pt[:], op=ALU.mult),
                          reads=[pk, ("RA", c)], writes=ukeys_c(c))
                    tA = self.tmp[:, 0, :]
                    tB = self.tmp[:, 1, :]
                    kA, kB = ("tmp", 0), ("tmp", 1)
                    uk = ukeys_c(c)
                    w0 = self.cw[:, l, 0 * 8 + c: 0 * 8 + c + 1]
                    w1 = self.cw[:, l, 1 * 8 + c: 1 * 8 + c + 1]
                    w2 = self.cw[:, l, 2 * 8 + c: 2 * 8 + c + 1]
                    P.add("act", lambda eng, c=c, tA=tA, w2=w2: eng.activation(out=tA, in_=uT[:, c, 2:514], func=AF.Identity, scale=w2),
                          reads=uk + [("cw", l)], writes=[kA])
                    P.add("act", lambda eng, c=c, tB=tB, w1=w1: eng.activation(out=tB, in_=uT[:, c, 1:513], func=AF.Identity, scale=w1),
                          reads=uk + [("cw", l)], writes=[kB])
                    P.add("dve", lambda eng, tA=tA, tB=tB: eng.tensor_tensor(out=tA, in0=tA, in1=tB, op=ALU.add), reads=[kA, kB], writes=[kA])
                    P.add("act", lambda eng, c=c, tB=tB, w0=w0: eng.activation(out=tB, in_=uT[:, c, 0:512], func=AF.Identity, scale=w0),
                          reads=uk + [("cw", l)], writes=[kB])
                    P.add("dve", lambda eng, c=c, tA=tA, tB=tB: eng.tensor_tensor(out=self.RA[:, c * 512:(c + 1) * 512], in0=tA, in1=tB, op=ALU.add),
                          reads=[kA, kB], writes=[("RA", c)])
                self.wrelease()
            P.add("pool", lambda eng: eng.tensor_copy(out=uT[:, :, 0:2], in_=uT[:, :, 512:514]), reads=ukeys, writes=ukeys)
            for blk in range(9):
                col0 = blk * 512
                ws = self.wuse(("Win", l, col0))
                for s in range(4):
                    pt, pk = self.psum_get()

                    def mm(eng, s=s, ws=ws, pt=pt, hs=hs):
                        r = None
                        for kc in range(8):
                            r = eng.matmul(pt[:], self.hT[:, hs, kc * 512 + s * 128: kc * 512 + (s + 1) * 128],
                                           self.wr[:, ws, kc * 512:(kc + 1) * 512], start=(kc == 0), stop=(kc == 7))
                        return r
                    P.add("pe", mm, reads=hk + [("w", ws)], writes=[pk])
                    ss, sk, so = stg_slot("a" if blk < 6 else "v")
                    en = self.evac_engine()
                    r0 = t * 512 + s * 128
                    if blk < 6:
                        o = self.RC[:, so: so + 512]
                        P.add(en, self.copy_op(en, o, pt[:]), reads=[pk], writes=sk)
                        P.add("pool", lambda eng, o=o, blk=blk, r0=r0: eng.dma_start(out=self.d_qk[r0:r0 + 128, blk, :], in_=o),
                              reads=sk, writes=[("qk", blk, r0 // 128)], dma=("stg", ss))
                    else:
                        g = blk - 6
                        o = self.RC[:, so: so + 528].rearrange("p (h e) -> p h e", e=66)
                        P.add(en, self.copy_op(en, o[:, :, 0:64], pt[:].rearrange("p (h e) -> p h e", e=64)), reads=[pk], writes=sk)
                        o2 = self.RC[:, so: so + 528]
                        P.add("pool", lambda eng, o2=o2, g=g, r0=r0: eng.dma_start(out=self.d_v[r0:r0 + 128, g, :], in_=o2),
                              reads=sk, writes=[("v", g, r0 // 128)], dma=("stg", ss))
                self.wrelease()
            for half in range(2):
                ws = self.wuse(("Win", l, CB0 + half * 512))
                for j in range(4):
                    c = half * 4 + j
                    pt, pk = self.psum_get()
                    P.add("pe", self.fm_mm(pt, ws, j, hs), reads=hk + [("w", ws)], writes=[pk])
                    P.add("dve", lambda eng, c=c, pt=pt: eng.tensor_tensor(out=self.RA[:, c * 512:(c + 1) * 512], in0=self.RA[:, c * 512:(c + 1) * 512], in1=pt[:], op=ALU.mult),
                          reads=[pk, ("RA", c)], writes=[("RA", c)])
                self.wrelease()
            zk = [("RA", c) for c in range(8)]
            for db in range(2):
                wsg = self.wuse(("Win", l, GC0 + db * 512))
                wsc = self.wuse(("Wco", l, db))
                for j in range(4):
                    d = db * 4 + j
                    pg, kg = self.psum_get()
                    pc, kc_ = self.psum_get()
                    P.add("pe", self.fm_mm(pg, wsg, j, hs), reads=hk + [("w", wsg)], writes=[kg])

                    def mmc(eng, j=j, wsc=wsc, pc=pc):
                        r = None
                        for c in range(8):
                            r = eng.matmul(pc[:], self.wr[:, wsc, c * 512 + j * 128: c * 512 + (j + 1) * 128], self.RA[:, c * 512:(c + 1) * 512],
                                           start=(c == 0), stop=(c == 7))
                        return r
                    P.add("pe", mmc, reads=zk + [("w", wsc)], writes=[kc_])
                    sgs = self.cnt["sg"] % 2
                    self.cnt["sg"] += 1
                    P.add("act", lambda eng, pg=pg, sgs=sgs: eng.activation(out=self.sg[:, sgs, :], in_=pg[:], func=AF.Sigmoid),
                          reads=[kg], writes=[("sg", sgs)])
                    ss, sk, so = stg_slot()
                    o = self.RC[:, so: so + 512]
                    P.add("dve", lambda eng, o=o, pc=pc, sgs=sgs: eng.tensor_tensor(out=o, in0=self.sg[:, sgs, :], in1=pc[:], op=ALU.mult),
                          reads=[kc_, ("sg", sgs)], writes=sk)
                    P.add("pool", lambda eng, o=o, d=d, tc_=tc_: eng.dma_start(out=self.d_yc[d * 128:(d + 1) * 128, tc_], in_=o),
                          reads=sk, writes=[("yc", d, t)], dma=("stg", ss))
                self.wrelease()
            for db in range(2):
                ws = self.wuse(("Win", l, GA0 + db * 512))
                for j in range(4):
                    d = db * 4 + j
                    pt, pk = self.psum_get()
                    P.add("pe", self.fm_mm(pt, ws, j, hs), reads=hk + [("w", ws)], writes=[pk])
                    ss, sk, so = stg_slot()
                    o = self.RC[:, so: so + 512]
                    hnd = P.add("act", lambda eng, o=o, pt=pt: eng.activation(out=o, in_=pt[:], func=AF.Sigmoid), reads=[pk], writes=sk)
                    P.add("pool", lambda eng, o=o, d=d, tc_=tc_: eng.dma_start(out=self.d_sga[d * 128:(d + 1) * 128, tc_], in_=o),
                          reads=sk, writes=[("sga", d, t)], dma=("stg", ss))
                self.wrelease()
            if l == 0:
                self.cast_step(0, (None if t == NT - 1 else -(-38 // NT)), after=hnd)
            self.cast_step(l + 1, -(-(96 if l == 0 else 64) // NT), after=hnd)
            if t + 1 < NT:
                hs = self.norm(l, sub, t + 1)

    def fm_mm(self, pt, ws, j, hs):
        def mm(eng):
            r = None
            for kc in range(8):
                r = eng.matmul(pt[:], self.wr[:, ws, kc * 512 + j * 128: kc * 512 + (j + 1) * 128],
                               self.hT[:, hs, kc * 512:(kc + 1) * 512], start=(kc == 0), stop=(kc == 7))
            return r
        return mm

    def mixer_p2(self, l):
        P = self.P
        S = self.S
        if l + 1 < self.depth:
            self.ada_stream(l + 1)
        hflat = self.hT[:].rearrange("p a b -> p (a b)")
        ekeys = [("h", 0, c) for c in range(8)] + [("h", 1, c) for c in range(4)]
        P.add("sp", lambda eng: eng.dma_start(out=hflat[:, 0:6144], in_=self.d_E[:, :]), reads=[("Ed", i) for i in range(6)], writes=ekeys, dma=("Eld",))
        uo_base = 6144
        units = []
        for g, (win, dil) in enumerate(GROUPS):
            nb = (S // dil) // 128
            for r in range(dil):
                for b in range(nb):
                    units.append((g, dil, r, b))
        loaded = {}
        trd = {}
        rbk = [("RB", c) for c in range(9)]
        p2priv = [("p2V", i) for i in range(4)] + [("p2K", i) for i in range(3)]
        self.barrier(reads=[], writes=rbk + p2priv)

        def rec_loads(n):
            g, dil, r, b = units[n]
            row0 = dil * 128 * b + r
            rows = slice(row0, row0 + dil * 127 + 1, dil)
            blks = list(range(dil * b, dil * (b + 1)))
            qs = n % 2
            qkbuf = self.RA[:, qs * 1024:(qs + 1) * 1024]
            qkkeys = [("RA", 2 * qs), ("RA", 2 * qs + 1)]
            P.add("sp", lambda eng, qkbuf=qkbuf, rows=rows, g=g: [
                eng.dma_start(out=qkbuf[:, 0:512], in_=self.d_qk[rows, g, :]),
                eng.dma_start(out=qkbuf[:, 512:1024], in_=self.d_qk[rows, 3 + g, :])],
                reads=[("qk", g, bb) for bb in blks] + [("qk", 3 + g, bb) for bb in blks], writes=qkkeys, dma=("qkld", qs), ndma=2)
            vs = n % 4
            vbuf = self.RB[:, vs * 528: vs * 528 + 528]
            vkeys = [("p2V", vs)]
            P.add("sp", lambda eng, vbuf=vbuf, rows=rows, g=g: eng.dma_start(out=vbuf, in_=self.d_v[rows, g, :]),
                  reads=[("v", g, bb) for bb in blks], writes=vkeys, dma=("vld", vs))
            loaded[n] = (qkbuf, qkkeys, vbuf, vkeys, rows)

        def rec_tr(n):
            qkbuf, qkkeys, vbuf, vkeys, rows = loaded[n]
            pq, kq = self.psum_get()
            pkk, kk = self.psum_get()

            def trq(eng, qkbuf=qkbuf, pq=pq):
                r = None
                for hp in range(4):
                    r = eng.matmul(pq[:, hp * 128:(hp + 1) * 128], qkbuf[:, hp * 128:(hp + 1) * 128], self.ident[:], start=True, stop=True)
                return r
            P.add("pe", trq, reads=qkkeys + [("ident",)], writes=[kq])

            def trk(eng, qkbuf=qkbuf, pkk=pkk):
                r = None
                for hp in range(4):
                    r = eng.matmul(pkk[:, hp * 128:(hp + 1) * 128], qkbuf[:, 512 + hp * 128: 512 + (hp + 1) * 128], self.ident[:], start=True, stop=True)
                return r
            P.add("pe", trk, reads=qkkeys + [("ident",)], writes=[kk])
            qts = n % 2
            QT = self.RA[:, 2048 + qts * 1024: 2048 + (qts + 1) * 1024].rearrange("p (h n) -> p h n", n=512)
            qtk = ("RA", 4 + 2 * qts)
            qtk2 = ("RA", 5 + 2 * qts)
            P.add("act", self.copy_op("act", QT[0:64, 0, :], pq[0:64, :]), reads=[kq], writes=[qtk])
            P.add("act", self.copy_op("act", QT[64:128, 1, :], pq[64:128, :]), reads=[kq], writes=[qtk2])
            ks = n % 3
            KT = self.RB[:, 2112 + ks * 512: 2112 + (ks + 1) * 512]
            ktk = ("p2K", ks)
            P.add("dve", self.copy_op("dve", KT, pkk[:]), reads=[kk], writes=[ktk])
            trd[n] = (QT, qtk, qtk2, KT, ktk)

        P.add("pool", lambda eng: eng.memset(self.RA[:, 2048:4096], 0.0), writes=[("RA", c) for c in range(4, 8)])
        rec_loads(0)
        if len(units) > 1:
            rec_loads(1)
        rec_tr(0)
        prev = None
        for unit in range(len(units)):
            if True:
                if True:
                    g, dil, r, b = units[unit]
                    if b == 0:
                        prev = None
                    if unit + 2 < len(units):
                        rec_loads(unit + 2)
                    if unit + 1 < len(units):
                        rec_tr(unit + 1)
                    qkbuf, qkkeys, vbuf, vkeys, rows = loaded.pop(unit)
                    QT, qtk, qtk2, KT, ktk = trd.pop(unit)
                    pslot = unit
                    psl = pslot % 2
                    Pbuf = self.RC[:, psl * 2048:(psl + 1) * 2048]
                    pU = [self.psum_get(), self.psum_get()]
                    ncol = 256 if b > 0 else 128

                    def S_ops(hp, psS, kS, KT=KT, QT=QT, prev=prev, ktk=ktk, qtk=qtk, qtk2=qtk2):
                        def mm(eng):
                            r = None
                            for hh in range(2):
                                r = eng.matmul(psS[:, hh * 256: hh * 256 + 128], KT[:, hp * 128:(hp + 1) * 128], QT[:, hh, hp * 128:(hp + 1) * 128], start=True, stop=True)
                                if prev is not None:
                                    r = eng.matmul(psS[:, hh * 256 + 128: hh * 256 + 256], prev[0][:, hp * 128:(hp + 1) * 128], QT[:, hh, hp * 128:(hp + 1) * 128], start=True, stop=True)
                            return r
                        rd = [ktk, qtk, qtk2] + ([prev[1]] if prev is not None else [])
                        P.add("pe", mm, reads=rd, writes=[kS])

                    def E_ops(hp, psS, kS, Pbuf=Pbuf, psl=psl, ncol=ncol, g=g):
                        pk_ = ("RC", psl * 4 + hp)
                        src = psS[:].rearrange("p (h n) -> p h n", n=256)[:, :, 0:ncol]
                        dst = Pbuf[:, hp * 512:(hp + 1) * 512].rearrange("p (h n) -> p h n", n=256)[:, :, 0:ncol]
                        e0 = (g * 8 + 2 * hp) * 256
                        Ev = hflat[:, e0:e0 + 512].rearrange("p (h n) -> p h n", n=256)[:, :, 0:ncol]
                        if self.p2_level >= 1.7:
                            P.add("act", lambda eng: eng.activation(out=dst, in_=src, func=AF.Exp, scale=0.125), reads=[kS], writes=[pk_])
                        if self.p2_level >= 2:
                            P.add("dve", lambda eng: eng.tensor_tensor(out=dst, in0=dst, in1=Ev, op=ALU.mult), reads=[pk_] + ekeys, writes=[pk_])

                    def V_ops(hp, Pbuf=Pbuf, psl=psl, vbuf=vbuf, vkeys=vkeys, prev=prev, pU=pU):
                        pk_ = ("RC", psl * 4 + hp)
                        put, puk = pU[hp // 2]

                        def mm(eng):
                            r = None
                            for hh in range(2):
                                h = 2 * hp + hh
                                o = put[:, (h % 4) * 66: (h % 4) * 66 + 65]
                                r = eng.matmul(o, Pbuf[:, h * 256: h * 256 + 128], vbuf[:, h * 66: h * 66 + 65], start=True, stop=(prev is None))
                                if prev is not None:
                                    r = eng.matmul(o, Pbuf[:, h * 256 + 128: h * 256 + 256], prev[2][:, h * 66: h * 66 + 65], start=False, stop=True)
                            return r
                        rd = [pk_] + vkeys + (prev[3] if prev is not None else [])
                        P.add("pe", mm, reads=rd, writes=[puk])

                    if self.p2_level < 1.5:
                        prev = (KT, ktk, vbuf, vkeys)
                        continue
                    if self.p2_level < 3:
                        V_ops = lambda hp: None
                    psS = [None] * 4
                    psS[0] = self.psum_get()
                    S_ops(0, *psS[0])
                    E_ops(0, *psS[0])
                    psS[1] = self.psum_get()
                    S_ops(1, *psS[1])
                    E_ops(1, *psS[1])
                    for hp in range(4):
                        V_ops(hp)
                        if hp + 2 < 4:
                            psS[hp + 2] = self.psum_get()
                            S_ops(hp + 2, *psS[hp + 2])
                            E_ops(hp + 2, *psS[hp + 2])
                    if self.p2_level < 4:
                        prev = (KT, ktk, vbuf, vkeys)
                        continue
                    Uo = hflat[:, uo_base: uo_base + 1040].bitcast(F32)
                    uok = [("h", 1, 4), ("h", 1, 5), ("h", 1, 6)]
                    for half in range(2):
                        put, puk = pU[half]
                        src_ = put[:, 0:264].rearrange("p (h e) -> p h e", e=66)[:, :, 0:65]
                        dst_ = Uo[:, half * 260:(half + 1) * 260].rearrange("p (h e) -> p h e", e=65)
                        hnd = P.add("dve", self.copy_op("dve", dst_, src_), reads=[puk], writes=uok)
                    P.add("pool", lambda eng, Uo=Uo, rows=rows, g=g: eng.dma_start(out=self.d_U[g, rows, :], in_=Uo),
                          reads=uok, writes=[("U", g, r, b)], dma=("ust", 0))
                    self.cast_step(l + 1, 1, after=hnd)
                    prev = (KT, ktk, vbuf, vkeys)
        self.cast_step(l + 1, None)
        self.barrier(reads=[], writes=rbk + p2priv)

    def barrier(self, reads, writes):
        self.P.add("pool", lambda eng: eng.memset(self.bar[:, 0:1], 0.0), reads=list(reads), writes=list(writes) + [("bar",)])

    def mixer_p3(self, l):
        P = self.P
        S = self.S
        NT = self.NT
        sub = 1
        hflat = self.hT[:].rearrange("p a b -> p (a b)")
        hall = [("h", s_, c) for s_ in range(2) for c in range(8)]

        def f32v(off_bf, n):
            return hflat[:, off_bf: off_bf + 2 * n].bitcast(F32)
        Ust = [[f32v((k * 3 + g) * 1040, 520) for g in range(3)] for k in range(2)]
        Ustk = [[("Ust", k, g) for g in range(3)] for k in range(2)]
        obuf = [hflat[:, 6240 + i * 512: 6240 + (i + 1) * 512] for i in range(2)]
        obk = [("obuf", 0), ("obuf", 1)]
        priv = [k_ for ks in Ustk for k_ in ks] + obk
        p3x = [("RC", c) for c in range(8)] + [("mT", c, h) for c in range(8) for h in range(2)] + [("tmp", 0), ("tmp", 1)] + [("tmpH", i) for i in range(4)]
        self.barrier(reads=[], writes=hall + priv + p3x)
        def oT(fc):
            return (self.sq if fc < 2 else self.sg)[:, fc % 2, :]
        oTk = [("sq", 0), ("sq", 1), ("sg", 0), ("sg", 1)]
        state = {"blk": 0}

        def stageA1(t, s):
            stageA1_load(t, s)
            stageA1_comp(t, s)

        def stageA1_load(t, s):
            k = s % 2
            r0 = t * 512 + s * 128
            for g, (win, dil) in enumerate(GROUPS):
                b = r0 // (dil * 128)
                rk = [("U", g, r, b) for r in range(dil)]
                P.add("sp", lambda eng, g=g, r0=r0, k=k: eng.dma_start(out=Ust[k][g], in_=self.d_U[g, r0:r0 + 128, :]),
                      reads=rk, writes=[Ustk[k][g]], dma=("uld", k, g))

        def stageA1_comp(t, s):
            k = s % 2
            U0, U1, U2 = Ust[k]
            P.add("dve", lambda eng, U0=U0, U1=U1: eng.tensor_tensor(out=U0, in0=U0, in1=U1, op=ALU.add), reads=[Ustk[k][0], Ustk[k][1]], writes=[Ustk[k][0]])
            P.add("dve", lambda eng, U0=U0, U2=U2: eng.tensor_tensor(out=U0, in0=U0, in1=U2, op=ALU.add), reads=[Ustk[k][0], Ustk[k][2]], writes=[Ustk[k][0]])
            U3 = U0.rearrange("p (h e) -> p h e", e=65)
            P.add("dve", lambda eng, U3=U3: eng.reciprocal(out=self.rs8[:], in_=U3[:, :, 64]), reads=[Ustk[k][0]], writes=[("rs8",)])
            P.add("dve", lambda eng, U3=U3, k=k: eng.tensor_tensor(
                out=obuf[k].rearrange("p (h e) -> p h e", e=64), in0=U3[:, :, 0:64],
                in1=self.rs8[:].unsqueeze(2).to_broadcast([128, 8, 64]), op=ALU.mult),
                reads=[Ustk[k][0], ("rs8",)], writes=[obk[k]])

        def stageA2(t, s):
            k = s % 2
            pt, pk = self.psum_get()

            def tro(eng, k=k, pt=pt):
                r = None
                for fc in range(4):
                    r = eng.matmul(pt[:, fc * 128:(fc + 1) * 128], obuf[k][:, fc * 128:(fc + 1) * 128], self.ident[:], start=True, stop=True)
                return r
            P.add("pe", tro, reads=[obk[k], ("ident",)], writes=[pk])
            P.add("act", self.copy_op("act", self.sq[:, :, s * 128:(s + 1) * 128], pt[:, 0:256].rearrange("p (f t) -> p f t", t=128)),
                  reads=[pk], writes=oTk[0:2])
            P.add("act", self.copy_op("act", self.sg[:, :, s * 128:(s + 1) * 128], pt[:, 256:512].rearrange("p (f t) -> p f t", t=128)),
                  reads=[pk], writes=oTk[2:4])

        for s in range(4):
            stageA1(0, s)
            stageA2(0, s)
        for t in range(NT):
            tc_ = slice(t * 512, (t + 1) * 512)
            def ld_chunk(tt, d):
                tcc = slice(tt * 512, (tt + 1) * 512)
                P.add("sp", lambda eng, d=d, tcc=tcc: [
                    eng.dma_start(out=self.RA[:, d * 512:(d + 1) * 512], in_=self.d_yc[d * 128:(d + 1) * 128, tcc]),
                    eng.dma_start(out=self.RB[:, d * 512:(d + 1) * 512], in_=self.d_sga[d * 128:(d + 1) * 128, tcc])],
                    reads=[("yc", d, tt), ("sga", d, tt)], writes=[("RA", d), ("RB", d)], dma=("ycld", d), ndma=2)
            if t == 0:
                for d in range(8):
                    ld_chunk(0, d)
            wsa = self.wuse(("Wao", l))
            for h in range(2):
                hc = slice(h * 256, (h + 1) * 256)
                for d in range(8):
                    pt, pk = self.psum_get()

                    def mma(eng, d=d, pt=pt, wsa=wsa, hc=hc):
                        r = None
                        for fc in range(4):
                            r = eng.matmul(pt[:, 0:256], self.wr[:, wsa, fc * 1024 + d * 128: fc * 1024 + (d + 1) * 128], oT(fc)[:, hc],
                                           start=(fc == 0), stop=(fc == 3))
                        return r
                    P.add("pe", mma, reads=oTk + [("w", wsa)], writes=[pk])
                    th = self.cnt["tmp"] % 4
                    self.cnt["tmp"] += 1
                    tv = self.tmp[:, th // 2, (th % 2) * 256:(th % 2) * 256 + 256]
                    tk = ("tmpH", th)
                    P.add("dve", lambda eng, d=d, pt=pt, tv=tv, h=h: eng.tensor_tensor(out=tv, in0=pt[:, 0:256], in1=self.RB[:, d * 512 + h * 256: d * 512 + (h + 1) * 256], op=ALU.mult),
                          reads=[pk, ("RB", d)], writes=[tk])
                    P.add("dve", lambda eng, d=d, tv=tv, h=h: eng.tensor_tensor(out=self.RC[:, d * 512 + h * 256: d * 512 + (h + 1) * 256], in0=tv,
                                                                                 in1=self.RA[:, d * 512 + h * 256: d * 512 + (h + 1) * 256], op=ALU.add),
                          reads=[tk, ("RA", d)], writes=[("mT", d, h)])
                    if h == 1 and t + 1 < NT:
                        ld_chunk(t + 1, d)
                    if h == 0 and d in (1, 3) and t + 1 < NT:
                        stageA1(t + 1, d // 2)
                    if h == 0 and d in (5, 7) and t + 1 < NT:
                        stageA1_load(t + 1, d // 2)
            self.wrelease()
            if t + 1 < NT:
                stageA2(t + 1, 0)
                stageA1_comp(t + 1, 2)
                stageA2(t + 1, 1)
                stageA1_comp(t + 1, 3)
            for db in range(2):
                if t + 1 < NT:
                    stageA2(t + 1, 2 + db)
                ws = self.wuse(("Wo", l, db))
                for h in range(2):
                    mk = [("mT", c, h) for c in range(8)]
                    for j in range(4):
                        d = db * 4 + j
                        pt, pk = self.psum_get()

                        def mmo(eng, j=j, ws=ws, pt=pt, h=h):
                            r = None
                            for c in range(8):
                                r = eng.matmul(pt[:, 0:256], self.wr[:, ws, c * 512 + j * 128: c * 512 + (j + 1) * 128],
                                               self.RC[:, c * 512 + h * 256: c * 512 + (h + 1) * 256], start=(c == 0), stop=(c == 7))
                            return r
                        P.add("pe", mmo, reads=mk + [("w", ws)], writes=[pk])
                        tch = slice(t * 512 + h * 256, t * 512 + (h + 1) * 256)
                        P.add("dve", lambda eng, pt=pt, d=d, tch=tch: eng.scalar_tensor_tensor(
                            out=self.xT[:, d, tch], in0=pt[:, 0:256], scalar=self.Gc[:, l, sub * 8 + d: sub * 8 + d + 1], in1=self.xT[:, d, tch],
                            op0=ALU.mult, op1=ALU.add), reads=[pk, ("x", d, t), ("Gc", l, sub)], writes=[("x", d, t)])
                self.wrelease()
        self.barrier(reads=[], writes=hall + priv + p3x)

    def epilogue(self, raw=False):
        P = self.P
        NT = self.NT
        outk = []
        for t in range(NT):
            tc_ = slice(t * 512, (t + 1) * 512)
            if not raw:
                pst, pk = self.psum_get()
                for c in range(8):
                    ss = self.cnt["sq"] % 2
                    self.cnt["sq"] += 1
                    P.add("act", lambda eng, c=c, ss=ss, tc_=tc_: eng.activation(out=self.sq[:, ss, :], in_=self.xT[:, c, tc_], func=AF.Square),
                          reads=[("x", c, t)], writes=[("sq", ss)])
                    P.add("pe", lambda eng, c=c, ss=ss, pst=pst: eng.matmul(pst[:], self.ones[:], self.sq[:, ss, :], start=(c == 0), stop=(c == 7)),
                          reads=[("sq", ss), ("ones",)], writes=[pk])
                P.add("act", lambda eng, pst=pst: eng.activation(out=self.rstd[:], in_=pst[:], func=AF.Sqrt, scale=1.0 / D, bias=self.epsc[:, 0:1]),
                      reads=[pk, ("epsc",)], writes=[("rstd0",)])
                P.add("dve", lambda eng: eng.reciprocal(out=self.rstd[:], in_=self.rstd[:]), reads=[("rstd0",)], writes=[("rstd",), ("rstd0",)])
            for c in range(8):
                if raw:
                    P.add("sp", lambda eng, c=c, tc_=tc_: eng.dma_start(out=self.d_out[c * 128:(c + 1) * 128, tc_], in_=self.xT[:, c, tc_]),
                          reads=[("x", c, t)], writes=[("out", c, t)], dma=("ost", 0))
                    outk.append(("out", c, t))
                    continue
                ts_ = self.cnt["tmp"] % 2
                self.cnt["tmp"] += 1
                tk = ("tmp", ts_)
                P.add("dve", lambda eng, c=c, ts_=ts_, tc_=tc_: eng.tensor_tensor(out=self.tmp[:, ts_, :], in0=self.xT[:, c, tc_], in1=self.rstd[:], op=ALU.mult),
                      reads=[("x", c, t), ("rstd",)], writes=[tk])
                P.add("act", lambda eng, c=c, ts_=ts_: eng.activation(out=self.tmp[:, ts_, :], in_=self.tmp[:, ts_, :], func=AF.Identity,
                                                                      scale=self.fg32[:, c:c + 1]), reads=[tk, ("fg32",)], writes=[tk])
                P.add("sp", lambda eng, c=c, ts_=ts_, tc_=tc_: eng.dma_start(out=self.d_out[c * 128:(c + 1) * 128, tc_], in_=self.tmp[:, ts_, :]),
                      reads=[tk], writes=[("out", c, t)], dma=("ost", ts_))
                outk.append(("out", c, t))
        P.add("sp", None, reads=outk)


def ukeys_c(c):
    lo = (c * 514) // 512
    hi = ((c + 1) * 514 - 1) // 512
    return [("RB", k) for k in range(lo, hi + 1)]


_CACHE = {}


def get_builder(S=SEQ, depth=DEPTH, stop_after=None):
    key = (S, depth, stop_after)
    if key not in _CACHE:
        _CACHE[key] = Builder(S, depth, stop_after)
    return _CACHE[key]


def make_in_maps(inputs, S=SEQ, cores=NCORES, depth=DEPTH):
    f32 = np.float32
    x = np.asarray(inputs["x"], f32)
    c = np.asarray(inputs["c"], f32)
    ada_b = np.asarray(inputs["ada_b"], f32)
    norm_g = np.asarray(inputs["norm_g"], f32)
    conv_w = np.asarray(inputs["conv_w"], f32)
    rel_bias = np.asarray(inputs["rel_bias"], f32)
    final_g = np.asarray(inputs["final_g"], f32)
    ada_bT = np.ascontiguousarray(ada_b.reshape(DEPTH, 72, 128).transpose(0, 2, 1))
    ngT = np.ascontiguousarray(norm_g.reshape(DEPTH, 3, 8, 128).transpose(0, 3, 1, 2).reshape(DEPTH, 128, 24))
    cwT = np.ascontiguousarray(conv_w.reshape(DEPTH, 3, 8, 128).transpose(0, 3, 1, 2).reshape(DEPTH, 128, 24))
    fgT = np.ascontiguousarray(final_g.reshape(8, 128).T)
    bucket, valid = _bias_index()
    biasT = np.full((128, 24, 256), -30000.0, f32)
    for g in range(3):
        for h in range(8):
            vals = rel_bias[bucket[g], g * 8 + h]
            biasT[:, g * 8 + h, :] = np.where(valid, vals, f32(-30000.0))
    biasT = np.ascontiguousarray(biasT.reshape(128, 24 * 256))
    ident = np.eye(128, dtype=f32)
    shared = {
        "ada_w": np.ascontiguousarray(np.asarray(inputs["ada_w"], f32)),
        "ada_bT": ada_bT, "ngT": ngT, "cwT": cwT, "fgT": fgT, "biasT": biasT, "ident": ident,
        "ffn_w_gate": np.ascontiguousarray(np.asarray(inputs["ffn_w_gate"], f32)),
        "ffn_w_up": np.ascontiguousarray(np.asarray(inputs["ffn_w_up"], f32)),
        "ffn_w_down": np.ascontiguousarray(np.asarray(inputs["ffn_w_down"], f32)),
        "w_in": np.ascontiguousarray(np.asarray(inputs["w_in"], f32)),
        "w_conv_out": np.ascontiguousarray(np.asarray(inputs["w_conv_out"], f32)),
        "w_attn_out": np.ascontiguousarray(np.asarray(inputs["w_attn_out"], f32)),
        "w_o": np.ascontiguousarray(np.asarray(inputs["w_o"], f32)),
    }
    if depth != DEPTH:
        for k in ("ada_w", "ada_bT", "ngT", "cwT", "ffn_w_gate", "ffn_w_up", "ffn_w_down", "w_in", "w_conv_out", "w_attn_out", "w_o"):
            shared[k] = np.ascontiguousarray(shared[k][:depth])
    maps = []
    for b in range(cores):
        m = dict(shared)
        m["xT"] = np.ascontiguousarray(x[b, :S].T)
        m["cT"] = np.ascontiguousarray(c[b].reshape(8, 128).T)
        maps.append(m)
    return maps


def kernel(x, c, ada_w, ada_b, norm_g, ffn_w_gate, ffn_w_up, ffn_w_down, w_in, conv_w,
           w_conv_out, w_attn_out, w_o, rel_bias, final_g):
    inputs = dict(x=x, c=c, ada_w=ada_w, ada_b=ada_b, norm_g=norm_g, ffn_w_gate=ffn_w_gate, ffn_w_up=ffn_w_up,
                  ffn_w_down=ffn_w_down, w_in=w_in, conv_w=conv_w, w_conv_out=w_conv_out, w_attn_out=w_attn_out,
                  w_o=w_o, rel_bias=rel_bias, final_g=final_g)
    bld = get_builder()
    maps = make_in_maps(inputs)
    res = run_bass_kernel_spmd(bld.nc, maps, core_ids=list(range(NCORES)))
    out = np.stack([np.ascontiguousarray(r["outT"].T) for r in res.results], axis=0)
    return out.astype(np.float32)
```
